# Optimizing a Trainium2 kernel written in Bass

```python
import math
import jax
import jax.numpy as jnp
from jax import lax
import numpy as np


D_MODEL = 1024
BATCH = 8
SEQ = 4096
DEPTH = 2

GRID_W = 64
CTX_LEN = 256
EPS = 1e-6
F32 = jnp.float32

MLA_HEADS = 8
MLA_NOPE = 64
MLA_ROPE = 32
MLA_V = 64
MLA_Q_LORA = 256
MLA_KV_LORA = 128
ROPE_NF = MLA_ROPE // 4
ROPE_BASE = 10000.0
Q_BLOCK = 128

HY_CH = 512
HY_ORDER = 2
HY_BANDS = 16
HY_EMB_DIM = 1 + 2 * HY_BANDS
HY_FILTER_WIDTH = 64
HY_SIN_FREQ = 1.0
HY_DECAY_TARGET = 1e-2
HY_FAST_DECAY = 0.3
HY_SLOW_DECAY = 1.5
HY_MIN_DECAY = math.log(HY_DECAY_TARGET) / HY_SLOW_DECAY
HY_MAX_DECAY = math.log(HY_DECAY_TARGET) / HY_FAST_DECAY

AB_IN = MLA_Q_LORA + MLA_KV_LORA + MLA_ROPE + (HY_ORDER + 1) * HY_CH
AB_OUT = MLA_HEADS * MLA_V + HY_CH

HG_EXPAND = 128
HG_HEADS = D_MODEL // HG_EXPAND
HG_WIDTH = HG_HEADS * HG_EXPAND
HG_CHUNK = 64

FFN_HIDDEN = 2816

N_EVEN = (DEPTH + 1) // 2
N_ODD = DEPTH // 2

kernel_name = 'hybrid_mla_hyena_hgrn2_prefix_dit'


def rmsnorm(x, g):
    xf = x.astype(F32)
    y = xf * lax.rsqrt(jnp.mean(xf * xf, axis=-1, keepdims=True) + EPS)
    return (y * g.astype(F32)).astype(x.dtype)


def modulate(x, g, shift, scale):
    return rmsnorm(x, g) * (1.0 + scale) + shift


def dwconv3(x, w, b):
    xp = jnp.pad(x, ((0, 0), (1, 1), (0, 0)))
    return xp[:, :-2] * w[0] + xp[:, 1:-1] * w[1] + xp[:, 2:] * w[2] + b


def axial_rope_tables(n):
    rows = n // GRID_W
    row = jnp.repeat(jnp.arange(rows), GRID_W).astype(F32)
    col = jnp.tile(jnp.arange(GRID_W), rows).astype(F32)
    inv = ROPE_BASE ** (-jnp.arange(ROPE_NF, dtype=F32) / ROPE_NF)
    ang = jnp.stack([row, col], axis=-1)[..., None] * inv
    return jnp.cos(ang), jnp.sin(ang)


def axial_rope(x, cos, sin):
    xf = x.astype(F32).reshape(x.shape[:-1] + (2, 2, ROPE_NF))
    x1, x2 = xf[..., 0, :], xf[..., 1, :]
    out = jnp.stack([x1 * cos - x2 * sin, x2 * cos + x1 * sin], axis=-2)
    return out.reshape(x.shape).astype(x.dtype)


def attend(q, k, v):
    s = jnp.einsum('bqhd,bkhd->bhqk', q.astype(F32), k.astype(F32)) * (1.0 / math.sqrt(q.shape[-1]))
    p = jax.nn.softmax(s, axis=-1)
    return jnp.einsum('bhqk,bkhd->bqhd', p, v.astype(F32)).astype(v.dtype)


def blocked_attend(q, k, v):
    b, n, h, dq = q.shape
    qb = q.reshape(b, n // Q_BLOCK, Q_BLOCK, h, dq).swapaxes(0, 1)
    ob = lax.map(lambda qi: attend(qi, k, v), qb)
    return ob.swapaxes(0, 1).reshape(b, n, h, v.shape[-1])


def hyena_filters(length, w1, b1, w2, b2, w3):
    t = jnp.linspace(0.0, 1.0, length, dtype=F32)[:, None]
    w = 2.0 * math.pi * jnp.arange(length, dtype=F32)[:, None] / length
    f = jnp.linspace(1e-4, HY_BANDS - 1, HY_BANDS, dtype=F32)
    z = jnp.concatenate([t, jnp.cos(f * w), -jnp.sin(f * w)], axis=-1)
    a = jnp.sin(HY_SIN_FREQ * (z @ w1.astype(F32) + b1.astype(F32)))
    a = jnp.sin(HY_SIN_FREQ * (a @ w2.astype(F32) + b2.astype(F32)))
    filt = (a @ w3.astype(F32)).reshape(length, HY_ORDER, 2, HY_CH)
    deltas = jnp.abs(jnp.linspace(HY_MIN_DECAY, HY_MAX_DECAY, HY_CH, dtype=F32))
    return filt * jnp.exp(-t * deltas)[:, None, None, :]


def bidir_long_conv(u, h_fwd, h_bwd, d):
    length = u.shape[1]
    k2 = jnp.concatenate([h_fwd, jnp.zeros_like(h_fwd[:1]), h_bwd[:0:-1]], axis=0)
    spec = jnp.fft.rfft(u, n=2 * length, axis=1) * jnp.fft.rfft(k2, axis=0)[None]
    y = jnp.fft.irfft(spec, n=2 * length, axis=1)[:, :length]
    return y + u * d.astype(F32)


def hyena(u, conv_w, conv_b, w1, b1, w2, b2, w3, hy_bias):
    length = u.shape[1]
    z = dwconv3(u, conv_w, conv_b).astype(F32)
    v, x1, x2 = jnp.split(z, 3, axis=-1)
    filt = hyena_filters(length, w1, b1, w2, b2, w3)
    y = x1 * bidir_long_conv(v, filt[:, 0, 0], filt[:, 0, 1], hy_bias[0])
    y = x2 * bidir_long_conv(y, filt[:, 1, 0], filt[:, 1, 1], hy_bias[1])
    return y.astype(u.dtype)


def mixer_ab(h, hc, rope_cos, rope_sin, w_in, q_norm_g, w_q_b, kv_norm_g, w_kv_b,
             conv_w, conv_b, w1, b1, w2, b2, w3, hy_bias, w_out, need_ctx):
    kv_end = MLA_Q_LORA + MLA_KV_LORA
    r_end = kv_end + MLA_ROPE

    def project(u):
        z = u @ w_in
        lead = u.shape[:2]
        q = (rmsnorm(z[..., :MLA_Q_LORA], q_norm_g) @ w_q_b).reshape(lead + (MLA_HEADS, MLA_NOPE + MLA_ROPE))
        kv = (rmsnorm(z[..., MLA_Q_LORA:kv_end], kv_norm_g) @ w_kv_b).reshape(lead + (MLA_HEADS, MLA_NOPE + MLA_V))
        return q, kv, z[..., kv_end:r_end], z[..., r_end:]

    def keys_values(kv, k_rope):
        k_rope = jnp.broadcast_to(k_rope[:, :, None, :], kv.shape[:3] + (MLA_ROPE,))
        return jnp.concatenate([kv[..., :MLA_NOPE], k_rope], axis=-1), kv[..., MLA_NOPE:]

    def hyena_branch(hy):
        return hyena(hy, conv_w, conv_b, w1, b1, w2, b2, w3, hy_bias)

    q_c, kv_c, kr_c, hy_c = project(hc)
    k_c, v_c = keys_values(kv_c, kr_c)
    q, kv, kr, hy = project(h)
    q = jnp.concatenate([q[..., :MLA_NOPE], axial_rope(q[..., MLA_NOPE:], rope_cos[:, None], rope_sin[:, None])], axis=-1)
    k, v = keys_values(kv, axial_rope(kr, rope_cos, rope_sin))
    o = blocked_attend(q, jnp.concatenate([k_c, k], axis=1), jnp.concatenate([v_c, v], axis=1))
    y = jnp.concatenate([o.reshape(o.shape[:2] + (-1,)), hyena_branch(hy)], axis=-1) @ w_out
    y_c = None
    if need_ctx:
        o_c = attend(q_c, k_c, v_c)
        y_c = jnp.concatenate([o_c.reshape(o_c.shape[:2] + (-1,)), hyena_branch(hy_c)], axis=-1) @ w_out
    return y, y_c


def gla_chunk_scan(q, k, v, log_f, s0):
    b, n, h, _ = q.shape
    nc = n // HG_CHUNK

    def chunks(a):
        return a.reshape(b, nc, HG_CHUNK, h, a.shape[-1]).transpose(1, 0, 3, 2, 4)

    lower = jnp.tril(jnp.ones((HG_CHUNK, HG_CHUNK), dtype=bool))

    def step(state, inp):
        qc, kc, vc, gc = inp
        cum = jnp.cumsum(gc, axis=2)
        diff = cum[:, :, :, None, :] - cum[:, :, None, :, :]
        decay = jnp.exp(jnp.where(lower[:, :, None], diff, -jnp.inf))
        att = jnp.einsum('bhtk,bhtsk,bhsk->bhts', qc, decay, kc)
        o = jnp.einsum('bhts,bhsv->bhtv', att, vc) + jnp.einsum('bhtk,bhkv->bhtv', qc * jnp.exp(cum), state)
        end = cum[:, :, -1]
        state = jnp.exp(end)[..., None] * state + jnp.einsum('bhsk,bhsv->bhkv', kc * jnp.exp(end[:, :, None] - cum), vc)
        return state, o

    s_final, o = lax.scan(step, s0, (chunks(q), chunks(k), chunks(v), chunks(log_f)))
    return s_final, o.transpose(1, 0, 3, 2, 4).reshape(b, n, h, v.shape[-1])


def hgrn2_mixer(h, hc, lb, w_in, norm_g, w_out, need_ctx):
    def heads(a):
        return a.reshape(a.shape[:2] + (HG_HEADS, HG_EXPAND)).astype(F32)

    def project(u):
        q, f_fwd, f_bwd, i, g = jnp.split(u @ w_in, 5, axis=-1)
        gates = []
        for d, f_raw in enumerate((f_fwd, f_bwd)):
            fr = f_raw.astype(F32)
            log_f = jnp.logaddexp(jnp.log(lb[d]), jnp.log1p(-lb[d]) + jax.nn.log_sigmoid(fr))
            k = (1.0 - lb[d]) * jax.nn.sigmoid(-fr)
            gates.append((heads(k), heads(log_f)))
        return heads(jax.nn.silu(q)), heads(i), g, gates

    def readout(o, g):
        o = rmsnorm(o, norm_g)
        o = o.reshape(o.shape[:2] + (HG_WIDTH,)) * jax.nn.silu(g.astype(F32))
        return o.astype(g.dtype) @ w_out

    def rev(a):
        return a[:, ::-1]

    q_c, i_c, g_c, gates_c = project(hc)
    q, i, g, gates = project(h)
    s0 = jnp.zeros((hc.shape[0], HG_HEADS, HG_EXPAND, HG_EXPAND), F32)
    (k_cf, lf_cf), (k_cb, lf_cb) = gates_c
    (k_f, lf_f), (k_b, lf_b) = gates
    s_cf, o_cf = gla_chunk_scan(q_c, k_cf, i_c, lf_cf, s0)
    s_cb, o_cb = gla_chunk_scan(rev(q_c), rev(k_cb), rev(i_c), rev(lf_cb), s0)
    _, o_f = gla_chunk_scan(q, k_f, i, lf_f, s_cf)
    _, o_b = gla_chunk_scan(rev(q), rev(k_b), rev(i), rev(lf_b), s_cb)
    y = readout(o_f + rev(o_b), g)
    y_c = readout(o_cf + rev(o_cb), g_c) if need_ctx else None
    return y, y_c


def conv_ffn(u, w_up, conv_w, conv_b, w_down):
    a = dwconv3(u @ w_up, conv_w, conv_b)
    gate, val = jnp.split(a, 2, axis=-1)
    return (jax.nn.silu(gate) * val) @ w_down


def setup_inputs(seed: int = 0) -> dict:
    key = jax.random.key(seed)
    ks = iter(jax.random.split(key, 40))

    def nrm(shape, scale):
        return jax.random.normal(next(ks), shape, F32) * scale

    def gain(shape):
        return 1.0 + nrm(shape, 0.02)

    d = D_MODEL
    return {
        'x': nrm((BATCH, SEQ, d), 1.0),
        'c': nrm((BATCH, d), 1.0),
        'ctx': nrm((BATCH, CTX_LEN, d), 1.0),
        'c_ctx': nrm((d,), 1.0),
        'mod_w': nrm((DEPTH, d, 6 * d), 0.2 * d ** -0.5),
        'mod_b': nrm((DEPTH, 6 * d), 0.01),
        'norm1_g': gain((DEPTH, d)),
        'norm2_g': gain((DEPTH, d)),
        'ffn_w_up': nrm((DEPTH, d, 2 * FFN_HIDDEN), d ** -0.5),
        'ffn_conv_w': nrm((DEPTH, 3, 2 * FFN_HIDDEN), 3 ** -0.5),
        'ffn_conv_b': nrm((DEPTH, 2 * FFN_HIDDEN), 0.01),
        'ffn_w_down': nrm((DEPTH, FFN_HIDDEN, d), FFN_HIDDEN ** -0.5),
        'ab_w_in': nrm((N_EVEN, d, AB_IN), d ** -0.5),
        'mla_q_norm_g': gain((N_EVEN, MLA_Q_LORA)),
        'mla_w_q_b': nrm((N_EVEN, MLA_Q_LORA, MLA_HEADS * (MLA_NOPE + MLA_ROPE)), MLA_Q_LORA ** -0.5),
        'mla_kv_norm_g': gain((N_EVEN, MLA_KV_LORA)),
        'mla_w_kv_b': nrm((N_EVEN, MLA_KV_LORA, MLA_HEADS * (MLA_NOPE + MLA_V)), MLA_KV_LORA ** -0.5),
        'hy_conv_w': nrm((N_EVEN, 3, (HY_ORDER + 1) * HY_CH), 3 ** -0.5),
        'hy_conv_b': nrm((N_EVEN, (HY_ORDER + 1) * HY_CH), 0.01),
        'hy_w1': nrm((N_EVEN, HY_EMB_DIM, HY_FILTER_WIDTH), HY_EMB_DIM ** -0.5),
        'hy_b1': nrm((N_EVEN, HY_FILTER_WIDTH), 0.1),
        'hy_w2': nrm((N_EVEN, HY_FILTER_WIDTH, HY_FILTER_WIDTH), HY_FILTER_WIDTH ** -0.5),
        'hy_b2': nrm((N_EVEN, HY_FILTER_WIDTH), 0.1),
        'hy_w3': nrm((N_EVEN, HY_FILTER_WIDTH, HY_ORDER * 2 * HY_CH), 0.05 * HY_FILTER_WIDTH ** -0.5),
        'hy_bias': nrm((N_EVEN, HY_ORDER, HY_CH), 1.0),
        'ab_w_out': nrm((N_EVEN, AB_OUT, d), AB_OUT ** -0.5),
        'hg_w_in': nrm((N_ODD, d, 5 * HG_WIDTH), d ** -0.5),
        'hg_lb_logits': nrm((DEPTH, 2, HG_WIDTH), 0.5),
        'hg_norm_g': gain((N_ODD, HG_EXPAND)),
        'hg_w_out': nrm((N_ODD, HG_WIDTH, d), HG_WIDTH ** -0.5),
        'final_norm_g': gain((d,)),
    }


def reference(x, c, ctx, c_ctx, mod_w, mod_b, norm1_g, norm2_g, ffn_w_up, ffn_conv_w, ffn_conv_b,
              ffn_w_down, ab_w_in, mla_q_norm_g, mla_w_q_b, mla_kv_norm_g, mla_w_kv_b, hy_conv_w,
              hy_conv_b, hy_w1, hy_b1, hy_w2, hy_b2, hy_w3, hy_bias, ab_w_out, hg_w_in, hg_lb_logits,
              hg_norm_g, hg_w_out, final_norm_g):
    n = x.shape[1]
    rope_cos, rope_sin = axial_rope_tables(n)
    lb_probs = jax.nn.softmax(hg_lb_logits.astype(F32), axis=0)
    lb_all = jnp.cumsum(lb_probs, axis=0) - lb_probs[0]
    xc = ctx
    for layer in range(DEPTH):
        need_ctx = layer < DEPTH - 1
        j = layer // 2
        m = (jax.nn.silu(c) @ mod_w[layer] + mod_b[layer])[:, None, :]
        mc = jax.nn.silu(c_ctx) @ mod_w[layer] + mod_b[layer]
        sh1, sc1, g1, sh2, sc2, g2 = jnp.split(m, 6, axis=-1)
        sh1c, sc1c, g1c, sh2c, sc2c, g2c = jnp.split(mc, 6, axis=-1)
        h = modulate(x, norm1_g[layer], sh1, sc1)
        hc = modulate(xc, norm1_g[layer], sh1c, sc1c)
        if layer % 2 == 0:
            y, y_c = mixer_ab(h, hc, rope_cos, rope_sin, ab_w_in[j], mla_q_norm_g[j], mla_w_q_b[j],
                              mla_kv_norm_g[j], mla_w_kv_b[j], hy_conv_w[j], hy_conv_b[j], hy_w1[j],
                              hy_b1[j], hy_w2[j], hy_b2[j], hy_w3[j], hy_bias[j], ab_w_out[j], need_ctx)
        else:
            y, y_c = hgrn2_mixer(h, hc, lb_all[layer], hg_w_in[j], hg_norm_g[j], hg_w_out[j], need_ctx)
        x = x + g1 * y
        x = x + g2 * conv_ffn(modulate(x, norm2_g[layer], sh2, sc2), ffn_w_up[layer],
                              ffn_conv_w[layer], ffn_conv_b[layer], ffn_w_down[layer])
        if need_ctx:
            xc = xc + g1c * y_c
            xc = xc + g2c * conv_ffn(modulate(xc, norm2_g[layer], sh2c, sc2c), ffn_w_up[layer],
                                     ffn_conv_w[layer], ffn_conv_b[layer], ffn_w_down[layer])
    return rmsnorm(x, final_norm_g)
```

```python
import math
from contextlib import ExitStack

import numpy as np
import concourse.bass as bass
import concourse.mybir as mybir
from concourse.bass_utils import run_bass_kernel_spmd

F32 = mybir.dt.float32
BF16 = mybir.dt.bfloat16
AF = mybir.ActivationFunctionType
ALU = mybir.AluOpType

NCTX, NLAT, NTOK, DM = 256, 4096, 4352, 1024
EPS = 1e-6
FFH = 2816
ENG = ("pe", "act", "dve", "pool", "sp")
NSEM = {"sp": 16, "act": 4, "pool": 12}


class Sch:
    def __init__(self, nc, stack):
        self.nc = nc
        self.ops = {e: [] for e in ENG}
        self.cnt = {e: 0 for e in ENG}
        self.seen = {e: {} for e in ENG}
        self.res = {}
        self.esem = {e: stack.enter_context(nc.semaphore("s_" + e)) for e in ENG}
        self.dsem, self.dval, self.drr = {}, {}, {}
        for q, n in NSEM.items():
            self.dsem[q] = [stack.enter_context(nc.semaphore("d_%s%d" % (q, i))) for i in range(n)]
            self.dval[q] = [0] * n
            self.drr[q] = 0

    def _deps(self, rd, wr, prd=(), eng=None):
        deps = []
        for r in rd:
            st = self.res.get(r)
            if st and st["w"] is not None:
                deps.append(st["w"] + ("raw",))
        for r in prd:
            st = self.res.get(r)
            if st:
                if st["w"] is not None:
                    deps.append(st["w"] + ("raw",))
                for k, dp in st["r"].items():
                    if k != eng:
                        deps.append(dp + ("war",))
        for w in wr:
            st = self.res.get(w)
            if st:
                if st["w"] is not None:
                    deps.append(st["w"] + ("waw",))
                for dp in st["r"].values():
                    deps.append(dp + ("war",))
        return deps

    def _mark(self, rd, wr, dep, prd=()):
        for r in list(rd) + list(prd):
            st = self.res.setdefault(r, {"w": None, "r": {}})
            st["r"][dep[0]] = dep
        for w in wr:
            self.res[w] = {"w": dep, "r": {}}

    def _waits(self, eng, deps, is_dma):
        out = {}
        for key, sem, val, kind in deps:
            if (not is_dma) and key == eng and (eng == "pe" or kind != "raw"):
                continue
            if self.seen[eng].get(key, 0) >= val:
                continue
            if key not in out or out[key][1] < val:
                out[key] = (sem, val)
        for key, (sem, val) in out.items():
            self.seen[eng][key] = val
        return list(out.values())

    def op(self, eng, fn, rd=(), wr=(), prd=()):
        waits = self._waits(eng, self._deps(rd, wr, prd, eng), False)
        self.cnt[eng] += 1
        self.ops[eng].append((waits, fn, self.esem[eng], 1))
        self._mark(rd, wr, (eng, self.esem[eng], self.cnt[eng]), prd)

    def dma(self, q, out, in_, rd=(), wr=(), **kw):
        deps = self._deps(rd, wr)
        i = self.drr[q]
        self.drr[q] = (i + 1) % len(self.dsem[q])
        sem = self.dsem[q][i]
        key = ("d", q, i)
        if self.dval[q][i] > 0:
            deps.append((key, sem, self.dval[q][i], "waw"))
        waits = self._waits(q, deps, True)
        self.dval[q][i] += 16
        self.ops[q].append((waits, (lambda e, o=out, s=in_, k=kw: e.dma_start(out=o, in_=s, **k)), sem, 16))
        self._mark(rd, wr, (key, sem, self.dval[q][i]))

    def final_wait(self, eng, keys):
        waits = self._waits(eng, self._deps(keys, ()), True)
        self.ops[eng].append((waits, None, None, 0))

    def emit(self):
        nc = self.nc
        with nc.Block() as block:
            def run(name):
                lst = self.ops[name]

                def body(e):
                    for waits, fn, sem, inc in lst:
                        for (s, v) in waits:
                            e.wait_ge(s, v)
                        if fn is not None:
                            fn(e).then_inc(sem, inc)
                return body
            block.tensor(run("pe"))
            block.scalar(run("act"))
            block.vector(run("dve"))
            block.gpsimd(run("pool"))
            block.sync(run("sp"))
        self.ops = {e: [] for e in ENG}

    def mm(self, out, lhsT, rhs, start, stop, rd, wr, **kw):
        self.op("pe", lambda e: e.matmul(out, lhsT=lhsT, rhs=rhs, start=start, stop=stop, **kw), rd, wr)

    def tr(self, out, in_, ident, rd, wr):
        self.op("pe", lambda e: e.transpose(out, in_, ident), rd, wr)

    def act(self, out, in_, func, rd, wr, prd=(), **kw):
        self.op("act", lambda e: e.activation(out=out, in_=in_, func=func, **kw), rd, wr, prd)

    def tt(self, eng, out, in0, in1, op, rd, wr, prd=()):
        self.op(eng, lambda e: e.tensor_tensor(out=out, in0=in0, in1=in1, op=op), rd, wr, prd)

    def ts(self, eng, out, in0, s1, s2, op0, op1, rd, wr, prd=()):
        if s2 is None:
            self.op(eng, lambda e: e.tensor_scalar(out=out, in0=in0, scalar1=s1, scalar2=None, op0=op0), rd, wr, prd)
        else:
            self.op(eng, lambda e: e.tensor_scalar(out=out, in0=in0, scalar1=s1, scalar2=s2, op0=op0, op1=op1), rd, wr, prd)

    def stt(self, eng, out, in0, scalar, in1, op0, op1, rd, wr, prd=()):
        self.op(eng, lambda e: e.scalar_tensor_tensor(out=out, in0=in0, scalar=scalar, in1=in1, op0=op0, op1=op1), rd, wr, prd)

    def cp(self, eng, out, in_, rd, wr, prd=()):
        self.op(eng, lambda e: e.tensor_copy(out=out, in_=in_), rd, wr, prd)

    def ms(self, eng, out, val, wr):
        self.op(eng, lambda e: e.memset(out, val), (), wr)


class Ring:
    def __init__(self, tiles, name):
        self.t, self.name, self.i = tiles, name, -1

    def nxt(self):
        self.i += 1
        j = self.i % len(self.t)
        return self.t[j], "%s%d" % (self.name, j)


class Ctx:
    pass


def _run_skewed(units, skew=2):
    n = len(units)
    for i in range(n + skew):
        if i < n:
            units[i][0]()
        if i >= skew:
            units[i - skew][1]()


def _mk(K, ph):
    nc = K.nc
    K.uid = getattr(K, "uid", 0) + 1
    u = "_%d" % K.uid

    def sb(name, shape, dt):
        return ph.enter_context(nc.sbuf_tensor(name + u, shape, dt))

    def pp(name, shape, dt):
        return ph.enter_context(nc.psum_tensor(name + u, shape, dt))

    def sring(name, n, shape, dt):
        return Ring([sb("%s%d" % (name, i), shape, dt) for i in range(n)], name)

    def pring(name, n, shape, dt):
        return Ring([pp("%s%d" % (name, i), shape, dt) for i in range(n)], name)
    return sb, pp, sring, pring


def phase_mod(K):
    S, d = K.S, K.d
    with ExitStack() as ph:
        sb, pp, sring, pring = _mk(K, ph)
        csil = sb("csil", [128, 8, 2], F32)
        S.dma("sp", csil[:], d["ccol"][:, :, :], wr=["csil"])
        S.act(csil[:], csil[:], AF.Silu, ["csil"], ["csil"])
        pcol_t = pp("pcol", [128, 512], F32)
        pcol = pcol_t[:, 0:128].rearrange("p (l g w) -> p l g w", l=2, g=32)
        prow = pring("prow", 2, [128, 512], F32)
        wch = sring("wch", 4, [128, 8, 512], F32)
        brow = sring("brow", 2, [1, 512], F32)
        mbc = sb("mbc", [128, 2, 32, 2], F32)
        ngc = sb("ngc", [128, 2, 2, 8, 2], F32)
        S.dma("sp", mbc[:], d["mod_b_col"][:, :, :, :], wr=["mbc"])
        S.dma("sp", ngc[:], d["ng_col"][:, :, :, :, :], wr=["ngc"])
        vmap = {0: 0, 1: 1, 3: 2, 4: 3}
        for l in range(2):
            mw = d["mod_w"][l].rearrange("(kc p) n -> p kc n", p=128)
            for ci in range(12):
                vec, half = ci // 2, ci % 2
                w_t, w_k = wch.nxt()
                for kq in range(2):
                    S.dma("sp" if kq == 0 else "act", w_t[:, kq * 4:(kq + 1) * 4, :], mw[:, kq * 4:(kq + 1) * 4, ci * 512:(ci + 1) * 512], wr=[w_k + "q%d" % kq])
                w_k2 = [w_k + "q0", w_k + "q1"]
                if vec in (2, 5):
                    gi = 0 if vec == 2 else 1
                    for w in range(2):
                        p_t, p_k = prow.nxt()
                        b_t, b_k = brow.nxt()
                        for kc in range(8):
                            S.mm(p_t[0:1, :], csil[:, kc, w:w + 1], w_t[:, kc, :], kc == 0, kc == 7, ["csil"] + w_k2, [p_k])
                        S.dma("sp", b_t[:], d["mod_b"][l:l + 1, ci * 512:(ci + 1) * 512], wr=[b_k])
                        S.tt("dve", b_t[:], p_t[0:1, :], b_t[:], ALU.add, [b_k], [b_k], [p_k])
                        S.dma("sp", d["modrow"][l, gi, w:w + 1, half * 512:(half + 1) * 512], b_t[:], rd=[b_k], wr=["modrow"])
                else:
                    vi = vmap[vec]
                    for jj in range(4):
                        g = vi * 8 + half * 4 + jj
                        for kc in range(8):
                            S.mm(pcol[:, l, g, :], w_t[:, kc, jj * 128:(jj + 1) * 128], csil[:, kc, :], kc == 0, kc == 7,
                                 ["csil"] + w_k2, ["pcol"])
        modt = K.modt
        S.tt("dve", modt[:], pcol, mbc[:], ALU.add, ["mbc"], ["modt"], ["pcol"])
        for l in range(2):
            for n in range(2):
                sl = modt[:, l, (2 * n + 1) * 8:(2 * n + 2) * 8, :]
                S.ts("dve", sl, sl, 1.0, None, ALU.add, None, ["modt"], ["modt"])
                S.tt("dve", sl, sl, ngc[:, l, n, :, :], ALU.mult, ["ngc", "modt"], ["modt"])
        S.emit()


def Gcol(K, l, n, kc, w):
    return K.modt[:, l, (2 * n + 1) * 8 + kc, w:w + 1]


def Shcol(K, l, n, kc, w):
    return K.modt[:, l, (2 * n) * 8 + kc, w:w + 1]


MHALF = [None]


def rstd_ops(S, ss, rs, rd_k, wr_k, n_feat):
    S.ts("dve", rs, ss, 1.0 / n_feat, EPS, ALU.mult, ALU.add, [rd_k], [wr_k])
    S.tt("pool", rs, rs, MHALF[0][:, 0:rs.shape[1]], ALU.pow, [wr_k], [wr_k])


def phase_norm(K, l, n, srcs):
    S, d = K.S, K.d
    with ExitStack() as ph:
        sb, pp, sring, pring = _mk(K, ph)
        xt = sring("nx", 4, [128, DM], F32)
        xn = sring("nxn", 2, [128, DM], BF16)
        hT = sring("nh", 2, [128, 8, 512], BF16)
        ssr = sring("nss", 4, [128, 2], F32)
        pt = pring("npt", 2, [128, DM], BF16)
        junk = sb("njunk", [128, DM], BF16)
        units = []
        cur = {}
        for (src, off, w) in srcs:
            nt = src.shape[0] // 128
            for t in range(nt):
                def mk(src=src, off=off, w=w, nt=nt, t=t):
                    st = {}

                    def a_():
                        x_t, x_k = xt.nxt()
                        s_t, s_k = ssr.nxt()
                        S.dma("sp", x_t[:], src[t * 128:(t + 1) * 128, :], rd=["XA", "XB"], wr=[x_k])
                        S.ms("pool", s_t[:], 0.0, [s_k])
                        S.act(junk[:], x_t[:], AF.Square, [x_k], [s_k], accum_out=s_t[:, 0:1])
                        rstd_ops(S, s_t[:, 0:1], s_t[:, 1:2], s_k, s_k, DM)
                        st["v"] = (x_t, x_k, s_t, s_k)

                    def b_():
                        x_t, x_k, s_t, s_k = st["v"]
                        n_t, n_k = xn.nxt()
                        p_t, p_k = pt.nxt()
                        j = t % 4
                        if j == 0:
                            cur["h"] = hT.nxt()
                        h_t, h_k = cur["h"]
                        S.act(n_t[:], x_t[:], AF.Copy, [x_k, s_k], [n_k], scale=s_t[:, 1:2])
                        for kc in range(8):
                            S.tr(p_t[:, kc * 128:(kc + 1) * 128], n_t[:, kc * 128:(kc + 1) * 128], K.identb[:], [n_k], [p_k])
                        for kc in range(8):
                            S.ts("dve", h_t[:, kc, j * 128:(j + 1) * 128], p_t[:, kc * 128:(kc + 1) * 128], Gcol(K, l, n, kc, w),
                                 Shcol(K, l, n, kc, w), ALU.mult, ALU.add, ["modt"], [h_k + "j%d" % j], [p_k])
                        if j == 3 or t == nt - 1:
                            c0 = off + (t - j) * 128
                            S.dma("sp", d["HT"][:, :, c0:c0 + (j + 1) * 128], h_t[:, :, 0:(j + 1) * 128],
                                  rd=[h_k + "j%d" % jj for jj in range(j + 1)], wr=["HT"])
                    return a_, b_
                units.append(mk())
        _run_skewed(units, 2)
        S.emit()


def phase_ffn(K, l, src_lat, src_ctx, dst_lat, dst_ctx, final, do_ctx=True):
    S, d = K.S, K.d
    with ExitStack() as ph:
        sb, pp, sring, pring = _mk(K, ph)
        wup = sb("wup", [128, 8, 2 * FFH], BF16)
        wdn = sb("wdn", [128, 22, DM], BF16)
        wu = d["ffn_w_up"][l].rearrange("(kc p) n -> p kc n", p=128)
        wd = d["ffn_w_down"][l].rearrange("(j p) n -> p j n", p=128)
        for c in range(11):
            S.dma("pool", wup[:, :, c * 512:(c + 1) * 512], wu[:, :, c * 512:(c + 1) * 512], wr=["wup"])
        for c in range(11):
            S.dma("pool", wdn[:, 2 * c:2 * c + 2, :], wd[:, 2 * c:2 * c + 2, :], wr=["wdn"])
        cw = sb("fcw", [128, 3, 44], F32)
        cb = sb("fcb", [128, 44], F32)
        S.dma("sp", cw[:], d["ffn_cw_col"][l], wr=["fcw"])
        S.dma("sp", cb[:], d["ffn_cb_col"][l], wr=["fcw"])
        g2bc = sb("g2bc", [128, 2, DM], F32)
        for w in range(2):
            S.dma("sp", g2bc[:, w, :], d["modrow"][l, 1, w:w + 1, :].to_broadcast([128, DM]), rd=["modrow"], wr=["g2bc"])
        if final:
            fng = sb("fng", [128, DM], F32)
            S.dma("sp", fng[:], d["final_g"][0:1, :].to_broadcast([128, DM]), wr=["fng"])
        hb = sring("fhb", 2, [128, 8, 258], BF16)
        gT = sb("fgT", [128, 22, 256], BF16)
        av = sring("fav", 4, [128, 256], F32)
        sg = sring("fsg", 2, [128, 256], F32)
        xr = sring("fx", 2, [128, DM], F32)
        tmp = sring("ftmp", 2, [128, DM], F32)
        ssr = sring("fss", 2, [128, 2], F32)
        junk = sb("fjunk", [128, DM], BF16)
        pup = pring("fpu", 4, [128, 512], F32)
        pdn = [pp("fpd%d" % i, [128, 512], F32) for i in range(4)]
        blocks = [(0, 1, src_ctx, dst_ctx, 0, True, True)] if do_ctx else []
        for i in range(16):
            blocks.append((256 + 256 * i, 0, src_lat, dst_lat, 256 * i, i == 0, i == 15))
        for (t0, w, src, dst, r0, first, last) in blocks:
            h_t, h_k = hb.nxt()
            lo = 0 if not first else 1
            hi = 258 if not last else 257
            if first:
                S.ms("pool", h_t[:, :, 0:1], 0.0, [h_k])
            if last:
                S.ms("pool", h_t[:, :, 257:258], 0.0, [h_k])
            S.dma("sp", h_t[:, :, lo:hi], d["HT"][:, :, t0 - 1 + lo:t0 - 1 + hi], rd=["HT"], wr=[h_k])

            def down(j):
                for tt_ in range(2):
                    for nh in range(2):
                        S.mm(pdn[tt_ * 2 + nh][:], gT[:, j, tt_ * 128:(tt_ + 1) * 128], wdn[:, j, nh * 512:(nh + 1) * 512],
                             j == 0, j == 21, ["fgT%d" % j, "wdn"], ["fpd%d" % (tt_ * 2 + nh)])

            for j in range(23):
                if j < 22:
                    pts = []
                    for part in range(2):
                        ch = part * 22 + j
                        p_t, p_k = pup.nxt()
                        a_t, a_k = av.nxt()
                        for kc in range(8):
                            S.mm(p_t[:, 0:258], wup[:, kc, ch * 128:(ch + 1) * 128], h_t[:, kc, :], kc == 0, kc == 7,
                                 ["wup", h_k], [p_k])
                        pts.append((p_t, p_k, a_t, a_k, ch))
                    for (p_t, p_k, a_t, a_k, ch) in pts:
                        S.act(a_t[:], p_t[:, 1:257], AF.Identity, ["fcw"], [a_k], [p_k], scale=cw[:, 1, ch:ch + 1], bias=cb[:, ch:ch + 1])
                    for tap, c0 in ((0, 0), (2, 2)):
                        for (p_t, p_k, a_t, a_k, ch) in pts:
                            S.stt("dve", a_t[:], p_t[:, c0:c0 + 256], cw[:, tap, ch:ch + 1], a_t[:], ALU.mult, ALU.add,
                                  ["fcw", a_k], [a_k], [p_k])
                    s_t, s_k = sg.nxt()
                    S.act(s_t[:], pts[0][2][:], AF.Silu, [pts[0][3]], [s_k])
                    S.tt("pool", gT[:, j, :], s_t[:], pts[1][2][:], ALU.mult, [s_k, pts[1][3]], ["fgT%d" % j])
                if j > 0:
                    down(j - 1)
            for tt_ in range(2):
                x_t, x_k = xr.nxt()
                m_t, m_k = tmp.nxt()
                rows = slice(r0 + tt_ * 128, r0 + (tt_ + 1) * 128)
                S.dma("sp", x_t[:], src[rows, :], rd=["XA"], wr=[x_k])
                for nh in range(2):
                    cs = slice(nh * 512, (nh + 1) * 512)
                    S.tt("dve", m_t[:, cs], pdn[tt_ * 2 + nh][:], g2bc[:, w, cs], ALU.mult, ["g2bc"], [m_k], ["fpd%d" % (tt_ * 2 + nh)])
                S.tt("pool", x_t[:], x_t[:], m_t[:], ALU.add, [m_k, x_k], [x_k])
                if final:
                    if w == 1:
                        continue
                    s_t, s_k = ssr.nxt()
                    S.ms("pool", s_t[:], 0.0, [s_k])
                    S.act(junk[:], x_t[:], AF.Square, [x_k], [s_k], accum_out=s_t[:, 0:1])
                    rstd_ops(S, s_t[:, 0:1], s_t[:, 1:2], s_k, s_k, DM)
                    S.stt("dve", m_t[:], x_t[:], s_t[:, 1:2], fng[:], ALU.mult, ALU.mult, [x_k, s_k, "fng"], [m_k])
                    S.dma("pool", dst[rows, :], m_t[:], rd=[m_k], wr=["OUT"])
                else:
                    S.dma("pool", dst[rows, :], x_t[:], rd=[x_k], wr=["XB"])
        S.emit()


def phase_outproj(K, l, wname, src_lat, src_ctx, dst, do_ctx):
    S, d = K.S, K.d
    with ExitStack() as ph:
        sb, pp, sring, pring = _mk(K, ph)
        wo = sb("owo", [128, 8, DM], BF16)
        wv = d[wname].rearrange("(kc p) n -> p kc n", p=128)
        for c in range(4):
            S.dma("pool", wo[:, 2 * c:2 * c + 2, :], wv[:, 2 * c:2 * c + 2, :], wr=["owo"])
        g1bc = sb("og1", [128, 2, DM], F32)
        for w in range(2):
            S.dma("sp", g1bc[:, w, :], d["modrow"][l, 0, w:w + 1, :].to_broadcast([128, DM]), rd=["modrow"], wr=["og1"])
        ot = sring("oot", 2, [128, 8, 512], BF16)
        xr = sring("ox", 3, [128, DM], F32)
        tmp = sring("otmp", 2, [128, DM], F32)
        po = pring("opo", 4, [128, 512], F32)
        tiles = []
        if do_ctx:
            tiles += [(t * 128, 1, src_ctx, t * 128) for t in range(2)]
        tiles += [(NCTX + t * 128, 0, src_lat, t * 128) for t in range(32)]
        for ti, (c0, w, src, r0) in enumerate(tiles):
            jj = ((c0 - NCTX) % 512) // 128 if c0 >= NCTX else (c0 // 128)
            if jj == 0:
                o_t, o_k = ot.nxt()
                wd_ = 256 if c0 < NCTX else 512
                S.dma("sp", o_t[:, :, 0:wd_], d["OT"][:, :, c0:c0 + wd_], rd=["OT"], wr=[o_k])
            x_t, x_k = xr.nxt()
            m_t, m_k = tmp.nxt()
            S.dma("sp", x_t[:], src[r0:r0 + 128, :], rd=["XB"], wr=[x_k])
            for nh in range(2):
                p_t, p_k = po.nxt()
                cs = slice(nh * 512, (nh + 1) * 512)
                for kc in range(8):
                    S.mm(p_t[:], o_t[:, kc, jj * 128:(jj + 1) * 128], wo[:, kc, cs], kc == 0, kc == 7, [o_k, "owo"], [p_k])
                S.tt("dve", m_t[:, cs], p_t[:], g1bc[:, w, cs], ALU.mult, ["og1"], [m_k], [p_k])
            S.tt("pool", x_t[:], x_t[:], m_t[:], ALU.add, [m_k, x_k], [x_k])
            S.dma("pool", dst[c0:c0 + 128, :], x_t[:], rd=[x_k], wr=["XA"])
        S.emit()


def chunks512():
    return [(c * 512, 512) for c in range(8)] + [(4096, 256)]


def phase_mla_proj(K):
    S, d = K.S, K.d
    with ExitStack() as ph:
        sb, pp, sring, pring = _mk(K, ph)
        HTs = sb("mHT", [128, 8, NTOK], BF16)
        for kc in range(8):
            S.dma("sp", HTs[:, kc, :], d["HT"][:, kc, :], rd=["HT"], wr=["mHT"])
        win = d["ab_w_in"].rearrange("(kc p) n -> p kc n", p=128)
        wlo = sb("mwlo", [128, 8, 384], BF16)
        S.dma("pool", wlo[:], win[:, :, 0:384], wr=["mw"])
        wkr = sb("mwkr", [128, 2, 8, 96], BF16)
        S.ms("pool", wkr[:], 0.0, ["mw"])
        S.dma("pool", wkr[:, 0, :, 64:96], win[:, :, 384:416], wr=["mw"])
        S.dma("pool", wkr[:, 1, :, 64:96], d["w_in_krp"].rearrange("(kc p) n -> p kc n", p=128), wr=["mw"])
        wq = sb("mwq", [128, 2, 768], BF16)
        S.dma("pool", wq[:], d["mla_w_q_b"].rearrange("(kc p) n -> p kc n", p=128), wr=["mw"])
        wqp = sb("mwqp", [128, 2, 8, 96], BF16)
        S.ms("pool", wqp[:], 0.0, ["mw"])
        for kc in range(2):
            S.dma("pool", wqp[:, kc, :, 64:96], d["w_q_perm"][kc * 128:(kc + 1) * 128, :, :], wr=["mw"])
        wkv = sb("mwkv", [128, 1024], BF16)
        S.dma("pool", wkv[:], d["mla_w_kv_b"][:, :], wr=["mw"])
        gcol = sb("mgc", [128, 3], F32)
        S.dma("sp", gcol[:], d["mla_g_col"][:, :], wr=["mgc"])
        rC = sb("mrC", [96, NTOK], F32)
        rS = sb("mrS", [96, NTOK], F32)
        S.dma("sp", rC[64:96, :], d["ropeC"][:, :], wr=["mrC"])
        S.dma("sp", rS[64:96, :], d["ropeS"][:, :], wr=["mrC"])
        qnT = sb("mqnT", [128, 2, NTOK], BF16)
        kvnT = sb("mkvnT", [128, NTOK], BF16)
        KRT = sb("mKRT", [96, NTOK], BF16)
        pl = pring("mpl", 2, [128, 512], F32)
        ptr = pring("mpt", 2, [128, 1024], BF16)
        nrm = sring("mnrm", 2, [128, 384], BF16)
        ssr = sring("mss", 3, [128, 4], F32)
        junk = sb("mjunk", [128, 384], BF16)
        units = []
        for t in range(34):
            def mk(t=t):
                st = {}
                ts_ = slice(t * 128, (t + 1) * 128)

                def a_():
                    p_t, p_k = pl.nxt()
                    s_t, s_k = ssr.nxt()
                    for kc in range(8):
                        S.mm(p_t[:, 0:384], HTs[:, kc, ts_], wlo[:, kc, :], kc == 0, kc == 7, ["mHT", "mw"], [p_k])
                    S.ms("pool", s_t[:], 0.0, [s_k])
                    S.act(junk[:, 0:256], p_t[:, 0:256], AF.Square, [], [s_k], [p_k], accum_out=s_t[:, 0:1])
                    S.act(junk[:, 256:384], p_t[:, 256:384], AF.Square, [s_k], [s_k], [p_k], accum_out=s_t[:, 1:2])
                    rstd_ops(S, s_t[:, 0:1], s_t[:, 2:3], s_k, s_k, 256)
                    rstd_ops(S, s_t[:, 1:2], s_t[:, 3:4], s_k, s_k, 128)
                    st["v"] = (p_t, p_k, s_t, s_k)

                def b_():
                    p_t, p_k, s_t, s_k = st["v"]
                    n_t, n_k = nrm.nxt()
                    q_t, q_k = ptr.nxt()
                    S.act(n_t[:, 0:256], p_t[:, 0:256], AF.Copy, [s_k], [n_k], [p_k], scale=s_t[:, 2:3])
                    S.act(n_t[:, 256:384], p_t[:, 256:384], AF.Copy, [s_k], [n_k], [p_k], scale=s_t[:, 3:4])
                    for c in range(3):
                        S.tr(q_t[:, c * 128:(c + 1) * 128], n_t[:, c * 128:(c + 1) * 128], K.identb[:], [n_k], [q_k])
                    for c in range(2):
                        S.ts("dve", qnT[:, c, ts_], q_t[:, c * 128:(c + 1) * 128], gcol[:, c:c + 1], None, ALU.mult, None, ["mgc"], ["mqnT"], [q_k])
                    S.ts("dve", kvnT[:, ts_], q_t[:, 256:384], gcol[:, 2:3], None, ALU.mult, None, ["mgc"], ["mkvnT"], [q_k])
                return a_, b_
            units.append(mk())
        _run_skewed(units, 1)
        pk = pring("mpk", 4, [128, 512], F32)
        t12 = sring("mt12", 4, [96, 512], F32)
        for (c0, W) in chunks512():
            cs = slice(c0, c0 + W)
            pa, pa_k = pk.nxt()
            pb, pb_k = pk.nxt()
            for kc in range(8):
                S.mm(pa[0:96, 0:W], wkr[:, 0, kc, :], HTs[:, kc, cs], kc == 0, kc == 7, ["mHT", "mw"], [pa_k])
            for kc in range(8):
                S.mm(pb[0:96, 0:W], wkr[:, 1, kc, :], HTs[:, kc, cs], kc == 0, kc == 7, ["mHT", "mw"], [pb_k])
            ta, ta_k = t12.nxt()
            tb, tb_k = t12.nxt()
            S.tt("dve", ta[64:96, 0:W], pa[64:96, 0:W], rC[64:96, cs], ALU.mult, ["mrC"], [ta_k], [pa_k])
            S.tt("dve", tb[64:96, 0:W], pb[64:96, 0:W], rS[64:96, cs], ALU.mult, ["mrC"], [tb_k], [pb_k])
            S.tt("pool", KRT[64:96, cs], ta[64:96, 0:W], tb[64:96, 0:W], ALU.add, [ta_k, tb_k], ["mKRT"])
        vt = sring("mvt", 2, [128, 512], BF16)
        wv = sb("mwv", [128, 512], BF16)
        S.dma("pool", wv[:, :].rearrange("p (h x) -> p h x", x=64),
              d["mla_w_kv_b"].rearrange("k (h x) -> k h x", x=128)[:, :, 64:128], wr=["mw"])
        for t in range(34):
            ts_ = slice(t * 128, (t + 1) * 128)
            p_t, p_k = pk.nxt()
            v_t, v_k = vt.nxt()
            S.mm(p_t[:], kvnT[:, ts_], wv[:], True, True, ["mkvnT", "mw"], [p_k])
            S.act(v_t[:], p_t[:], AF.Copy, [], [v_k], [p_k])
            S.dma("pool", d["V"][ts_, :], v_t[:], rd=[v_k], wr=["V"])
        qt = sring("mqt", 2, [96, 512], BF16)
        kt = sring("mkt", 2, [96, 512], BF16)
        for h in range(8):
            for (c0, W) in chunks512():
                cs = slice(c0, c0 + W)
                pq, pq_k = pk.nxt()
                pqp, pqp_k = pk.nxt()
                for kc in range(2):
                    S.mm(pq[0:96, 0:W], wq[:, kc, h * 96:(h + 1) * 96], qnT[:, kc, cs], kc == 0, kc == 1, ["mqnT", "mw"], [pq_k])
                for kc in range(2):
                    S.mm(pqp[0:96, 0:W], wqp[:, kc, h, :], qnT[:, kc, cs], kc == 0, kc == 1, ["mqnT", "mw"], [pqp_k])
                q_t, q_k = qt.nxt()
                ta, ta_k = t12.nxt()
                tb, tb_k = t12.nxt()
                S.act(q_t[0:64, 0:W], pq[0:64, 0:W], AF.Copy, [], [q_k], [pq_k])
                S.tt("dve", ta[64:96, 0:W], pq[64:96, 0:W], rC[64:96, cs], ALU.mult, ["mrC"], [ta_k], [pq_k])
                S.tt("dve", tb[64:96, 0:W], pqp[64:96, 0:W], rS[64:96, cs], ALU.mult, ["mrC"], [tb_k], [pqp_k])
                S.tt("pool", q_t[64:96, 0:W], ta[64:96, 0:W], tb[64:96, 0:W], ALU.add, [ta_k, tb_k, q_k], [q_k])
                S.dma("sp", d["QT"][h, :, cs], q_t[:, 0:W], rd=[q_k], wr=["QT"])
                pk2, pk2_k = pk.nxt()
                S.mm(pk2[0:64, 0:W], wkv[:, h * 128:h * 128 + 64], kvnT[:, cs], True, True, ["mkvnT", "mw"], [pk2_k])
                k_t, k_k = kt.nxt()
                S.act(k_t[0:64, 0:W], pk2[0:64, 0:W], AF.Copy, [], [k_k], [pk2_k])
                S.cp("pool", k_t[64:96, 0:W], KRT[64:96, cs], ["mKRT", k_k], [k_k])
                S.dma("sp", d["KT"][h, :, cs], k_t[:, 0:W], rd=[k_k], wr=["KT"])
        S.emit()


def phase_attn(K):
    S, d = K.S, K.d
    sc = 1.0 / math.sqrt(96.0)
    with ExitStack() as ph:
        sb, pp, sring, pring = _mk(K, ph)
        Vs = sb("aV", [128, 34, 8, 65], BF16)
        S.ms("pool", Vs[:], 1.0, ["aV"])
        vv = d["V"].rearrange("(t p) (h x) -> p t h x", p=128, x=64)
        for h in range(8):
            S.dma("sp", Vs[:, :, h, 0:64], vv[:, :, h, :], rd=["V"], wr=["aV"])
        onesf = sb("aones", [128, 64], F32)
        S.ms("pool", onesf[:], 1.0, ["aones"])
        QTh = sring("aQ", 2, [96, NTOK], BF16)
        KTh = sring("aK", 2, [96, NTOK], BF16)
        pT = sring("apT", 4, [128, 1024], BF16)
        rc = sring("arc", 2, [65, 512], F32)
        oT = sring("aoT", 2, [64, 512], BF16)
        ps = pring("aps", 2, [128, 1024], F32)
        po = pring("apo", 2, [128, 512], F32)
        pb = pring("apb", 2, [128, 512], F32)
        qch = [(0, 256, 2)] + [(256 + 512 * i, 512, 34) for i in range(8)]
        rbt = sring("arb", 2, [64, 512], F32)
        tail = [None]

        def make_tail(h, q0, W, o_t, o_k):
            def run():
                r_t, r_k = rc.nxt()
                b_t, b_k = pb.nxt()
                f_t, f_k = rbt.nxt()
                g_t, g_k = oT.nxt()
                S.act(r_t[64:65, 0:W], o_t[64:65, 0:W], AF.Copy, [], [r_k], [o_k])
                S.mm(b_t[0:64, 0:W], onesf[64:65, 0:64], r_t[64:65, 0:W], True, True, ["aones", r_k], [b_k])
                S.op("dve", lambda e: e.reciprocal(out=f_t[:, 0:W], in_=b_t[0:64, 0:W]), [], [f_k], [b_k])
                S.tt("dve", g_t[:, 0:W], o_t[0:64, 0:W], f_t[:, 0:W], ALU.mult, [f_k], [g_k], [o_k])
                S.dma("pool", d["OT"][(h % 2) * 64:(h % 2) * 64 + 64, h // 2, q0:q0 + W], g_t[:, 0:W], rd=[g_k], wr=["OT"])
            return run

        SK = 2
        for h in range(8):
            Q, Q_k = QTh.nxt()
            Kt, K_k = KTh.nxt()
            S.dma("sp", Q[:], d["QT"][h], rd=["QT"], wr=[Q_k])
            S.dma("sp", Kt[:], d["KT"][h], rd=["KT"], wr=[K_k])
            for (q0, W, nk) in qch:
                o_t, o_k = po.nxt()
                pend = []
                npair = nk // 2
                for kp in range(npair + SK):
                    if kp < npair:
                        s_t, s_k = ps.nxt()
                        e_t, e_k = pT.nxt()
                        for u_ in range(2):
                            kt_ = 2 * kp + u_
                            S.mm(s_t[:, u_ * 512:u_ * 512 + W], Kt[:, kt_ * 128:(kt_ + 1) * 128], Q[:, q0:q0 + W], True, True, [Q_k, K_k], [s_k])
                        if W == 512:
                            S.act(e_t[:, :], s_t[:, :], AF.Exp, [], [e_k], [s_k], scale=sc)
                        else:
                            S.act(e_t[:, :].rearrange("p (u c) -> p u c", c=512)[:, :, 0:W], s_t[:, :].rearrange("p (u c) -> p u c", c=512)[:, :, 0:W],
                                  AF.Exp, [], [e_k], [s_k], scale=sc)
                        pend.append((kp, e_t, e_k))
                    if kp >= SK:
                        pp_, e_t2, e_k2 = pend.pop(0)
                        for u_ in range(2):
                            kt_ = 2 * pp_ + u_
                            S.mm(o_t[0:65, 0:W], Vs[:, kt_, h, :], e_t2[:, u_ * 512:u_ * 512 + W], kt_ == 0, kt_ == nk - 1, ["aV", e_k2], [o_k])
                    if kp == min(2, npair - 1) and tail[0] is not None:
                        tail[0]()
                        tail[0] = None
                if tail[0] is not None:
                    tail[0]()
                tail[0] = make_tail(h, q0, W, o_t, o_k)
        tail[0]()
        S.emit()


HYSEQ = {0: dict(L=NLAT, NA=64, NF=128, tok0=NCTX, base=259), 1: dict(L=NCTX, NA=4, NF=8, tok0=0, base=1)}
FBG = [(g * 8, 8) for g in range(8)] + [(64, 1)]


def _load_adft(K, sb, S, seq, kind):
    if seq == 0:
        t = sb("hadft", [128, 3, 128], BF16)
        S.dma("pool", t[:], K.d["hyA0"][:, :, :], wr=["hadft"])
    else:
        rows = 64 if kind == "s" else 128
        t = sb("hadft" + kind, [rows, 3, 128], BF16)
        S.dma("pool", t[:], K.d["hyA1" + kind][:, :, :], wr=["hadft"])
    return t


def _fwd_unit(S, Re, Re_k, Im, Im_k, has_im, Kr, M, A, pz):
    z, z_k = pz.nxt()
    Cf, Sf, nSf = A[0:Kr, 0, 0:M], A[0:Kr, 1, 0:M], A[0:Kr, 2, 0:M]
    S.mm(z[0:M, 0:512], Cf, Re, True, not has_im, [Re_k, "hadft"], [z_k])
    if has_im:
        S.mm(z[0:M, 0:512], Sf, Im, False, True, [Im_k, "hadft"], [z_k])
        S.mm(z[0:M, 512:1024], Cf, Im, True, False, [Im_k, "hadft"], [z_k])
    S.mm(z[0:M, 512:1024], nSf, Re, not has_im, True, [Re_k, "hadft"], [z_k])
    return z, z_k


def _mid_unit(S, Re, Re_k, Im, Im_k, has_im, Kr, M, AF_, AI, k4, k4_k, Rout, R_k, pz, pr, Zs, Pt, Yt):
    st = {}

    def stage_a():
        z, z_k = _fwd_unit(S, Re, Re_k, Im, Im_k, has_im, Kr, M, AF_, pz)
        z_t, zt_k = Zs.nxt()
        p_t, p_k = Pt.nxt()
        y_t, y_k = Yt.nxt()
        S.act(z_t[0:M, :, :], z[0:M, :].rearrange("p (r c) -> p r c", c=512), AF.Copy, [], [zt_k], [z_k])
        S.tt("dve", p_t[0:M, 0:2, :], z_t[0:M, :, :], k4[:, 0:2, :], ALU.mult, [zt_k, k4_k], [p_k + "a"])
        S.tt("dve", p_t[0:M, 2:4, :], z_t[0:M, :, :], k4[:, 2:4, :], ALU.mult, [zt_k, k4_k], [p_k + "b"])
        S.tt("pool", y_t[0:M, 0, :], p_t[0:M, 0, :], p_t[0:M, 1, :], ALU.subtract, [p_k + "a"], [y_k + "r"])
        S.tt("pool", y_t[0:M, 1, :], p_t[0:M, 2, :], p_t[0:M, 3, :], ALU.add, [p_k + "b"], [y_k + "i"])
        st["y"] = (y_t, y_k)

    def stage_b():
        y_t, y_k = st["y"]
        r, r_k = pr.nxt()
        Ci, Si, nSi = AI[0:M, 0, 0:M], AI[0:M, 1, 0:M], AI[0:M, 2, 0:M]
        yk = [y_k + "r", y_k + "i", "hadft"]
        S.mm(r[0:M, 0:512], Ci, y_t[0:M, 0, :], True, False, yk, [r_k])
        S.mm(r[0:M, 0:512], nSi, y_t[0:M, 1, :], False, True, yk, [r_k])
        if has_im:
            S.mm(r[0:M, 512:1024], Si, y_t[0:M, 0, :], True, False, yk, [r_k])
            S.mm(r[0:M, 512:1024], Ci, y_t[0:M, 1, :], False, True, yk, [r_k])
            S.act(Rout, r[0:M, :].rearrange("p (r c) -> p r c", c=512), AF.Copy, [], [R_k], [r_k])
        else:
            S.act(Rout[:, 0, :], r[0:M, 0:512], AF.Copy, [], [R_k], [r_k])
    return stage_a, stage_b


def _k4_unit(S, Re, Re_k, Im, Im_k, has_im, Kr, M, A, pz, k4, k4_k):
    z, z_k = _fwd_unit(S, Re, Re_k, Im, Im_k, has_im, Kr, M, A, pz)
    S.act(k4[:, 0:2, :], z[0:M, :].rearrange("p (r c) -> p r c", c=512), AF.Copy, [], [k4_k], [z_k])
    S.cp("dve", k4[:, 3, :], z[0:M, 0:512], [], [k4_k], [z_k])
    S.cp("dve", k4[:, 2, :], z[0:M, 512:1024], [], [k4_k], [z_k])


def phase_hy_filters(K, seq):
    S, d = K.S, K.d
    q = HYSEQ[seq]
    L, NA, NF = q["L"], q["NA"], q["NF"]
    P2 = 2 * L
    npt = P2 // 128
    with ExitStack() as ph:
        sb, pp, sring, pring = _mk(K, ph)
        ZT = sb("hZT", [33, P2], F32)
        S.dma("sp", ZT[:], d["hyZ%d" % seq][:, :], wr=["hZT"])
        w1 = sb("hw1", [33, 64], F32)
        w2 = sb("hw2", [64, 64], F32)
        w3 = sb("hw3", [64, 2048], F32)
        bb = sb("hbb", [64, 2], F32)
        S.dma("sp", w1[:], d["hy_w1"][:, :], wr=["hw"])
        S.dma("sp", w2[:], d["hy_w2"][:, :], wr=["hw"])
        S.dma("sp", w3[:], d["hy_w3"][:, :], wr=["hw"])
        S.dma("sp", bb[:], d["hy_b_col"][:, :], wr=["hw"])
        ntn = sb("hntn", [128, npt], F32)
        S.dma("sp", ntn[:], d["hyT%d" % seq][:, :], wr=["hw"])
        dl = sb("hdl", [128, 512], F32)
        S.dma("sp", dl[:], d["hy_delta"][0:1, :].to_broadcast([128, 512]), wr=["hw"])
        Fb2 = sb("hFb2", [128, 128], BF16)
        S.dma("pool", Fb2[:], d["hyFb2"][:, :], wr=["hw"])
        A = _load_adft(K, sb, S, seq, "k")
        a2T = sb("ha2T", [64, P2], F32)
        pm = pring("hpm", 1, [128, 512], F32)
        ut = sring("hut", 2, [64, 512], F32)
        s2t = sring("hs2", 2, [64, 512], F32)
        s4t = sring("hs4", 2, [64, 512], F32)
        a1t = sring("ha1", 2, [64, 512], F32)

        def sinop(p_t, p_k, bcol, out, out_k, W):
            u, u_k = ut.nxt()
            a, a_k = s2t.nxt()
            b, b_k = s4t.nxt()
            S.act(u[:, 0:W], p_t[0:64, 0:W], AF.Identity, ["hw"], [u_k], [p_k], bias=bcol)
            S.act(a[:, 0:W], u[:, 0:W], AF.Sin, [u_k], [a_k], scale=0.5)
            S.act(b[:, 0:W], u[:, 0:W], AF.Sin, [u_k], [b_k], scale=0.25)
            S.tt("dve", b[:, 0:W], b[:, 0:W], b[:, 0:W], ALU.mult, [b_k], [b_k])
            S.ts("dve", b[:, 0:W], b[:, 0:W], -2.0, 1.0, ALU.mult, ALU.add, [b_k], [b_k])
            S.stt("dve", out, a[:, 0:W], 2.0, b[:, 0:W], ALU.mult, ALU.mult, [a_k, b_k], [out_k])

        W = min(512, P2)
        for c in range(P2 // W):
            cs = slice(c * W, (c + 1) * W)
            p_t, p_k = pm.nxt()
            a1, a1_k = a1t.nxt()
            S.mm(p_t[0:64, 0:W], w1[:, :], ZT[:, cs], True, True, ["hw", "hZT"], [p_k])
            sinop(p_t, p_k, bb[:, 0:1], a1[:, 0:W], a1_k, W)
            p_t, p_k = pm.nxt()
            S.mm(p_t[0:64, 0:W], w2[:, :], a1[:, 0:W], True, True, ["hw", a1_k], [p_k])
            sinop(p_t, p_k, bb[:, 1:2], a2T[:, cs], "ha2T", W)
        S.ms("pool", a2T[:, L:L + 1], 0.0, ["ha2T"])
        pf = pring("hpf", 1, [128, 512], F32)
        px = pring("hpx", 2, [128, 512], F32)
        dec = sring("hdec", 2, [128, 512], F32)
        k2 = sring("hk2", 3, [128, 512], BF16)
        xk = sring("hxk", 3, [128, 512], BF16)
        pz = pring("hpz", 2, [128, 1024], F32)
        if seq == 0:
            Ret = sring("hRe", 2, [NF, 8, 512], BF16)
            Imt = sring("hIm", 2, [NF, 8, 512], BF16)
            K4 = sring("hK4", 2, [NF, 8, 4, 512], BF16)
        else:
            Ret = sring("hRe", 2, [128, 512], BF16)
            Imt = sring("hIm", 2, [128, 512], BF16)
            K4 = sring("hK4", 2, [128, 4, 512], BF16)
        D1k = d["D1k"] if seq == 0 else d["D1k1"]
        KH = d["KH%d" % seq]
        for o in range(2):
            units = []
            for pt_ in range(npt):
                def mk(pt_=pt_):
                    st = {}

                    def a_():
                        dirn = 0 if pt_ * 128 < L else 1
                        f_t, f_k = pf.nxt()
                        e_t, e_k = dec.nxt()
                        k_t, k_k = k2.nxt()
                        S.mm(f_t[:], a2T[:, pt_ * 128:(pt_ + 1) * 128], w3[:, o * 1024 + dirn * 512:o * 1024 + dirn * 512 + 512], True, True,
                             ["ha2T", "hw"], [f_k])
                        S.act(e_t[:], dl[:], AF.Exp, ["hw"], [e_k], scale=ntn[:, pt_:pt_ + 1])
                        S.tt("dve", k_t[:], f_t[:], e_t[:], ALU.mult, [e_k], [k_k], [f_k])
                        st["k"] = (k_t, k_k)

                    def b_():
                        k_t, k_k = st["k"]
                        for half in range(2):
                            x_t, x_k = px.nxt()
                            y_t, y_k = xk.nxt()
                            hs = slice(half * 64, half * 64 + 64)
                            S.mm(x_t[:], Fb2[hs, :], k_t[hs, :], True, True, ["hw", k_k], [x_k])
                            S.act(y_t[:], x_t[:], AF.Copy, [], [y_k], [x_k])
                            al = 2 * pt_ + half
                            S.dma("sp", D1k[al, :, :] if seq == 0 else D1k[:, al, :], y_t[:], rd=[y_k], wr=["D1k"])
                    return a_, b_
                units.append(mk())
            _run_skewed(units, 1)
            if seq == 0:
                for (fb0, G) in FBG:
                    Re, Re_k = Ret.nxt()
                    Im, Im_k = Imt.nxt()
                    k4, k4_k = K4.nxt()
                    S.dma("sp", Re[:, 0:G, :], D1k[0:NF, fb0:fb0 + G, :], rd=["D1k"], wr=[Re_k])
                    if fb0 < 64:
                        S.dma("sp", Im[:, 0:G, :], D1k[0:NF, 64 + fb0:64 + fb0 + G, :], rd=["D1k"], wr=[Im_k])
                    for g in range(G):
                        fb = fb0 + g
                        _k4_unit(S, Re[:, g, :], Re_k, Im[:, g, :], Im_k, fb not in (0, 64), NF, NF, A, pz, k4[:, g, :, :], k4_k + "g%d" % g)
                    S.dma("sp", KH[o, :, fb0:fb0 + G, :, :], k4[:, 0:G, :, :], rd=[k4_k + "g%d" % g for g in range(G)], wr=["KH%d" % seq])
            else:
                for fb0 in (0, 16, 32, 48, 64):
                    G = 16 if fb0 < 64 else 1
                    R_ = G * NF
                    Re, Re_k = Ret.nxt()
                    Im, Im_k = Imt.nxt()
                    k4, k4_k = K4.nxt()
                    S.dma("sp", Re[0:R_, :], D1k[fb0:fb0 + G].rearrange("f a c -> (f a) c"), rd=["D1k"], wr=[Re_k])
                    if fb0 < 64:
                        S.dma("sp", Im[0:R_, :], D1k[64 + fb0:64 + fb0 + G].rearrange("f a c -> (f a) c"), rd=["D1k"], wr=[Im_k])
                        if fb0 == 0:
                            S.ms("pool", Im[0:NF, :], 0.0, [Im_k])
                    _k4_unit(S, Re[0:R_, :], Re_k, Im[0:R_, :], Im_k, fb0 < 64, R_, R_, A, pz, k4[0:R_, :, :], k4_k)
                    S.dma("sp", KH[o, fb0:fb0 + G].rearrange("f a k c -> (f a) k c"), k4[0:R_, :, :], rd=[k4_k], wr=["KH%d" % seq])
        S.emit()


def _s1_store(S, Fb2, src_bf, src_k, px, xk, dst, a0, seq=0):
    for half in range(2):
        x_t, x_k = px.nxt()
        y_t, y_k = xk.nxt()
        hs = slice(half * 64, half * 64 + 64)
        S.mm(x_t[:], Fb2[hs, :], src_bf[hs, :], True, True, ["hFb2", src_k], [x_k])
        S.act(y_t[:], x_t[:], AF.Copy, [], [y_k], [x_k])
        S.dma("sp", dst[a0 + half, :, :] if seq == 0 else dst[:, a0 + half, :], y_t[:], rd=[y_k], wr=["D1"])


def phase_hy_in(K):
    S, d = K.S, K.d
    with ExitStack() as ph:
        sb, pp, sring, pring = _mk(K, ph)
        HTs = sb("iHT", [128, 8, NTOK + 4], BF16)
        for c in (0, 257, 258, NTOK + 3):
            S.ms("pool", HTs[:, :, c:c + 1], 0.0, ["iHT"])
        for kc in range(8):
            S.dma("sp", HTs[:, kc, 1:257], d["HT"][:, kc, 0:NCTX], rd=["HT"], wr=["iHT"])
            S.dma("sp", HTs[:, kc, 259:259 + NLAT], d["HT"][:, kc, NCTX:NTOK], rd=["HT"], wr=["iHT"])
        Fb2 = sb("iFb2", [128, 128], BF16)
        S.dma("pool", Fb2[:], d["hyFb2"][:, :], wr=["hFb2"])
        cwbc = sb("icw", [128, 3, 1536], F32)
        for tap in range(3):
            S.dma("sp", cwbc[:, tap, :], d["hy_conv_w"][tap:tap + 1, :].to_broadcast([128, 1536]), wr=["icw"])
        brow = sb("ibrow", [1, 1536], BF16)
        S.dma("pool", brow[:], d["hy_conv_b"][0:1, :], wr=["ibrow"])
        ones = sb("iones", [1, 128], BF16)
        S.ms("pool", ones[:], 1.0, ["iones"])
        win = d["ab_w_in"].rearrange("(kc p) n -> p kc n", p=128)
        stg = sring("istg", 2, [128, 8, 512], F32)
        W3 = sring("iW3", 2, [128, 3, 8, 512], BF16)
        pz = pring("ipz", 3, [128, 512], F32)
        px = pring("ipx", 2, [128, 512], F32)
        xk = sring("ixk", 3, [128, 512], BF16)
        vb = sring("ivb", 2, [128, 512], BF16)
        vf = sring("ivf", 3, [128, 512], F32)
        for n in range(3):
            g_t, g_k = stg.nxt()
            w_t, w_k = W3.nxt()
            S.dma("sp", g_t[:], win[:, :, 416 + n * 512:416 + (n + 1) * 512], wr=[g_k])
            for tap in range(3):
                for kc in range(8):
                    S.tt("dve" if (kc % 2 == 0) else "pool", w_t[:, tap, kc, :], g_t[:, kc, :], cwbc[:, tap, n * 512:(n + 1) * 512],
                         ALU.mult, [g_k, "icw"], [w_k])
            for seq in (1, 0):
                q = HYSEQ[seq]
                for t in range(q["L"] // 128):
                    p_t, p_k = pz.nxt()
                    first = True
                    for tap in range(3):
                        c0 = q["base"] + t * 128 + tap - 1
                        for kc in range(8):
                            S.mm(p_t[:], HTs[:, kc, c0:c0 + 128], w_t[:, tap, kc, :], first, False, ["iHT", w_k], [p_k])
                            first = False
                    S.mm(p_t[:], ones[0:1, :], brow[0:1, n * 512:(n + 1) * 512], False, True, ["iones", "ibrow"], [p_k])
                    rows = slice(q["tok0"] + t * 128, q["tok0"] + (t + 1) * 128)
                    f_t, f_k = vf.nxt()
                    S.cp("dve", f_t[:], p_t[:], [], [f_k], [p_k])
                    S.dma("pool", d["VX"][rows, n, :], f_t[:], rd=[f_k], wr=["VX"])
                    if n == 0:
                        b_t, b_k = vb.nxt()
                        S.act(b_t[:], p_t[:], AF.Copy, [], [b_k], [p_k])
                        _s1_store(S, Fb2, b_t, b_k, px, xk, d["D1_%d" % seq], 2 * t, seq)
        S.emit()


def phase_hy_mid(K, seq, o):
    S, d = K.S, K.d
    q = HYSEQ[seq]
    NA, NF = q["NA"], q["NF"]
    with ExitStack() as ph:
        sb, pp, sring, pring = _mk(K, ph)
        Zs = sring("mZs", 3, [128, 2, 512], BF16)
        Pt = sring("mP", 3, [128, 4, 512], BF16)
        Yt = sring("mY", 3, [128, 2, 512], BF16)
        pz = pring("mpz", 2, [128, 1024], F32)
        pr = pring("mpr", 2, [128, 1024], F32)
        D1, D2, KH = d["D1_%d" % seq], d["D2_%d" % seq], d["KH%d" % seq]
        if seq == 0:
            A = _load_adft(K, sb, S, 0, "k")
            Ret = sring("mRe", 2, [NA, 8, 512], BF16)
            Imt = sring("mIm", 2, [NA, 8, 512], BF16)
            K4g = sring("mK4", 2, [NF, 8, 4, 512], BF16)
            Rg = sring("mRg", 2, [NF, 2, 8, 512], BF16)
            units = []
            for (fb0, G) in FBG:
                def load(fb0=fb0, G=G):
                    Re, Re_k = Ret.nxt()
                    Im, Im_k = Imt.nxt()
                    kg, kg_k = K4g.nxt()
                    rg, rg_k = Rg.nxt()
                    S.dma("sp", Re[:, 0:G, :], D1[0:NA, fb0:fb0 + G, :], rd=["D1"], wr=[Re_k])
                    if fb0 < 64:
                        S.dma("sp", Im[:, 0:G, :], D1[0:NA, 64 + fb0:64 + fb0 + G, :], rd=["D1"], wr=[Im_k])
                    S.dma("sp", kg[:, 0:G, :, :], KH[o, :, fb0:fb0 + G, :, :], rd=["KH%d" % seq], wr=[kg_k])
                    return Re, Re_k, Im, Im_k, kg, kg_k, rg, rg_k
                grp = {}
                for g in range(G):
                    fb = fb0 + g

                    def mk(fb=fb, g=g, fb0=fb0, G=G, grp=grp, load=load):
                        ab = {}

                        def a_():
                            if g == 0:
                                grp["t"] = load()
                            Re, Re_k, Im, Im_k, kg, kg_k, rg, rg_k = grp["t"]
                            ab["u"] = _mid_unit(S, Re[:, g, :], Re_k, Im[:, g, :], Im_k, fb not in (0, 64), NA, NF, A, A,
                                                kg[:, g, :, :], kg_k, rg[:, :, g, :], rg_k + "g%d" % g, pz, pr, Zs, Pt, Yt)
                            ab["u"][0]()

                        def b_():
                            ab["u"][1]()
                            if g == G - 1:
                                Re, Re_k, Im, Im_k, kg, kg_k, rg, rg_k = grp["t"]
                                rk = [rg_k + "g%d" % q_ for q_ in range(G)]
                                S.dma("sp", D2[0:NF, fb0:fb0 + G, :], rg[:, 0, 0:G, :], rd=rk, wr=["D2"])
                                if fb0 < 64:
                                    s0 = 1 if fb0 == 0 else 0
                                    S.dma("sp", D2[0:NF, 64 + fb0 + s0:64 + fb0 + G, :], rg[:, 1, s0:G, :], rd=rk, wr=["D2"])
                        return a_, b_
                    units.append(mk())
            _run_skewed(units, 2)
        else:
            As = _load_adft(K, sb, S, 1, "s")
            Ak = _load_adft(K, sb, S, 1, "k")
            Ret = sring("mRe", 2, [64, 512], BF16)
            Imt = sring("mIm", 2, [64, 512], BF16)
            K4g = sring("mK4", 2, [128, 4, 512], BF16)
            Rg = sring("mRg", 2, [128, 2, 512], BF16)
            for fb0 in (0, 16, 32, 48, 64):
                G = 16 if fb0 < 64 else 1
                Kr, M = G * NA, G * NF
                Re, Re_k = Ret.nxt()
                Im, Im_k = Imt.nxt()
                kg, kg_k = K4g.nxt()
                rg, rg_k = Rg.nxt()
                S.dma("sp", Re[0:Kr, :], D1[fb0:fb0 + G].rearrange("f a c -> (f a) c"), rd=["D1"], wr=[Re_k])
                if fb0 < 64:
                    S.dma("sp", Im[0:Kr, :], D1[64 + fb0:64 + fb0 + G].rearrange("f a c -> (f a) c"), rd=["D1"], wr=[Im_k])
                    if fb0 == 0:
                        S.ms("pool", Im[0:NA, :], 0.0, [Im_k])
                S.dma("sp", kg[0:M, :, :], KH[o, fb0:fb0 + G].rearrange("f a k c -> (f a) k c"), rd=["KH%d" % seq], wr=[kg_k])
                ua, ub = _mid_unit(S, Re[0:Kr, :], Re_k, Im[0:Kr, :], Im_k, fb0 < 64, Kr, M, As, Ak, kg[0:M, :, :], kg_k,
                                   rg[0:M, :, :], rg_k, pz, pr, Zs, Pt, Yt)
                ua()
                ub()
                S.dma("sp", D2[fb0:fb0 + G].rearrange("f a c -> (f a) c"), rg[0:M, 0, :], rd=[rg_k], wr=["D2"])
                if fb0 < 64:
                    s0 = 1 if fb0 == 0 else 0
                    S.dma("sp", D2[64 + fb0 + s0:64 + fb0 + G].rearrange("f a c -> (f a) c"), rg[s0 * NF:M, 1, :], rd=[rg_k], wr=["D2"])
        S.emit()


def phase_hy_out(K, seq, o):
    S, d = K.S, K.d
    q = HYSEQ[seq]
    NA, NF = q["NA"], q["NF"]
    with ExitStack() as ph:
        sb, pp, sring, pring = _mk(K, ph)
        Gb = sb("oGb", [128, 128], BF16)
        S.dma("pool", Gb[:], d["hyGb%d" % seq][:, :], wr=["oGb"])
        Fb2 = sb("oFb2", [128, 128], BF16)
        S.dma("pool", Fb2[:], d["hyFb2"][:, :], wr=["hFb2"])
        dbc = sb("odbc", [128, 512], F32)
        S.dma("sp", dbc[:], d["hy_bias"][o:o + 1, :].to_broadcast([128, 512]), wr=["odbc"])
        Tt = sring("oT", 2, [128, 3, 512], BF16)
        ut = sring("ou", 2, [128, 512], F32)
        xt = sring("oxg", 3, [128, 512], F32)
        tt_ = sring("ot", 3, [128, 512], F32)
        zb = sring("ozb", 2, [128, 512], BF16)
        xk = sring("oxk", 3, [128, 512], BF16)
        yo = sring("oyo", 2, [128, 4, 128], BF16)
        py = pring("opy", 2, [128, 512], F32)
        px = pring("opx", 2, [128, 512], F32)
        ptr = pring("optr", 2, [128, 1024], BF16)
        D2 = d["D2_%d" % seq]
        D2v = D2.rearrange("a f c -> f a c") if seq == 0 else D2
        units = []
        for i in range(NA // 2):
            def mk(i=i):
                st = {}

                def a_():
                    A0 = 2 * i
                    T, T_k = Tt.nxt()
                    if i == 0:
                        S.dma("sp", T[:, 0:1, :], D2v[:, NF - 1:NF, :], rd=["D2"], wr=[T_k])
                        S.dma("sp", T[:, 1:3, :], D2v[:, 0:2, :], rd=["D2"], wr=[T_k])
                    else:
                        S.dma("sp", T[:], D2v[:, A0 - 1:A0 + 2, :], rd=["D2"], wr=[T_k])
                    y_t, y_k = py.nxt()
                    for s_ in range(2):
                        rs_ = slice(s_ * 64, s_ * 64 + 64)
                        S.mm(y_t[rs_, :], Gb[:, 0:64], T[:, 1 + s_, :], True, False, ["oGb", T_k], [y_k])
                        S.mm(y_t[rs_, :], Gb[:, 64:128], T[:, s_, :], False, True, ["oGb", T_k], [y_k])
                    rows = slice(q["tok0"] + i * 128, q["tok0"] + (i + 1) * 128)
                    u_t, u_k = ut.nxt()
                    g_t, g_k = xt.nxt()
                    t_t, t_k = tt_.nxt()
                    if o == 0:
                        S.dma("sp", u_t[:], d["VX"][rows, 0, :], rd=["VX"], wr=[u_k])
                        S.dma("sp", g_t[:], d["VX"][rows, 1, :], rd=["VX"], wr=[g_k])
                    else:
                        S.dma("sp", u_t[:], d["Z1"][rows, :], rd=["Z1"], wr=[u_k])
                        S.dma("sp", g_t[:], d["VX"][rows, 2, :], rd=["VX"], wr=[g_k])
                    S.tt("pool", t_t[:], u_t[:], dbc[:], ALU.mult, [u_k, "odbc"], [t_k])
                    S.tt("dve", t_t[:], t_t[:], y_t[:], ALU.add, [t_k], [t_k], [y_k])
                    st['v'] = (t_t, t_k, g_t, g_k, rows, A0)

                def b_():
                    t_t, t_k, g_t, g_k, rows, A0 = st['v']
                    if o == 0:
                        S.tt("pool", t_t[:], t_t[:], g_t[:], ALU.mult, [t_k, g_k], [t_k])
                        S.dma("pool", d["Z1"][rows, :], t_t[:], rd=[t_k], wr=["Z1"])
                        b_t, b_k = zb.nxt()
                        S.act(b_t[:], t_t[:], AF.Copy, [t_k], [b_k])
                        _s1_store(S, Fb2, b_t, b_k, px, xk, d["D1_%d" % seq], A0, seq)
                    else:
                        b_t, b_k = zb.nxt()
                        S.tt("pool", b_t[:], t_t[:], g_t[:], ALU.mult, [t_k, g_k], [b_k])
                        p_t, p_k = ptr.nxt()
                        for cc in range(4):
                            S.tr(p_t[:, cc * 128:(cc + 1) * 128], b_t[:, cc * 128:(cc + 1) * 128], K.identb[:], [b_k], [p_k])
                        o_t, o_k = yo.nxt()
                        S.act(o_t[:, :, :], p_t[:, 0:512].rearrange("p (c t) -> p c t", t=128), AF.Copy, [], [o_k], [p_k])
                        S.dma("pool", d["OT"][:, 4:8, rows], o_t[:], rd=[o_k], wr=["OT"])

                return a_, b_
            units.append(mk())
        _run_skewed(units, 1)
        S.emit()


def hyena_all(K):
    for seq in (0, 1):
        phase_hy_filters(K, seq)
    phase_hy_in(K)
    for seq in (0, 1):
        for o in range(2):
            phase_hy_mid(K, seq, o)
            phase_hy_out(K, seq, o)


def phase_hg_proj(K):
    S, d = K.S, K.d
    with ExitStack() as ph:
        sb, pp, sring, pring = _mk(K, ph)
        win = sb("gwin", [128, 8, 5120], BF16)
        wv = d["hg_w_in"].rearrange("(kc p) n -> p kc n", p=128)
        for c in range(10):
            S.dma("pool", win[:, :, c * 512:(c + 1) * 512], wv[:, :, c * 512:(c + 1) * 512], wr=["gwin"])
        lg = sb("glg", [128, 2, 2, DM], F32)
        for l in range(2):
            for dd in range(2):
                S.dma("sp", lg[:, l, dd, :], d["hg_lb_logits"][l, dd:dd + 1, :].to_broadcast([128, DM]), wr=["glg"])
        lb = sb("glb", [128, 2, DM], F32)
        oml = sb("goml", [128, 2, DM], F32)
        S.tt("dve", lg[:, 0, :, :], lg[:, 1, :, :], lg[:, 0, :, :], ALU.subtract, ["glg"], ["glg"])
        S.act(lb[:], lg[:, 0, :, :], AF.Sigmoid, ["glg"], ["glb"])
        S.act(oml[:], lg[:, 0, :, :], AF.Sigmoid, ["glg"], ["glb"], scale=-1.0)
        hT = sring("ghT", 2, [128, 8, 512], BF16)
        pz = pring("gpz", 4, [128, 512], F32)
        ob = sring("gob", 4, [128, DM], BF16)
        of = sring("gof", 3, [128, DM], F32)
        sg = sring("gsg", 2, [128, DM], F32)
        for t in range(34):
            rows = slice(t * 128, (t + 1) * 128)
            jj = t if t < 2 else (t - 2) % 4
            if jj == 0:
                h_t, h_k = hT.nxt()
                wd_ = 256 if t < 2 else 512
                S.dma("sp", h_t[:, :, 0:wd_], d["HT"][:, :, t * 128:t * 128 + wd_], rd=["HT"], wr=[h_k])
            for grp in range(5):
                if grp == 4 and t < 2:
                    continue
                pzs = []
                for nh in range(2):
                    p_t, p_k = pz.nxt()
                    c0 = grp * 1024 + nh * 512
                    for kc in range(8):
                        S.mm(p_t[:], h_t[:, kc, jj * 128:(jj + 1) * 128], win[:, kc, c0:c0 + 512], kc == 0, kc == 7, [h_k, "gwin"], [p_k])
                    pzs.append((p_t, p_k))
                if grp in (0, 4):
                    o_t, o_k = ob.nxt()
                    for nh in range(2):
                        S.act(o_t[:, nh * 512:(nh + 1) * 512], pzs[nh][0][:], AF.Silu, [], [o_k], [pzs[nh][1]])
                    if grp == 0:
                        S.dma("pool", d["QS"][rows, :], o_t[:], rd=[o_k], wr=["QS"])
                    else:
                        S.dma("pool", d["GG"][t * 128 - NCTX:(t + 1) * 128 - NCTX, :], o_t[:], rd=[o_k], wr=["GG"])
                elif grp == 3:
                    o_t, o_k = ob.nxt()
                    for nh in range(2):
                        S.cp("dve", o_t[:, nh * 512:(nh + 1) * 512], pzs[nh][0][:], [], [o_k], [pzs[nh][1]])
                    S.dma("pool", d["VV"][rows, :], o_t[:], rd=[o_k], wr=["VV"])
                else:
                    dd = grp - 1
                    s_t, s_k = sg.nxt()
                    f_t, f_k = of.nxt()
                    o_t, o_k = ob.nxt()
                    for nh in range(2):
                        S.act(s_t[:, nh * 512:(nh + 1) * 512], pzs[nh][0][:], AF.Sigmoid, [], [s_k], [pzs[nh][1]])
                    S.tt("dve", s_t[:], s_t[:], oml[:, dd, :], ALU.mult, [s_k, "glb"], [s_k])
                    S.tt("pool", s_t[:], s_t[:], lb[:, dd, :], ALU.add, [s_k, "glb"], [s_k])
                    S.act(f_t[:], s_t[:], AF.Ln, [s_k], [f_k])
                    S.ts("pool", o_t[:], s_t[:], -1.0, 1.0, ALU.mult, ALU.add, [s_k], [o_k])
                    S.dma("pool", d["LF"][rows, dd, :], f_t[:], rd=[f_k], wr=["LF"])
                    S.dma("pool", d["KK"][rows, dd, :], o_t[:], rd=[o_k], wr=["KK"])
        S.emit()


def phase_hg_scan(K, dd):
    S, d = K.S, K.d
    with ExitStack() as ph:
        sb, pp, sring, pring = _mk(K, ph)
        C = sb("sC", [128, 3, 128], F32)
        S.dma("sp", C[:], d["hgC"][:, dd, 0:3, :], wr=["sC"])
        TRI, TRIM, REV = (C[:, i, :] for i in range(3))
        MK = sb("sMK", [128, 2, 512], F32)
        for j in range(4):
            S.dma("sp", MK[:, :, j * 128:(j + 1) * 128], d["hgC"][:, dd, 3:5, :], wr=["sC"])
        PB, NB = MK[:, 0, :], MK[:, 1, :]
        St = sb("sS", [128, 8, 128], F32)
        Sb = sb("sSb", [128, 8, 128], BF16)
        S.ms("pool", St[:], 0.0, ["sS0", "sS1"])
        S.ms("pool", Sb[:], 0.0, ["sSb0", "sSb1"])
        lf = sring("slf", 2, [128, DM], F32)
        kk = sring("skk", 2, [128, DM], BF16)
        qs = sring("sqs", 2, [128, DM], BF16)
        vv = sring("svv", 2, [128, DM], BF16)
        E1 = sring("sE1", 4, [128, 512], F32)
        E2 = sring("sE2", 2, [128, 512], F32)
        E3 = sring("sE3", 2, [128, 512], F32)
        Er = sring("sEr", 2, [128, 512], F32)
        kb = sring("skb", 2, [128, DM], BF16)
        q1r = sring("sq1", 4, [128, 512], BF16)
        q2r = sring("sq2", 2, [128, 512], BF16)
        k2r = sring("sk2", 2, [128, 512], BF16)
        am = sring("sam", 2, [128, 512], F32)
        ab = sring("sab", 2, [128, 512], BF16)
        tmpS = sring("stS", 2, [128, 4, 128], F32)
        ot = sring("sot", 2, [128, DM], F32)
        pcum = pring("spcu", 1, [128, 512], F32)
        pcr = pring("spcr", 1, [128, 512], F32)
        prv = pring("sprv", 1, [128, 512], F32)
        ptp = pring("sptp", 1, [128, 1024], BF16)
        pat = pring("spat", 1, [128, 512], F32)
        po = pring("spo", 2, [128, 512], F32)
        pS = pring("spS", 1, [128, 512], F32)
        order = [0, 1] + list(range(2, 34)) if dd == 0 else [1, 0] + list(range(33, 1, -1))
        chunks = (0, 1) if dd == 0 else (1, 0)
        endcol = (63, 127) if dd == 0 else (0, 64)
        dst = d["OF"] if dd == 0 else d["OB"]
        for t in order:
            rows = slice(t * 128, (t + 1) * 128)
            need_o = t >= 2
            lf_t, lf_k = lf.nxt()
            kk_t, kk_k = kk.nxt()
            vv_t, vv_k = vv.nxt()
            S.dma("sp", lf_t[:], d["LF"][rows, dd, :], rd=["LF"], wr=[lf_k])
            S.dma("sp", kk_t[:], d["KK"][rows, dd, :], rd=["KK"], wr=[kk_k])
            S.dma("sp", vv_t[:], d["VV"][rows, :], rd=["VV"], wr=[vv_k])
            if need_o:
                qs_t, qs_k = qs.nxt()
                S.dma("sp", qs_t[:], d["QS"][rows, :], rd=["QS"], wr=[qs_k])
                o_sb, o_sk = ot.nxt()
            kb_t, kb_k = kb.nxt()
            G = []
            for g in range(2):
                cs = slice(g * 512, (g + 1) * 512)
                st = {}
                c_t, c_k = pcum.nxt()
                for j in range(4):
                    hs = slice((4 * g + j) * 128, (4 * g + j + 1) * 128)
                    S.mm(c_t[:, j * 128:(j + 1) * 128], lf_t[:, hs], TRI, True, True, ["sC", lf_k], [c_k])
                e1, e1_k = E1.nxt()
                S.act(e1[:], c_t[:], AF.Exp, [], [e1_k], [c_k])
                r_t, r_k = prv.nxt()
                S.mm(r_t[:], REV, lf_t[:, cs], True, True, ["sC", lf_k], [r_k])
                er, er_k = Er.nxt()
                S.act(er[:], r_t[:], AF.Exp, [], [er_k], [r_k])
                S.tt("pool", kb_t[:, cs], kk_t[:, cs], er[:], ALU.mult, [kk_k, er_k], [kb_k + "g%d" % g])
                st.update(e1=e1, e1_k=e1_k)
                if need_o:
                    cr_t, cr_k = pcr.nxt()
                    for j in range(4):
                        hs = slice((4 * g + j) * 128, (4 * g + j + 1) * 128)
                        S.mm(cr_t[:, j * 128:(j + 1) * 128], lf_t[:, hs], TRIM, True, True, ["sC", lf_k], [cr_k])
                    e2, e2_k = E2.nxt()
                    e3, e3_k = E3.nxt()
                    S.act(e2[:], cr_t[:], AF.Exp, [], [e2_k], [cr_k])
                    S.act(e3[:], cr_t[:], AF.Exp, [], [e3_k], [cr_k], scale=-1.0)
                    tp, tp_k = ptp.nxt()
                    for j in range(4):
                        hs = slice((4 * g + j) * 128, (4 * g + j + 1) * 128)
                        S.tr(tp[:, j * 128:(j + 1) * 128], qs_t[:, hs], K.identb[:], [qs_k], [tp_k])
                        S.tr(tp[:, 512 + j * 128:512 + (j + 1) * 128], kk_t[:, hs], K.identb[:], [kk_k], [tp_k])
                    q1, q1_k = q1r.nxt()
                    q2, q2_k = q2r.nxt()
                    k2, k2_k = k2r.nxt()
                    S.tt("dve", q1[:], tp[:, 0:512], e1[:], ALU.mult, [e1_k], [q1_k], [tp_k])
                    S.tt("dve", q2[:], tp[:, 0:512], e2[:], ALU.mult, [e2_k], [q2_k], [tp_k])
                    S.tt("dve", k2[:], tp[:, 512:1024], e3[:], ALU.mult, [e3_k], [k2_k], [tp_k])
                    a_t, a_k = pat.nxt()
                    for j in range(4):
                        js = slice(j * 128, (j + 1) * 128)
                        S.mm(a_t[:, js], k2[:, js], q2[:, js], True, True, [k2_k, q2_k], [a_k])
                    m_t, m_k = am.nxt()
                    b_t, b_k = ab.nxt()
                    S.tt("dve", m_t[:], a_t[:], NB, ALU.max, ["sC"], [m_k], [a_k])
                    S.tt("dve", b_t[:], m_t[:], PB, ALU.min, ["sC", m_k], [b_k])
                    o_t, o_k = po.nxt()
                    for j in range(4):
                        js = slice(j * 128, (j + 1) * 128)
                        hs = slice((4 * g + j) * 128, (4 * g + j + 1) * 128)
                        S.mm(o_t[:, js], b_t[:, js], vv_t[:, hs], j == 0, False, [b_k, vv_k], [o_k], skip_group_check=True)
                    st.update(q1=q1, q1_k=q1_k, o_t=o_t, o_k=o_k)
                G.append(st)
            for ci, c in enumerate(chunks):
                rs_ = slice(c * 64, c * 64 + 64)
                for g in range(2):
                    st = G[g]
                    e1, e1_k = st["e1"], st["e1_k"]
                    if need_o:
                        for j in range(4):
                            js = slice(j * 128, (j + 1) * 128)
                            S.mm(st["o_t"][rs_, js], st["q1"][:, j * 128 + c * 64:j * 128 + c * 64 + 64], Sb[:, 4 * g + j, :], False,
                                 (ci == 1 and j == 3), [st["q1_k"], "sSb%d" % g], [st["o_k"]], skip_group_check=True)
                    s_t, s_k = pS.nxt()
                    for j in range(4):
                        js = slice(j * 128, (j + 1) * 128)
                        hs = slice((4 * g + j) * 128, (4 * g + j + 1) * 128)
                        S.mm(s_t[:, js], kb_t[rs_, hs], vv_t[rs_, hs], j == 0, j == 3, [kb_k + "g%d" % g, vv_k], [s_k], skip_group_check=True)
                    x_t, x_k = tmpS.nxt()
                    ev = e1[:, :].rearrange("p (j t) -> p j t", t=128)[:, :, endcol[c]:endcol[c] + 1].to_broadcast([128, 4, 128])
                    S.tt("dve", x_t[:], St[:, 4 * g:4 * g + 4, :], ev, ALU.mult, [e1_k, "sS%d" % g], [x_k])
                    S.tt("dve", St[:, 4 * g:4 * g + 4, :], x_t[:], s_t[:, :].rearrange("p (j v) -> p j v", v=128), ALU.add,
                         [x_k], ["sS%d" % g], [s_k])
                    S.act(Sb[:, 4 * g:4 * g + 4, :], St[:, 4 * g:4 * g + 4, :], AF.Copy, ["sS%d" % g], ["sSb%d" % g])
            if need_o:
                for g in range(2):
                    S.act(o_sb[:, g * 512:(g + 1) * 512], G[g]["o_t"][:], AF.Copy, [], [o_sk + "g%d" % g], [G[g]["o_k"]])
                S.dma("pool", dst[t * 128 - NCTX:(t + 1) * 128 - NCTX, :], o_sb[:], rd=[o_sk + "g0", o_sk + "g1"], wr=["OFB"])
        S.emit()


def phase_hg_read(K):
    S, d = K.S, K.d
    with ExitStack() as ph:
        sb, pp, sring, pring = _mk(K, ph)
        ngb = sb("rng", [128, DM], F32)
        S.dma("sp", ngb[:], d["hg_ng_row"][0:1, :].to_broadcast([128, DM]), wr=["rng"])
        oa = sring("roa", 4, [128, DM], F32)
        obt = sring("rob", 4, [128, DM], F32)
        gg = sring("rgg", 4, [128, DM], BF16)
        ssr = sring("rss", 4, [128, 16], F32)
        yb = sring("ryb", 2, [128, DM], BF16)
        yT = sring("ryT", 2, [128, 8, 512], BF16)
        junk = sb("rjunk", [128, 128], BF16)
        ptr = pring("rpt", 2, [128, DM], BF16)
        units = []
        cur = {}
        for t in range(32):
            def mk(t=t):
                st = {}

                def a_():
                    rows = slice(t * 128, (t + 1) * 128)
                    a_t, a_k = oa.nxt()
                    b_t, b_k = obt.nxt()
                    g_t, g_k = gg.nxt()
                    s_t, s_k = ssr.nxt()
                    S.dma("sp", a_t[:], d["OF"][rows, :], rd=["OFB"], wr=[a_k])
                    S.dma("sp", b_t[:], d["OB"][rows, :], rd=["OFB"], wr=[b_k])
                    S.dma("sp", g_t[:], d["GG"][rows, :], rd=["GG"], wr=[g_k])
                    S.tt("pool", a_t[:], a_t[:], b_t[:], ALU.add, [a_k, b_k], [a_k])
                    S.ms("pool", s_t[:], 0.0, [s_k])
                    for h in range(8):
                        S.act(junk[:], a_t[:, h * 128:(h + 1) * 128], AF.Square, [a_k, s_k], [s_k], accum_out=s_t[:, h:h + 1])
                    rstd_ops(S, s_t[:, 0:8], s_t[:, 8:16], s_k, s_k, 128)
                    st["v"] = (a_t, a_k, b_t, b_k, g_t, g_k, s_t, s_k)

                def b_():
                    a_t, a_k, b_t, b_k, g_t, g_k, s_t, s_k = st["v"]
                    for h in range(8):
                        hs = slice(h * 128, (h + 1) * 128)
                        S.stt("dve", b_t[:, hs], a_t[:, hs], s_t[:, 8 + h:9 + h], ngb[:, hs], ALU.mult, ALU.mult,
                              [a_k, s_k, "rng"], [b_k + "h%d" % h])
                    y_t, y_k = yb.nxt()
                    S.tt("pool", y_t[:], b_t[:], g_t[:], ALU.mult, [b_k + "h%d" % h for h in range(8)] + [g_k], [y_k])
                    p_t, p_k = ptr.nxt()
                    for kc in range(8):
                        S.tr(p_t[:, kc * 128:(kc + 1) * 128], y_t[:, kc * 128:(kc + 1) * 128], K.identb[:], [y_k], [p_k])
                    jj = t % 4
                    if jj == 0:
                        cur["o"] = yT.nxt()
                    o_t, o_k = cur["o"]
                    S.act(o_t[:, :, jj * 128:(jj + 1) * 128], p_t[:, :].rearrange("p (c t) -> p c t", t=128), AF.Copy, [], [o_k + "j%d" % jj], [p_k])
                    if jj == 3:
                        S.dma("sp", d["OT"][:, :, NCTX + (t - 3) * 128:NCTX + (t + 1) * 128], o_t[:], rd=[o_k + "j%d" % q for q in range(4)], wr=["OT"])
                return a_, b_
            units.append(mk())
        _run_skewed(units, 2)
        S.emit()


SCRATCH = {
    "modrow": ([2, 2, 2, DM], F32),
    "HT": ([128, 8, NTOK], BF16),
    "OT": ([128, 8, NTOK], BF16),
    "QT": ([8, 96, NTOK], BF16),
    "KT": ([8, 96, NTOK], BF16),
    "V": ([NTOK, 512], BF16),
    "D1k": ([128, 128, 512], BF16),
    "D1_0": ([64, 128, 512], BF16), "D1_1": ([128, 4, 512], BF16), "D1k1": ([128, 8, 512], BF16),
    "D2_0": ([128, 128, 512], BF16), "D2_1": ([128, 8, 512], BF16),
    "KH0": ([2, 128, 65, 4, 512], BF16), "KH1": ([2, 65, 8, 4, 512], BF16),
    "VX": ([NTOK, 3, 512], F32), "Z1": ([NTOK, 512], F32),
    "QS": ([NTOK, DM], BF16), "LF": ([NTOK, 2, DM], F32), "KK": ([NTOK, 2, DM], BF16), "VV": ([NTOK, DM], BF16),
    "GG": ([NLAT, DM], BF16), "OF": ([NLAT, DM], F32), "OB": ([NLAT, DM], F32),
    "XA": ([NTOK, DM], F32),
    "XB": ([NTOK, DM], F32),
}

INPUTS = {
    "x": ([NLAT, DM], F32), "ctx": ([NCTX, DM], F32), "ccol": ([128, 8, 2], F32),
    "mod_w": ([2, DM, 6 * DM], F32), "mod_b": ([2, 6 * DM], F32), "mod_b_col": ([128, 2, 32, 2], F32),
    "ng_col": ([128, 2, 2, 8, 2], F32),
    "ffn_w_up": ([2, DM, 2 * FFH], F32), "ffn_w_down": ([2, FFH, DM], F32),
    "ffn_cw_col": ([2, 128, 3, 44], F32), "ffn_cb_col": ([2, 128, 44], F32),
    "final_g": ([1, DM], F32),
    "ab_w_in": ([DM, 1952], F32), "w_in_krp": ([DM, 32], F32), "mla_w_q_b": ([256, 768], F32),
    "w_q_perm": ([256, 8, 32], F32), "mla_w_kv_b": ([128, 1024], F32), "mla_g_col": ([128, 3], F32),
    "ropeC": ([32, NTOK], F32), "ropeS": ([32, NTOK], F32), "ab_w_out": ([DM, DM], F32),
    "hy_conv_w": ([3, 1536], F32), "hy_conv_b": ([1, 1536], F32), "hy_w1": ([33, 64], F32), "hy_w2": ([64, 64], F32),
    "hy_w3": ([64, 2048], F32), "hy_b_col": ([64, 2], F32), "hy_bias": ([2, 512], F32),
    "hyZ0": ([33, 2 * NLAT], F32), "hyZ1": ([33, 2 * NCTX], F32), "hyT0": ([128, 64], F32), "hyT1": ([128, 4], F32),
    "hy_delta": ([1, 512], F32), "hyFb2": ([128, 128], F32), "hyGb0": ([128, 128], F32), "hyGb1": ([128, 128], F32),
    "hyA0": ([128, 3, 128], F32), "hyA1s": ([64, 3, 128], F32), "hyA1k": ([128, 3, 128], F32),
    "hg_w_in": ([DM, 5120], F32), "hg_lb_logits": ([2, 2, DM], F32), "hg_ng_row": ([1, DM], F32),
    "hg_w_out": ([DM, DM], F32), "hgC": ([128, 2, 5, 128], F32),
}


def build(stage="all", ext=None):
    ext = ext or {}
    nc = bass.Bass("TRN2", target_bir_lowering=False)
    K = Ctx()
    K.nc = nc
    d = K.d = {}
    for name, (shape, dt) in INPUTS.items():
        d[name] = nc.dram_tensor(name, shape, dt, kind="ExternalInput").ap()
    for name, (shape, dt) in SCRATCH.items():
        d[name] = nc.dram_tensor(name, shape, dt, kind=ext.get(name, "Internal")).ap()
    d["out"] = nc.dram_tensor("out", [NLAT, DM], F32, kind="ExternalOutput").ap()
    with ExitStack() as st:
        S = K.S = Sch(nc, st)
        K.modt = st.enter_context(nc.sbuf_tensor("modt", [128, 2, 32, 2], F32))
        K.identb = st.enter_context(nc.sbuf_tensor("identb", [128, 128], BF16))
        identf = st.enter_context(nc.sbuf_tensor("identf", [128, 128], F32))
        MHALF[0] = st.enter_context(nc.sbuf_tensor("mhalf", [128, 16], F32))
        S.ms("pool", MHALF[0][:], -0.5, ["mhalf"])
        S.ms("pool", identf[:], 1.0, ["identf"])
        S.op("pool", lambda e: e.affine_select(out=identf[:], in_=identf[:], pattern=[[-1, 128]], compare_op=ALU.is_equal,
                                               fill=0.0, base=0, channel_multiplier=1), ["identf"], ["identf"])
        S.cp("pool", K.identb[:], identf[:], ["identf"], ["identb"])
        XA, XB = d["XA"], d["XB"]
        phase_mod(K)
        if stage == "all":
            x, ctx = d["x"], d["ctx"]
            phase_norm(K, 0, 0, [(ctx, 0, 1), (x, NCTX, 0)])
            phase_mla_proj(K)
            phase_attn(K)
            hyena_all(K)
            phase_outproj(K, 0, "ab_w_out", x, ctx, XA, True)
            phase_norm(K, 0, 1, [(XA[0:NCTX], 0, 1), (XA[NCTX:NTOK], NCTX, 0)])
            phase_ffn(K, 0, XA[NCTX:NTOK], XA[0:NCTX], XB[NCTX:NTOK], XB[0:NCTX], False)
            phase_norm(K, 1, 0, [(XB[0:NCTX], 0, 1), (XB[NCTX:NTOK], NCTX, 0)])
            phase_hg_proj(K)
            phase_hg_scan(K, 0)
            phase_hg_scan(K, 1)
            phase_hg_read(K)
            phase_outproj(K, 1, "hg_w_out", XB[NCTX:NTOK], XB[0:NCTX], XA, False)
            phase_norm(K, 1, 1, [(XA[NCTX:NTOK], NCTX, 0)])
            phase_ffn(K, 1, XA[NCTX:NTOK], None, d["out"], None, True, do_ctx=False)
            S.final_wait("sp", ["OUT"])
            S.final_wait("pool", ["OUT"])
        if stage == "mla0":
            phase_norm(K, 0, 0, [(d["ctx"], 0, 1), (d["x"], NCTX, 0)])
            phase_mla_proj(K)
            phase_attn(K)
            S.final_wait("sp", ["OT"])
        if stage in ("hyf0", "hyf1"):
            phase_hy_filters(K, int(stage[-1]))
            S.final_wait("sp", ["KH0", "KH1", "D1k"])
        if stage == "hyin":
            phase_norm(K, 0, 0, [(d["ctx"], 0, 1), (d["x"], NCTX, 0)])
            phase_hy_in(K)
            S.final_wait("sp", ["VX", "D1"])
            S.final_wait("pool", ["VX", "D1"])
        if stage == "hg1":
            phase_norm(K, 1, 0, [(XB[0:NCTX], 0, 1), (XB[NCTX:NTOK], NCTX, 0)])
            phase_hg_proj(K)
            phase_hg_scan(K, 0)
            phase_hg_scan(K, 1)
            phase_hg_read(K)
            phase_outproj(K, 1, "hg_w_out", XB[NCTX:NTOK], XB[0:NCTX], XA, False)
            S.final_wait("sp", ["XA"])
        if stage.startswith("hyx:"):
            phase_norm(K, 0, 0, [(d["ctx"], 0, 1), (d["x"], NCTX, 0)])
            for tok in stage[4:].split(","):
                if tok[0] == "f":
                    phase_hy_filters(K, int(tok[1]))
                elif tok == "in":
                    phase_hy_in(K)
                elif tok[0] == "m":
                    phase_hy_mid(K, int(tok[1]), int(tok[2]))
                elif tok[0] == "o":
                    phase_hy_out(K, int(tok[1]), int(tok[2]))
            S.final_wait("sp", ["D2", "D1", "Z1", "OT", "VX"])
            S.final_wait("pool", ["D2", "D1", "Z1", "OT", "VX"])
        if stage in ("hym", "hymo"):
            phase_norm(K, 0, 0, [(d["ctx"], 0, 1), (d["x"], NCTX, 0)])
            phase_hy_filters(K, 1)
            phase_hy_in(K)
            phase_hy_mid(K, 1, 0)
            if stage == "hymo":
                phase_hy_out(K, 1, 0)
            S.final_wait("sp", ["D2", "D1", "Z1"])
            S.final_wait("pool", ["D2", "D1", "Z1"])
        if stage == "hy0":
            phase_norm(K, 0, 0, [(d["ctx"], 0, 1), (d["x"], NCTX, 0)])
            hyena_all(K)
            S.final_wait("sp", ["OT", "Z1", "VX"])
        if stage == "ffn0":
            phase_norm(K, 0, 1, [(XA[0:NCTX], 0, 1), (XA[NCTX:NTOK], NCTX, 0)])
            phase_ffn(K, 0, XA[NCTX:NTOK], XA[0:NCTX], XB[NCTX:NTOK], XB[0:NCTX], False)
            S.final_wait("sp", ["XB"])
        S.emit()
    return nc


_CONST = {}


def rope_tables():
    if "rope" not in _CONST:
        pos = np.arange(NLAT)
        inv = (10000.0 ** (-np.arange(8, dtype=np.float32) / 8)).astype(np.float32)
        ang = np.stack([pos // 64, pos % 64], -1)[..., None].astype(np.float32) * inv
        C = np.ones((32, NTOK), np.float32)
        Sn = np.zeros((32, NTOK), np.float32)
        for dd in range(32):
            ax, half, fr = dd // 16, (dd % 16) // 8, dd % 8
            C[dd, NCTX:] = np.cos(ang[:, ax, fr])
            Sn[dd, NCTX:] = np.sin(ang[:, ax, fr]) * (-1.0 if half == 0 else 1.0)
        _CONST["rope"] = (C, Sn)
    return _CONST["rope"]


def hg_consts():
    if "hg" in _CONST:
        return _CONST["hg"]
    s_ = np.arange(128)[:, None]
    t_ = np.arange(128)[None, :]
    same = (s_ // 64) == (t_ // 64)
    C = np.zeros((128, 2, 5, 128), np.float32)
    for dd in range(2):
        tri = (same & ((s_ <= t_) if dd == 0 else (s_ >= t_))).astype(np.float32)
        mid = (np.arange(128) // 64) * 64 + 32
        trim = tri - tri[:, mid]
        rev = (same & ((s_ > t_) if dd == 0 else (s_ < t_))).astype(np.float32)
        C[:, dd, 0] = tri
        C[:, dd, 1] = trim
        C[:, dd, 2] = rev
        C[:, dd, 3] = tri * 3.0e38
        C[:, dd, 4] = -tri * 3.0e38
    _CONST["hg"] = C
    return C


def hy_consts():
    if "hy" in _CONST:
        return _CONST["hy"]
    c = {}
    f32 = np.float32
    deltas = np.abs(np.linspace(math.log(1e-2) / 1.5, math.log(1e-2) / 0.3, 512, dtype=f32))
    c["hy_delta"] = deltas.reshape(1, 512).astype(f32)
    bb = np.arange(64)[:, None].astype(np.float64)
    Fb = np.zeros((64, 128))
    fr = np.arange(65)[None, :]
    Fb[:, 0:65] = np.cos(2 * np.pi * fr * bb / 128)
    fr = np.arange(1, 64)[None, :]
    Fb[:, 65:128] = -np.sin(2 * np.pi * fr * bb / 128)
    c["hyFb2"] = np.concatenate([Fb, Fb], 0).astype(f32)
    Bq = np.arange(128)[None, :].astype(np.float64)
    Gb = np.zeros((128, 128))
    Gb[0] = 1.0
    fr = np.arange(1, 64)[:, None]
    Gb[1:64] = 2 * np.cos(2 * np.pi * fr * Bq / 128)
    Gb[64] = np.cos(np.pi * Bq[0])
    Gb[65:128] = -2 * np.sin(2 * np.pi * fr * Bq / 128)
    for seq, (L, NF) in enumerate(((NLAT, 128), (NCTX, 8))):
        c["hyGb%d" % seq] = (Gb / (128.0 * NF)).astype(f32)
        a = np.arange(NF)[:, None].astype(np.float64)
        th = 2 * np.pi * a * a.T / NF
        A3 = np.stack([np.cos(th), np.sin(th), -np.sin(th)], axis=1)
        if seq == 0:
            c["hyA0"] = A3.astype(f32)
        else:
            As = np.zeros((64, 3, 128)); Ak = np.zeros((128, 3, 128))
            for g in range(16):
                As[g * 4:(g + 1) * 4, :, g * 8:(g + 1) * 8] = A3[0:4]
                Ak[g * 8:(g + 1) * 8, :, g * 8:(g + 1) * 8] = A3
            c["hyA1s"] = As.astype(f32)
            c["hyA1k"] = Ak.astype(f32)
        p = np.arange(2 * L)
        lag = np.where(p <= L, p, 2 * L - p)
        lag = np.minimum(lag, L - 1)
        t = np.linspace(0.0, 1.0, L, dtype=f32)[lag]
        w = (2.0 * math.pi * lag.astype(f32) / L).astype(f32)
        fq = np.linspace(1e-4, 15, 16, dtype=f32)
        Z = np.concatenate([t[:, None], np.cos(fq[None, :] * w[:, None]), -np.sin(fq[None, :] * w[:, None])], axis=1).astype(f32)
        c["hyZ%d" % seq] = np.ascontiguousarray(Z.T)
        c["hyT%d" % seq] = np.ascontiguousarray((-t).reshape(2 * L // 128, 128).T).astype(f32)
    _CONST["hy"] = c
    return c


def host_inputs(inp, b):
    f = lambda a: np.ascontiguousarray(a, dtype=np.float32)
    m = {}
    m["x"] = f(inp["x"][b])
    m["ctx"] = f(inp["ctx"][b])
    cc = np.stack([inp["c"][b], inp["c_ctx"]], axis=-1)
    m["ccol"] = f(cc.reshape(8, 128, 2).transpose(1, 0, 2))
    m["mod_w"] = f(inp["mod_w"])
    m["mod_b"] = f(inp["mod_b"])
    mb = inp["mod_b"].reshape(2, 6, 8, 128)[:, [0, 1, 3, 4]]
    mb = mb.transpose(3, 0, 1, 2).reshape(128, 2, 32)
    m["mod_b_col"] = f(np.repeat(mb[..., None], 2, axis=-1))
    ng = np.stack([inp["norm1_g"], inp["norm2_g"]], axis=1)
    ng = ng.reshape(2, 2, 8, 128).transpose(3, 0, 1, 2)
    m["ng_col"] = f(np.repeat(ng[..., None], 2, axis=-1))
    m["ffn_w_up"] = f(inp["ffn_w_up"])
    m["ffn_w_down"] = f(inp["ffn_w_down"])
    m["ffn_cw_col"] = f(inp["ffn_conv_w"].reshape(2, 3, 44, 128).transpose(0, 3, 1, 2))
    m["ffn_cb_col"] = f(inp["ffn_conv_b"].reshape(2, 44, 128).transpose(0, 2, 1))
    m["final_g"] = f(inp["final_norm_g"].reshape(1, DM))
    m["ab_w_in"] = f(inp["ab_w_in"][0])
    perm = np.array([dd + 8 if (dd % 16) < 8 else dd - 8 for dd in range(32)])
    m["w_in_krp"] = f(inp["ab_w_in"][0][:, 384 + perm])
    wq = inp["mla_w_q_b"][0]
    m["mla_w_q_b"] = f(wq)
    m["w_q_perm"] = f(wq.reshape(256, 8, 96)[:, :, 64 + perm])
    m["mla_w_kv_b"] = f(inp["mla_w_kv_b"][0])
    m["mla_g_col"] = f(np.concatenate([inp["mla_q_norm_g"][0].reshape(2, 128).T, inp["mla_kv_norm_g"][0].reshape(1, 128).T], axis=1))
    rc, rs = rope_tables()
    m["ropeC"], m["ropeS"] = rc, rs
    m["ab_w_out"] = f(inp["ab_w_out"][0])
    m["hy_conv_w"] = f(inp["hy_conv_w"][0])
    m["hy_conv_b"] = f(inp["hy_conv_b"][0].reshape(1, 1536))
    m["hy_w1"] = f(inp["hy_w1"][0]); m["hy_w2"] = f(inp["hy_w2"][0]); m["hy_w3"] = f(inp["hy_w3"][0])
    m["hy_b_col"] = f(np.stack([inp["hy_b1"][0], inp["hy_b2"][0]], axis=1))
    m["hy_bias"] = f(inp["hy_bias"][0])
    m.update(hy_consts())
    m["hg_w_in"] = f(inp["hg_w_in"][0])
    m["hg_lb_logits"] = f(inp["hg_lb_logits"])
    m["hg_ng_row"] = f(np.tile(inp["hg_norm_g"][0], 8).reshape(1, DM))
    m["hg_w_out"] = f(inp["hg_w_out"][0])
    m["hgC"] = hg_consts()
    return m


def kernel(**inputs):
    nc = build("all")
    in_maps = [host_inputs(inputs, b) for b in range(8)]
    res = run_bass_kernel_spmd(nc, in_maps, core_ids=list(range(8)))
    return np.stack([r["out"] for r in res.results], axis=0).astype(np.float32)
```

```python
import math
from contextlib import ExitStack

import numpy as np
import concourse.bass as bass
import concourse.mybir as mybir
from concourse.bass_utils import run_bass_kernel_spmd

F32 = mybir.dt.float32
BF16 = mybir.dt.bfloat16
AF = mybir.ActivationFunctionType
ALU = mybir.AluOpType

NCTX, NLAT, NTOK, DM = 256, 4096, 4352, 1024
EPS = 1e-6
FFH = 2816
ENG = ("pe", "act", "dve", "pool", "sp")
NSEM = {"sp": 16, "act": 8, "pool": 12}


class Sch:
    def __init__(self, nc, stack):
        self.nc = nc
        self.ops = {e: [] for e in ENG}
        self.cnt = {e: 0 for e in ENG}
        self.seen = {e: {} for e in ENG}
        self.res = {}
        self.esem = {e: stack.enter_context(nc.semaphore("s_" + e)) for e in ENG}
        self.dsem, self.dval, self.drr = {}, {}, {}
        for q, n in NSEM.items():
            self.dsem[q] = [stack.enter_context(nc.semaphore("d_%s%d" % (q, i))) for i in range(n)]
            self.dval[q] = [0] * n
            self.drr[q] = 0

    def _deps(self, rd, wr, prd=(), eng=None):
        deps = []
        for r in rd:
            st = self.res.get(r)
            if st and st["w"] is not None:
                deps.append(st["w"] + ("raw",))
        for r in prd:
            st = self.res.get(r)
            if st:
                if st["w"] is not None:
                    deps.append(st["w"] + ("raw",))
                for k, dp in st["r"].items():
                    if k != eng:
                        deps.append(dp + ("war",))
        for w in wr:
            st = self.res.get(w)
            if st:
                if st["w"] is not None:
                    deps.append(st["w"] + ("waw",))
                for dp in st["r"].values():
                    deps.append(dp + ("war",))
        return deps

    def _mark(self, rd, wr, dep, prd=()):
        for r in list(rd) + list(prd):
            st = self.res.setdefault(r, {"w": None, "r": {}})
            st["r"][dep[0]] = dep
        for w in wr:
            self.res[w] = {"w": dep, "r": {}}

    def _waits(self, eng, deps, is_dma):
        out = {}
        for key, sem, val, kind in deps:
            if (not is_dma) and key == eng and (eng == "pe" or kind != "raw"):
                continue
            if self.seen[eng].get(key, 0) >= val:
                continue
            if key not in out or out[key][1] < val:
                out[key] = (sem, val)
        for key, (sem, val) in out.items():
            self.seen[eng][key] = val
        return list(out.values())

    def op(self, eng, fn, rd=(), wr=(), prd=()):
        waits = self._waits(eng, self._deps(rd, wr, prd, eng), False)
        self.cnt[eng] += 1
        self.ops[eng].append((waits, fn, self.esem[eng], 1))
        self._mark(rd, wr, (eng, self.esem[eng], self.cnt[eng]), prd)

    def dma(self, q, out, in_, rd=(), wr=(), **kw):
        deps = self._deps(rd, wr)
        i = self.drr[q]
        self.drr[q] = (i + 1) % len(self.dsem[q])
        sem = self.dsem[q][i]
        key = ("d", q, i)
        if self.dval[q][i] > 0:
            deps.append((key, sem, self.dval[q][i], "waw"))
        waits = self._waits(q, deps, True)
        self.dval[q][i] += 16
        self.ops[q].append((waits, (lambda e, o=out, s=in_, k=kw: e.dma_start(out=o, in_=s, **k)), sem, 16))
        self._mark(rd, wr, (key, sem, self.dval[q][i]))

    def final_wait(self, eng, keys):
        waits = self._waits(eng, self._deps(keys, ()), True)
        self.ops[eng].append((waits, None, None, 0))

    def emit(self):
        nc = self.nc
        with nc.Block() as block:
            def run(name):
                lst = self.ops[name]

                def body(e):
                    for waits, fn, sem, inc in lst:
                        for (s, v) in waits:
                            e.wait_ge(s, v)
                        if fn is not None:
                            fn(e).then_inc(sem, inc)
                return body
            block.tensor(run("pe"))
            block.scalar(run("act"))
            block.vector(run("dve"))
            block.gpsimd(run("pool"))
            block.sync(run("sp"))
        self.ops = {e: [] for e in ENG}

    def mm(self, out, lhsT, rhs, start, stop, rd, wr, **kw):
        self.op("pe", lambda e: e.matmul(out, lhsT=lhsT, rhs=rhs, start=start, stop=stop, **kw), rd, wr)

    def tr(self, out, in_, ident, rd, wr):
        self.op("pe", lambda e: e.transpose(out, in_, ident), rd, wr)

    def act(self, out, in_, func, rd, wr, prd=(), **kw):
        self.op("act", lambda e: e.activation(out=out, in_=in_, func=func, **kw), rd, wr, prd)

    def tt(self, eng, out, in0, in1, op, rd, wr, prd=()):
        self.op(eng, lambda e: e.tensor_tensor(out=out, in0=in0, in1=in1, op=op), rd, wr, prd)

    def ts(self, eng, out, in0, s1, s2, op0, op1, rd, wr, prd=()):
        if s2 is None:
            self.op(eng, lambda e: e.tensor_scalar(out=out, in0=in0, scalar1=s1, scalar2=None, op0=op0), rd, wr, prd)
        else:
            self.op(eng, lambda e: e.tensor_scalar(out=out, in0=in0, scalar1=s1, scalar2=s2, op0=op0, op1=op1), rd, wr, prd)

    def stt(self, eng, out, in0, scalar, in1, op0, op1, rd, wr, prd=()):
        self.op(eng, lambda e: e.scalar_tensor_tensor(out=out, in0=in0, scalar=scalar, in1=in1, op0=op0, op1=op1), rd, wr, prd)

    def cp(self, eng, out, in_, rd, wr, prd=()):
        self.op(eng, lambda e: e.tensor_copy(out=out, in_=in_), rd, wr, prd)

    def ms(self, eng, out, val, wr):
        self.op(eng, lambda e: e.memset(out, val), (), wr)


class Ring:
    def __init__(self, tiles, name):
        self.t, self.name, self.i = tiles, name, -1

    def nxt(self):
        self.i += 1
        j = self.i % len(self.t)
        return self.t[j], "%s%d" % (self.name, j)


class Ctx:
    pass


def _run_skewed(units, skew=2):
    n = len(units)
    for i in range(n + skew):
        if i < n:
            units[i][0]()
        if i >= skew:
            units[i - skew][1]()


def _mk(K, ph):
    nc = K.nc
    K.uid = getattr(K, "uid", 0) + 1
    u = "_%d" % K.uid

    def sb(name, shape, dt):
        return ph.enter_context(nc.sbuf_tensor(name + u, shape, dt))

    def pp(name, shape, dt):
        return ph.enter_context(nc.psum_tensor(name + u, shape, dt))

    def sring(name, n, shape, dt):
        return Ring([sb("%s%d" % (name, i), shape, dt) for i in range(n)], name)

    def pring(name, n, shape, dt):
        return Ring([pp("%s%d" % (name, i), shape, dt) for i in range(n)], name)
    return sb, pp, sring, pring


def phase_mod(K):
    S, d = K.S, K.d
    with ExitStack() as ph:
        sb, pp, sring, pring = _mk(K, ph)
        csil = sb("csil", [128, 8, 2], F32)
        S.dma("sp", csil[:], d["ccol"][:, :, :], wr=["csil"])
        S.act(csil[:], csil[:], AF.Silu, ["csil"], ["csil"])
        pcol_t = pp("pcol", [128, 512], F32)
        pcol = pcol_t[:, 0:128].rearrange("p (l g w) -> p l g w", l=2, g=32)
        prow = pring("prow", 2, [128, 512], F32)
        wch = sring("wch", 4, [128, 8, 512], F32)
        brow = sring("brow", 2, [1, 512], F32)
        mbc = sb("mbc", [128, 2, 32, 2], F32)
        ngc = sb("ngc", [128, 2, 2, 8, 2], F32)
        S.dma("sp", mbc[:], d["mod_b_col"][:, :, :, :], wr=["mbc"])
        S.dma("sp", ngc[:], d["ng_col"][:, :, :, :, :], wr=["ngc"])
        vmap = {0: 0, 1: 1, 3: 2, 4: 3}
        for l in range(2):
            mw = d["mod_w"][l].rearrange("(kc p) n -> p kc n", p=128)
            for ci in range(12):
                vec, half = ci // 2, ci % 2
                w_t, w_k = wch.nxt()
                for kq in range(2):
                    S.dma("sp" if kq == 0 else "act", w_t[:, kq * 4:(kq + 1) * 4, :], mw[:, kq * 4:(kq + 1) * 4, ci * 512:(ci + 1) * 512], wr=[w_k + "q%d" % kq])
                w_k2 = [w_k + "q0", w_k + "q1"]
                if vec in (2, 5):
                    gi = 0 if vec == 2 else 1
                    for w in range(2):
                        p_t, p_k = prow.nxt()
                        b_t, b_k = brow.nxt()
                        for kc in range(8):
                            S.mm(p_t[0:1, :], csil[:, kc, w:w + 1], w_t[:, kc, :], kc == 0, kc == 7, ["csil"] + w_k2, [p_k])
                        S.dma("sp", b_t[:], d["mod_b"][l:l + 1, ci * 512:(ci + 1) * 512], wr=[b_k])
                        S.tt("dve", b_t[:], p_t[0:1, :], b_t[:], ALU.add, [b_k], [b_k], [p_k])
                        S.dma("sp", d["modrow"][l, gi, w:w + 1, half * 512:(half + 1) * 512], b_t[:], rd=[b_k], wr=["modrow"])
                else:
                    vi = vmap[vec]
                    for jj in range(4):
                        g = vi * 8 + half * 4 + jj
                        for kc in range(8):
                            S.mm(pcol[:, l, g, :], w_t[:, kc, jj * 128:(jj + 1) * 128], csil[:, kc, :], kc == 0, kc == 7,
                                 ["csil"] + w_k2, ["pcol"])
        modt = K.modt
        S.tt("dve", modt[:], pcol, mbc[:], ALU.add, ["mbc"], ["modt"], ["pcol"])
        for l in range(2):
            for n in range(2):
                sl = modt[:, l, (2 * n + 1) * 8:(2 * n + 2) * 8, :]
                S.ts("dve", sl, sl, 1.0, None, ALU.add, None, ["modt"], ["modt"])
                S.tt("dve", sl, sl, ngc[:, l, n, :, :], ALU.mult, ["ngc", "modt"], ["modt"])
        S.emit()


def Gcol(K, l, n, kc, w):
    return K.modt[:, l, (2 * n + 1) * 8 + kc, w:w + 1]


def Shcol(K, l, n, kc, w):
    return K.modt[:, l, (2 * n) * 8 + kc, w:w + 1]


MHALF = [None]


def rstd_ops(S, ss, rs, rd_k, wr_k, n_feat):
    S.ts("dve", rs, ss, 1.0 / n_feat, EPS, ALU.mult, ALU.add, [rd_k], [wr_k])
    S.tt("pool", rs, rs, MHALF[0][:, 0:rs.shape[1]], ALU.pow, [wr_k], [wr_k])


def phase_norm(K, l, n, srcs):
    S, d = K.S, K.d
    with ExitStack() as ph:
        sb, pp, sring, pring = _mk(K, ph)
        xt = sring("nx", 4, [128, DM], F32)
        xn = sring("nxn", 2, [128, DM], BF16)
        hT = sring("nh", 2, [128, 8, 512], BF16)
        ssr = sring("nss", 4, [128, 2], F32)
        pt = pring("npt", 2, [128, DM], BF16)
        junk = sb("njunk", [128, DM], BF16)
        units = []
        cur = {}
        for (src, off, w) in srcs:
            nt = src.shape[0] // 128
            for t in range(nt):
                def mk(src=src, off=off, w=w, nt=nt, t=t):
                    st = {}

                    def a_():
                        x_t, x_k = xt.nxt()
                        s_t, s_k = ssr.nxt()
                        S.dma("sp", x_t[:], src[t * 128:(t + 1) * 128, :], rd=["XA", "XB"], wr=[x_k])
                        S.ms("pool", s_t[:], 0.0, [s_k])
                        S.act(junk[:], x_t[:], AF.Square, [x_k], [s_k], accum_out=s_t[:, 0:1])
                        rstd_ops(S, s_t[:, 0:1], s_t[:, 1:2], s_k, s_k, DM)
                        st["v"] = (x_t, x_k, s_t, s_k)

                    def b_():
                        x_t, x_k, s_t, s_k = st["v"]
                        n_t, n_k = xn.nxt()
                        p_t, p_k = pt.nxt()
                        j = t % 4
                        if j == 0:
                            cur["h"] = hT.nxt()
                        h_t, h_k = cur["h"]
                        S.act(n_t[:], x_t[:], AF.Copy, [x_k, s_k], [n_k], scale=s_t[:, 1:2])
                        for kc in range(8):
                            S.tr(p_t[:, kc * 128:(kc + 1) * 128], n_t[:, kc * 128:(kc + 1) * 128], K.identb[:], [n_k], [p_k])
                        for kc in range(8):
                            S.ts("dve", h_t[:, kc, j * 128:(j + 1) * 128], p_t[:, kc * 128:(kc + 1) * 128], Gcol(K, l, n, kc, w),
                                 Shcol(K, l, n, kc, w), ALU.mult, ALU.add, ["modt"], [h_k + "j%d" % j], [p_k])
                        if j == 3 or t == nt - 1:
                            c0 = off + (t - j) * 128
                            S.dma("pool", d["HT"][:, :, c0:c0 + (j + 1) * 128], h_t[:, :, 0:(j + 1) * 128],
                                  rd=[h_k + "j%d" % jj for jj in range(j + 1)], wr=["HT"])
                    return a_, b_
                units.append(mk())
        _run_skewed(units, 2)
        S.emit()


def phase_ffn(K, l, src_lat, src_ctx, dst_lat, dst_ctx, final, do_ctx=True):
    S, d = K.S, K.d
    with ExitStack() as ph:
        sb, pp, sring, pring = _mk(K, ph)
        wup = sb("wup", [128, 8, 2 * FFH], BF16)
        wdn = sb("wdn", [128, 22, DM], BF16)
        wu = d["ffn_w_up"][l].rearrange("(kc p) n -> p kc n", p=128)
        wd = d["ffn_w_down"][l].rearrange("(j p) n -> p j n", p=128)
        for c in range(11):
            S.dma("pool", wup[:, :, c * 512:(c + 1) * 512], wu[:, :, c * 512:(c + 1) * 512], wr=["wup"])
        for c in range(11):
            S.dma("pool", wdn[:, 2 * c:2 * c + 2, :], wd[:, 2 * c:2 * c + 2, :], wr=["wdn"])
        cw = sb("fcw", [128, 3, 44], F32)
        cb = sb("fcb", [128, 44], F32)
        S.dma("sp", cw[:], d["ffn_cw_col"][l], wr=["fcw"])
        S.dma("sp", cb[:], d["ffn_cb_col"][l], wr=["fcw"])
        g2bc = sb("g2bc", [128, 2, DM], F32)
        for w in range(2):
            S.dma("sp", g2bc[:, w, :], d["modrow"][l, 1, w:w + 1, :].to_broadcast([128, DM]), rd=["modrow"], wr=["g2bc"])
        if final:
            fng = sb("fng", [128, DM], F32)
            S.dma("sp", fng[:], d["final_g"][0:1, :].to_broadcast([128, DM]), wr=["fng"])
        hb = sring("fhb", 2, [128, 8, 258], BF16)
        gT = sb("fgT", [128, 22, 256], BF16)
        av = sring("fav", 4, [128, 256], F32)
        sg = sring("fsg", 2, [128, 256], F32)
        xr = sring("fx", 2, [128, DM], F32)
        tmp = sring("ftmp", 2, [128, DM], F32)
        ssr = sring("fss", 2, [128, 2], F32)
        junk = sb("fjunk", [128, DM], BF16)
        pup = pring("fpu", 4, [128, 512], F32)
        pdn = [pp("fpd%d" % i, [128, 512], F32) for i in range(4)]
        blocks = [(0, 1, src_ctx, dst_ctx, 0, True, True)] if do_ctx else []
        for i in range(16):
            blocks.append((256 + 256 * i, 0, src_lat, dst_lat, 256 * i, i == 0, i == 15))
        for (t0, w, src, dst, r0, first, last) in blocks:
            h_t, h_k = hb.nxt()
            lo = 0 if not first else 1
            hi = 258 if not last else 257
            if first:
                S.ms("pool", h_t[:, :, 0:1], 0.0, [h_k])
            if last:
                S.ms("pool", h_t[:, :, 257:258], 0.0, [h_k])
            S.dma("sp", h_t[:, :, lo:hi], d["HT"][:, :, t0 - 1 + lo:t0 - 1 + hi], rd=["HT"], wr=[h_k])

            def down(j):
                for tt_ in range(2):
                    for nh in range(2):
                        S.mm(pdn[tt_ * 2 + nh][:], gT[:, j, tt_ * 128:(tt_ + 1) * 128], wdn[:, j, nh * 512:(nh + 1) * 512],
                             j == 0, j == 21, ["fgT%d" % j, "wdn"], ["fpd%d" % (tt_ * 2 + nh)])

            for j in range(23):
                if j < 22:
                    pts = []
                    for part in range(2):
                        ch = part * 22 + j
                        p_t, p_k = pup.nxt()
                        a_t, a_k = av.nxt()
                        for kc in range(8):
                            S.mm(p_t[:, 0:258], wup[:, kc, ch * 128:(ch + 1) * 128], h_t[:, kc, :], kc == 0, kc == 7,
                                 ["wup", h_k], [p_k])
                        pts.append((p_t, p_k, a_t, a_k, ch))
                    for (p_t, p_k, a_t, a_k, ch) in pts:
                        S.act(a_t[:], p_t[:, 1:257], AF.Identity, ["fcw"], [a_k], [p_k], scale=cw[:, 1, ch:ch + 1], bias=cb[:, ch:ch + 1])
                    for tap, c0 in ((0, 0), (2, 2)):
                        for (p_t, p_k, a_t, a_k, ch) in pts:
                            S.stt("dve", a_t[:], p_t[:, c0:c0 + 256], cw[:, tap, ch:ch + 1], a_t[:], ALU.mult, ALU.add,
                                  ["fcw", a_k], [a_k], [p_k])
                    s_t, s_k = sg.nxt()
                    S.act(s_t[:], pts[0][2][:], AF.Silu, [pts[0][3]], [s_k])
                    S.tt("pool", gT[:, j, :], s_t[:], pts[1][2][:], ALU.mult, [s_k, pts[1][3]], ["fgT%d" % j])
                if j > 0:
                    down(j - 1)
            for tt_ in range(2):
                x_t, x_k = xr.nxt()
                m_t, m_k = tmp.nxt()
                rows = slice(r0 + tt_ * 128, r0 + (tt_ + 1) * 128)
                S.dma("sp", x_t[:], src[rows, :], rd=["XA"], wr=[x_k])
                for nh in range(2):
                    cs = slice(nh * 512, (nh + 1) * 512)
                    S.tt("dve", m_t[:, cs], pdn[tt_ * 2 + nh][:], g2bc[:, w, cs], ALU.mult, ["g2bc"], [m_k], ["fpd%d" % (tt_ * 2 + nh)])
                S.tt("pool", x_t[:], x_t[:], m_t[:], ALU.add, [m_k, x_k], [x_k])
                if final:
                    if w == 1:
                        continue
                    s_t, s_k = ssr.nxt()
                    S.ms("pool", s_t[:], 0.0, [s_k])
                    S.act(junk[:], x_t[:], AF.Square, [x_k], [s_k], accum_out=s_t[:, 0:1])
                    rstd_ops(S, s_t[:, 0:1], s_t[:, 1:2], s_k, s_k, DM)
                    S.stt("dve", m_t[:], x_t[:], s_t[:, 1:2], fng[:], ALU.mult, ALU.mult, [x_k, s_k, "fng"], [m_k])
                    S.dma("pool", dst[rows, :], m_t[:], rd=[m_k], wr=["OUT"])
                else:
                    S.dma("pool", dst[rows, :], x_t[:], rd=[x_k], wr=["XB"])
        S.emit()


def phase_outproj(K, l, wname, src_lat, src_ctx, dst, do_ctx):
    S, d = K.S, K.d
    with ExitStack() as ph:
        sb, pp, sring, pring = _mk(K, ph)
        wo = sb("owo", [128, 8, DM], BF16)
        wv = d[wname].rearrange("(kc p) n -> p kc n", p=128)
        for c in range(4):
            S.dma("pool", wo[:, 2 * c:2 * c + 2, :], wv[:, 2 * c:2 * c + 2, :], wr=["owo"])
        g1bc = sb("og1", [128, 2, DM], F32)
        for w in range(2):
            S.dma("sp", g1bc[:, w, :], d["modrow"][l, 0, w:w + 1, :].to_broadcast([128, DM]), rd=["modrow"], wr=["og1"])
        ot = sring("oot", 2, [128, 8, 512], BF16)
        xr = sring("ox", 3, [128, DM], F32)
        tmp = sring("otmp", 2, [128, DM], F32)
        po = pring("opo", 4, [128, 512], F32)
        tiles = []
        if do_ctx:
            tiles += [(t * 128, 1, src_ctx, t * 128) for t in range(2)]
        tiles += [(NCTX + t * 128, 0, src_lat, t * 128) for t in range(32)]
        for ti, (c0, w, src, r0) in enumerate(tiles):
            jj = ((c0 - NCTX) % 512) // 128 if c0 >= NCTX else (c0 // 128)
            if jj == 0:
                o_t, o_k = ot.nxt()
                wd_ = 256 if c0 < NCTX else 512
                S.dma("sp", o_t[:, :, 0:wd_], d["OT"][:, :, c0:c0 + wd_], rd=["OT"], wr=[o_k])
            x_t, x_k = xr.nxt()
            m_t, m_k = tmp.nxt()
            S.dma("sp", x_t[:], src[r0:r0 + 128, :], rd=["XB"], wr=[x_k])
            for nh in range(2):
                p_t, p_k = po.nxt()
                cs = slice(nh * 512, (nh + 1) * 512)
                for kc in range(8):
                    S.mm(p_t[:], o_t[:, kc, jj * 128:(jj + 1) * 128], wo[:, kc, cs], kc == 0, kc == 7, [o_k, "owo"], [p_k])
                S.tt("dve", m_t[:, cs], p_t[:], g1bc[:, w, cs], ALU.mult, ["og1"], [m_k], [p_k])
            S.tt("pool", x_t[:], x_t[:], m_t[:], ALU.add, [m_k, x_k], [x_k])
            S.dma("pool", dst[c0:c0 + 128, :], x_t[:], rd=[x_k], wr=["XA"])
        S.emit()


def chunks512():
    return [(c * 512, 512) for c in range(8)] + [(4096, 256)]


def phase_mla_proj(K):
    S, d = K.S, K.d
    with ExitStack() as ph:
        sb, pp, sring, pring = _mk(K, ph)
        HTs = sb("mHT", [128, 8, NTOK], BF16)
        for kc in range(8):
            S.dma("sp", HTs[:, kc, :], d["HT"][:, kc, :], rd=["HT"], wr=["mHT"])
        win = d["ab_w_in"].rearrange("(kc p) n -> p kc n", p=128)
        wlo = sb("mwlo", [128, 8, 384], BF16)
        S.dma("pool", wlo[:], win[:, :, 0:384], wr=["mw"])
        wkr = sb("mwkr", [128, 2, 8, 96], BF16)
        S.ms("pool", wkr[:], 0.0, ["mw"])
        S.dma("pool", wkr[:, 0, :, 64:96], win[:, :, 384:416], wr=["mw"])
        S.dma("pool", wkr[:, 1, :, 64:96], d["w_in_krp"].rearrange("(kc p) n -> p kc n", p=128), wr=["mw"])
        wq = sb("mwq", [128, 2, 768], BF16)
        S.dma("pool", wq[:], d["mla_w_q_b"].rearrange("(kc p) n -> p kc n", p=128), wr=["mw"])
        wqp = sb("mwqp", [128, 2, 8, 96], BF16)
        S.ms("pool", wqp[:], 0.0, ["mw"])
        for kc in range(2):
            S.dma("pool", wqp[:, kc, :, 64:96], d["w_q_perm"][kc * 128:(kc + 1) * 128, :, :], wr=["mw"])
        wkv = sb("mwkv", [128, 1024], BF16)
        S.dma("pool", wkv[:], d["mla_w_kv_b"][:, :], wr=["mw"])
        gcol = sb("mgc", [128, 3], F32)
        S.dma("sp", gcol[:], d["mla_g_col"][:, :], wr=["mgc"])
        rC = sb("mrC", [96, NTOK], F32)
        rS = sb("mrS", [96, NTOK], F32)
        S.dma("sp", rC[64:96, :], d["ropeC"][:, :], wr=["mrC"])
        S.dma("sp", rS[64:96, :], d["ropeS"][:, :], wr=["mrC"])
        qnT = sb("mqnT", [128, 2, NTOK], BF16)
        kvnT = sb("mkvnT", [128, NTOK], BF16)
        KRT = sb("mKRT", [96, NTOK], BF16)
        pl = pring("mpl", 2, [128, 512], F32)
        ptr = pring("mpt", 2, [128, 1024], BF16)
        nrm = sring("mnrm", 2, [128, 384], BF16)
        ssr = sring("mss", 3, [128, 4], F32)
        junk = sb("mjunk", [128, 384], BF16)
        units = []
        for t in range(34):
            def mk(t=t):
                st = {}
                ts_ = slice(t * 128, (t + 1) * 128)

                def a_():
                    p_t, p_k = pl.nxt()
                    s_t, s_k = ssr.nxt()
                    for kc in range(8):
                        S.mm(p_t[:, 0:384], HTs[:, kc, ts_], wlo[:, kc, :], kc == 0, kc == 7, ["mHT", "mw"], [p_k])
                    S.ms("pool", s_t[:], 0.0, [s_k])
                    S.act(junk[:, 0:256], p_t[:, 0:256], AF.Square, [], [s_k], [p_k], accum_out=s_t[:, 0:1])
                    S.act(junk[:, 256:384], p_t[:, 256:384], AF.Square, [s_k], [s_k], [p_k], accum_out=s_t[:, 1:2])
                    rstd_ops(S, s_t[:, 0:1], s_t[:, 2:3], s_k, s_k, 256)
                    rstd_ops(S, s_t[:, 1:2], s_t[:, 3:4], s_k, s_k, 128)
                    st["v"] = (p_t, p_k, s_t, s_k)

                def b_():
                    p_t, p_k, s_t, s_k = st["v"]
                    n_t, n_k = nrm.nxt()
                    q_t, q_k = ptr.nxt()
                    S.act(n_t[:, 0:256], p_t[:, 0:256], AF.Copy, [s_k], [n_k], [p_k], scale=s_t[:, 2:3])
                    S.act(n_t[:, 256:384], p_t[:, 256:384], AF.Copy, [s_k], [n_k], [p_k], scale=s_t[:, 3:4])
                    for c in range(3):
                        S.tr(q_t[:, c * 128:(c + 1) * 128], n_t[:, c * 128:(c + 1) * 128], K.identb[:], [n_k], [q_k])
                    for c in range(2):
                        S.ts("dve", qnT[:, c, ts_], q_t[:, c * 128:(c + 1) * 128], gcol[:, c:c + 1], None, ALU.mult, None, ["mgc"], ["mqnT"], [q_k])
                    S.ts("dve", kvnT[:, ts_], q_t[:, 256:384], gcol[:, 2:3], None, ALU.mult, None, ["mgc"], ["mkvnT"], [q_k])
                return a_, b_
            units.append(mk())
        _run_skewed(units, 1)
        pk = pring("mpk", 4, [128, 512], F32)
        t12 = sring("mt12", 4, [96, 512], F32)
        for (c0, W) in chunks512():
            cs = slice(c0, c0 + W)
            pa, pa_k = pk.nxt()
            pb, pb_k = pk.nxt()
            for kc in range(8):
                S.mm(pa[0:96, 0:W], wkr[:, 0, kc, :], HTs[:, kc, cs], kc == 0, kc == 7, ["mHT", "mw"], [pa_k])
            for kc in range(8):
                S.mm(pb[0:96, 0:W], wkr[:, 1, kc, :], HTs[:, kc, cs], kc == 0, kc == 7, ["mHT", "mw"], [pb_k])
            ta, ta_k = t12.nxt()
            tb, tb_k = t12.nxt()
            S.tt("dve", ta[64:96, 0:W], pa[64:96, 0:W], rC[64:96, cs], ALU.mult, ["mrC"], [ta_k], [pa_k])
            S.tt("dve", tb[64:96, 0:W], pb[64:96, 0:W], rS[64:96, cs], ALU.mult, ["mrC"], [tb_k], [pb_k])
            S.tt("pool", KRT[64:96, cs], ta[64:96, 0:W], tb[64:96, 0:W], ALU.add, [ta_k, tb_k], ["mKRT"])
        vt = sring("mvt", 2, [128, 512], BF16)
        wv = sb("mwv", [128, 512], BF16)
        S.dma("pool", wv[:, :].rearrange("p (h x) -> p h x", x=64),
              d["mla_w_kv_b"].rearrange("k (h x) -> k h x", x=128)[:, :, 64:128], wr=["mw"])
        for t in range(34):
            ts_ = slice(t * 128, (t + 1) * 128)
            p_t, p_k = pk.nxt()
            v_t, v_k = vt.nxt()
            S.mm(p_t[:], kvnT[:, ts_], wv[:], True, True, ["mkvnT", "mw"], [p_k])
            S.act(v_t[:], p_t[:], AF.Copy, [], [v_k], [p_k])
            S.dma("act", d["V"][ts_, :], v_t[:], rd=[v_k], wr=["V"])
        qt = sring("mqt", 2, [96, 512], BF16)
        kt = sring("mkt", 2, [96, 512], BF16)
        for h in range(8):
            for (c0, W) in chunks512():
                cs = slice(c0, c0 + W)
                pq, pq_k = pk.nxt()
                pqp, pqp_k = pk.nxt()
                for kc in range(2):
                    S.mm(pq[0:96, 0:W], wq[:, kc, h * 96:(h + 1) * 96], qnT[:, kc, cs], kc == 0, kc == 1, ["mqnT", "mw"], [pq_k])
                for kc in range(2):
                    S.mm(pqp[0:96, 0:W], wqp[:, kc, h, :], qnT[:, kc, cs], kc == 0, kc == 1, ["mqnT", "mw"], [pqp_k])
                q_t, q_k = qt.nxt()
                ta, ta_k = t12.nxt()
                tb, tb_k = t12.nxt()
                S.act(q_t[0:64, 0:W], pq[0:64, 0:W], AF.Copy, [], [q_k], [pq_k])
                S.tt("dve", ta[64:96, 0:W], pq[64:96, 0:W], rC[64:96, cs], ALU.mult, ["mrC"], [ta_k], [pq_k])
                S.tt("dve", tb[64:96, 0:W], pqp[64:96, 0:W], rS[64:96, cs], ALU.mult, ["mrC"], [tb_k], [pqp_k])
                S.tt("pool", q_t[64:96, 0:W], ta[64:96, 0:W], tb[64:96, 0:W], ALU.add, [ta_k, tb_k, q_k], [q_k])
                S.dma("sp", d["QT"][h, :, cs], q_t[:, 0:W], rd=[q_k], wr=["QT"])
                pk2, pk2_k = pk.nxt()
                S.mm(pk2[0:64, 0:W], wkv[:, h * 128:h * 128 + 64], kvnT[:, cs], True, True, ["mkvnT", "mw"], [pk2_k])
                k_t, k_k = kt.nxt()
                S.act(k_t[0:64, 0:W], pk2[0:64, 0:W], AF.Copy, [], [k_k], [pk2_k])
                S.cp("pool", k_t[64:96, 0:W], KRT[64:96, cs], ["mKRT", k_k], [k_k])
                S.dma("sp", d["KT"][h, :, cs], k_t[:, 0:W], rd=[k_k], wr=["KT"])
        S.emit()


def phase_attn(K):
    S, d = K.S, K.d
    sc = 1.0 / math.sqrt(96.0)
    with ExitStack() as ph:
        sb, pp, sring, pring = _mk(K, ph)
        Vs = sb("aV", [128, 34, 8, 65], BF16)
        S.ms("pool", Vs[:], 1.0, ["aV"])
        vv = d["V"].rearrange("(t p) (h x) -> p t h x", p=128, x=64)
        for h in range(8):
            S.dma("sp", Vs[:, :, h, 0:64], vv[:, :, h, :], rd=["V"], wr=["aV"])
        onesf = sb("aones", [128, 64], F32)
        S.ms("pool", onesf[:], 1.0, ["aones"])
        QTh = sring("aQ", 2, [96, NTOK], BF16)
        KTh = sring("aK", 2, [96, NTOK], BF16)
        pT = sring("apT", 4, [128, 1024], BF16)
        rc = sring("arc", 2, [65, 512], F32)
        oT = sring("aoT", 2, [64, 512], BF16)
        ps = pring("aps", 2, [128, 1024], F32)
        po = pring("apo", 2, [128, 512], F32)
        pb = pring("apb", 2, [128, 512], F32)
        qch = [(0, 256, 2)] + [(256 + 512 * i, 512, 34) for i in range(8)]
        rbt = sring("arb", 2, [64, 512], F32)
        tail = [None]

        def make_tail(h, q0, W, o_t, o_k):
            def run():
                r_t, r_k = rc.nxt()
                b_t, b_k = pb.nxt()
                f_t, f_k = rbt.nxt()
                g_t, g_k = oT.nxt()
                S.act(r_t[64:65, 0:W], o_t[64:65, 0:W], AF.Copy, [], [r_k], [o_k])
                S.mm(b_t[0:64, 0:W], onesf[64:65, 0:64], r_t[64:65, 0:W], True, True, ["aones", r_k], [b_k])
                S.op("dve", lambda e: e.reciprocal(out=f_t[:, 0:W], in_=b_t[0:64, 0:W]), [], [f_k], [b_k])
                S.tt("dve", g_t[:, 0:W], o_t[0:64, 0:W], f_t[:, 0:W], ALU.mult, [f_k], [g_k], [o_k])
                S.dma("pool", d["OT"][(h % 2) * 64:(h % 2) * 64 + 64, h // 2, q0:q0 + W], g_t[:, 0:W], rd=[g_k], wr=["OT"])
            return run

        SK = 2
        for h in range(8):
            Q, Q_k = QTh.nxt()
            Kt, K_k = KTh.nxt()
            S.dma("sp", Q[:], d["QT"][h], rd=["QT"], wr=[Q_k])
            S.dma("sp", Kt[:], d["KT"][h], rd=["KT"], wr=[K_k])
            for (q0, W, nk) in qch:
                o_t, o_k = po.nxt()
                pend = []
                npair = nk // 2
                for kp in range(npair + SK):
                    if kp < npair:
                        s_t, s_k = ps.nxt()
                        e_t, e_k = pT.nxt()
                        for u_ in range(2):
                            kt_ = 2 * kp + u_
                            S.mm(s_t[:, u_ * 512:u_ * 512 + W], Kt[:, kt_ * 128:(kt_ + 1) * 128], Q[:, q0:q0 + W], True, True, [Q_k, K_k], [s_k])
                        if W == 512:
                            S.act(e_t[:, :], s_t[:, :], AF.Exp, [], [e_k], [s_k], scale=sc)
                        else:
                            S.act(e_t[:, :].rearrange("p (u c) -> p u c", c=512)[:, :, 0:W], s_t[:, :].rearrange("p (u c) -> p u c", c=512)[:, :, 0:W],
                                  AF.Exp, [], [e_k], [s_k], scale=sc)
                        pend.append((kp, e_t, e_k))
                    if kp >= SK:
                        pp_, e_t2, e_k2 = pend.pop(0)
                        for u_ in range(2):
                            kt_ = 2 * pp_ + u_
                            S.mm(o_t[0:65, 0:W], Vs[:, kt_, h, :], e_t2[:, u_ * 512:u_ * 512 + W], kt_ == 0, kt_ == nk - 1, ["aV", e_k2], [o_k])
                    if kp == min(2, npair - 1) and tail[0] is not None:
                        tail[0]()
                        tail[0] = None
                if tail[0] is not None:
                    tail[0]()
                tail[0] = make_tail(h, q0, W, o_t, o_k)
        tail[0]()
        S.emit()


HYSEQ = {0: dict(L=NLAT, NA=64, NF=128, tok0=NCTX, base=259), 1: dict(L=NCTX, NA=4, NF=8, tok0=0, base=1)}
FBG = [(g * 8, 8) for g in range(8)] + [(64, 1)]


def _load_adft(K, sb, S, seq, kind):
    if seq == 0:
        t = sb("hadft", [128, 3, 128], BF16)
        S.dma("pool", t[:], K.d["hyA0"][:, :, :], wr=["hadft"])
    else:
        rows = 64 if kind == "s" else 128
        t = sb("hadft" + kind, [rows, 3, 128], BF16)
        S.dma("pool", t[:], K.d["hyA1" + kind][:, :, :], wr=["hadft"])
    return t


def _fwd_unit(S, Re, Re_k, Im, Im_k, has_im, Kr, M, A, pz):
    z, z_k = pz.nxt()
    Cf, Sf, nSf = A[0:Kr, 0, 0:M], A[0:Kr, 1, 0:M], A[0:Kr, 2, 0:M]
    S.mm(z[0:M, 0:512], Cf, Re, True, not has_im, [Re_k, "hadft"], [z_k])
    if has_im:
        S.mm(z[0:M, 0:512], Sf, Im, False, True, [Im_k, "hadft"], [z_k])
        S.mm(z[0:M, 512:1024], Cf, Im, True, False, [Im_k, "hadft"], [z_k])
    S.mm(z[0:M, 512:1024], nSf, Re, not has_im, True, [Re_k, "hadft"], [z_k])
    return z, z_k


def _mid_unit(S, Re, Re_k, Im, Im_k, has_im, Kr, M, AF_, AI, k4, k4_k, Rout, R_k, pz, pr, Zs, Pt, Yt):
    st = {}

    def stage_a():
        z, z_k = _fwd_unit(S, Re, Re_k, Im, Im_k, has_im, Kr, M, AF_, pz)
        z_t, zt_k = Zs.nxt()
        p_t, p_k = Pt.nxt()
        y_t, y_k = Yt.nxt()
        S.act(z_t[0:M, :, :], z[0:M, :].rearrange("p (r c) -> p r c", c=512), AF.Copy, [], [zt_k], [z_k])
        S.tt("dve", p_t[0:M, 0:2, :], z_t[0:M, :, :], k4[:, 0:2, :], ALU.mult, [zt_k, k4_k], [p_k + "a"])
        S.tt("dve", p_t[0:M, 2:4, :], z_t[0:M, :, :], k4[:, 2:4, :], ALU.mult, [zt_k, k4_k], [p_k + "b"])
        S.tt("pool", y_t[0:M, 0, :], p_t[0:M, 0, :], p_t[0:M, 1, :], ALU.subtract, [p_k + "a"], [y_k + "r"])
        S.tt("pool", y_t[0:M, 1, :], p_t[0:M, 2, :], p_t[0:M, 3, :], ALU.add, [p_k + "b"], [y_k + "i"])
        st["y"] = (y_t, y_k)

    def stage_b():
        y_t, y_k = st["y"]
        r, r_k = pr.nxt()
        Ci, Si, nSi = AI[0:M, 0, 0:M], AI[0:M, 1, 0:M], AI[0:M, 2, 0:M]
        yk = [y_k + "r", y_k + "i", "hadft"]
        S.mm(r[0:M, 0:512], Ci, y_t[0:M, 0, :], True, False, yk, [r_k])
        S.mm(r[0:M, 0:512], nSi, y_t[0:M, 1, :], False, True, yk, [r_k])
        if has_im:
            S.mm(r[0:M, 512:1024], Si, y_t[0:M, 0, :], True, False, yk, [r_k])
            S.mm(r[0:M, 512:1024], Ci, y_t[0:M, 1, :], False, True, yk, [r_k])
            S.act(Rout, r[0:M, :].rearrange("p (r c) -> p r c", c=512), AF.Copy, [], [R_k], [r_k])
        else:
            S.act(Rout[:, 0, :], r[0:M, 0:512], AF.Copy, [], [R_k], [r_k])
    return stage_a, stage_b


def _k4_unit(S, Re, Re_k, Im, Im_k, has_im, Kr, M, A, pz, k4, k4_k):
    z, z_k = _fwd_unit(S, Re, Re_k, Im, Im_k, has_im, Kr, M, A, pz)
    S.act(k4[:, 0:2, :], z[0:M, :].rearrange("p (r c) -> p r c", c=512), AF.Copy, [], [k4_k], [z_k])
    S.cp("dve", k4[:, 3, :], z[0:M, 0:512], [], [k4_k], [z_k])
    S.cp("dve", k4[:, 2, :], z[0:M, 512:1024], [], [k4_k], [z_k])


def phase_hy_filters(K, seq):
    S, d = K.S, K.d
    q = HYSEQ[seq]
    L, NA, NF = q["L"], q["NA"], q["NF"]
    P2 = 2 * L
    npt = P2 // 128
    with ExitStack() as ph:
        sb, pp, sring, pring = _mk(K, ph)
        ZT = sb("hZT", [33, P2], F32)
        S.dma("sp", ZT[:], d["hyZ%d" % seq][:, :], wr=["hZT"])
        w1 = sb("hw1", [33, 64], F32)
        w2 = sb("hw2", [64, 64], F32)
        w3 = sb("hw3", [64, 2048], F32)
        bb = sb("hbb", [64, 2], F32)
        S.dma("sp", w1[:], d["hy_w1"][:, :], wr=["hw"])
        S.dma("sp", w2[:], d["hy_w2"][:, :], wr=["hw"])
        S.dma("sp", w3[:], d["hy_w3"][:, :], wr=["hw"])
        S.dma("sp", bb[:], d["hy_b_col"][:, :], wr=["hw"])
        ntn = sb("hntn", [128, npt], F32)
        S.dma("sp", ntn[:], d["hyT%d" % seq][:, :], wr=["hw"])
        dl = sb("hdl", [128, 512], F32)
        S.dma("sp", dl[:], d["hy_delta"][0:1, :].to_broadcast([128, 512]), wr=["hw"])
        Fb2 = sb("hFb2", [128, 128], BF16)
        S.dma("pool", Fb2[:], d["hyFb2"][:, :], wr=["hw"])
        A = _load_adft(K, sb, S, seq, "k")
        a2T = sb("ha2T", [64, P2], F32)
        pm = pring("hpm", 1, [128, 512], F32)
        ut = sring("hut", 2, [64, 512], F32)
        s2t = sring("hs2", 2, [64, 512], F32)
        s4t = sring("hs4", 2, [64, 512], F32)
        a1t = sring("ha1", 2, [64, 512], F32)

        def sinop(p_t, p_k, bcol, out, out_k, W):
            u, u_k = ut.nxt()
            a, a_k = s2t.nxt()
            b, b_k = s4t.nxt()
            S.act(u[:, 0:W], p_t[0:64, 0:W], AF.Identity, ["hw"], [u_k], [p_k], bias=bcol)
            S.act(a[:, 0:W], u[:, 0:W], AF.Sin, [u_k], [a_k], scale=0.5)
            S.act(b[:, 0:W], u[:, 0:W], AF.Sin, [u_k], [b_k], scale=0.25)
            S.tt("dve", b[:, 0:W], b[:, 0:W], b[:, 0:W], ALU.mult, [b_k], [b_k])
            S.ts("dve", b[:, 0:W], b[:, 0:W], -2.0, 1.0, ALU.mult, ALU.add, [b_k], [b_k])
            S.stt("dve", out, a[:, 0:W], 2.0, b[:, 0:W], ALU.mult, ALU.mult, [a_k, b_k], [out_k])

        W = min(512, P2)
        for c in range(P2 // W):
            cs = slice(c * W, (c + 1) * W)
            p_t, p_k = pm.nxt()
            a1, a1_k = a1t.nxt()
            S.mm(p_t[0:64, 0:W], w1[:, :], ZT[:, cs], True, True, ["hw", "hZT"], [p_k])
            sinop(p_t, p_k, bb[:, 0:1], a1[:, 0:W], a1_k, W)
            p_t, p_k = pm.nxt()
            S.mm(p_t[0:64, 0:W], w2[:, :], a1[:, 0:W], True, True, ["hw", a1_k], [p_k])
            sinop(p_t, p_k, bb[:, 1:2], a2T[:, cs], "ha2T", W)
        S.ms("pool", a2T[:, L:L + 1], 0.0, ["ha2T"])
        pf = pring("hpf", 1, [128, 512], F32)
        px = pring("hpx", 2, [128, 512], F32)
        dec = sring("hdec", 2, [128, 512], F32)
        k2 = sring("hk2", 3, [128, 512], BF16)
        xk = sring("hxk", 3, [128, 512], BF16)
        pz = pring("hpz", 2, [128, 1024], F32)
        if seq == 0:
            Ret = sring("hRe", 2, [NF, 8, 512], BF16)
            Imt = sring("hIm", 2, [NF, 8, 512], BF16)
            K4 = sring("hK4", 2, [NF, 8, 4, 512], BF16)
        else:
            Ret = sring("hRe", 2, [128, 512], BF16)
            Imt = sring("hIm", 2, [128, 512], BF16)
            K4 = sring("hK4", 2, [128, 4, 512], BF16)
        D1k = d["D1k"] if seq == 0 else d["D1k1"]
        KH = d["KH%d" % seq]
        for o in range(2):
            units = []
            for pt_ in range(npt):
                def mk(pt_=pt_):
                    st = {}

                    def a_():
                        dirn = 0 if pt_ * 128 < L else 1
                        f_t, f_k = pf.nxt()
                        e_t, e_k = dec.nxt()
                        k_t, k_k = k2.nxt()
                        S.mm(f_t[:], a2T[:, pt_ * 128:(pt_ + 1) * 128], w3[:, o * 1024 + dirn * 512:o * 1024 + dirn * 512 + 512], True, True,
                             ["ha2T", "hw"], [f_k])
                        S.act(e_t[:], dl[:], AF.Exp, ["hw"], [e_k], scale=ntn[:, pt_:pt_ + 1])
                        S.tt("dve", k_t[:], f_t[:], e_t[:], ALU.mult, [e_k], [k_k], [f_k])
                        st["k"] = (k_t, k_k)

                    def b_():
                        k_t, k_k = st["k"]
                        for half in range(2):
                            x_t, x_k = px.nxt()
                            y_t, y_k = xk.nxt()
                            hs = slice(half * 64, half * 64 + 64)
                            S.mm(x_t[:], Fb2[hs, :], k_t[hs, :], True, True, ["hw", k_k], [x_k])
                            S.act(y_t[:], x_t[:], AF.Copy, [], [y_k], [x_k])
                            al = 2 * pt_ + half
                            S.dma("act", D1k[al, :, :] if seq == 0 else D1k[:, al, :], y_t[:], rd=[y_k], wr=["D1k"])
                    return a_, b_
                units.append(mk())
            _run_skewed(units, 1)
            if seq == 0:
                for (fb0, G) in FBG:
                    Re, Re_k = Ret.nxt()
                    Im, Im_k = Imt.nxt()
                    k4, k4_k = K4.nxt()
                    S.dma("sp", Re[:, 0:G, :], D1k[0:NF, fb0:fb0 + G, :], rd=["D1k"], wr=[Re_k])
                    if fb0 < 64:
                        S.dma("sp", Im[:, 0:G, :], D1k[0:NF, 64 + fb0:64 + fb0 + G, :], rd=["D1k"], wr=[Im_k])
                    for g in range(G):
                        fb = fb0 + g
                        _k4_unit(S, Re[:, g, :], Re_k, Im[:, g, :], Im_k, fb not in (0, 64), NF, NF, A, pz, k4[:, g, :, :], k4_k + "g%d" % g)
                    S.dma("act", KH[o, :, fb0:fb0 + G, :, :], k4[:, 0:G, :, :], rd=[k4_k + "g%d" % g for g in range(G)], wr=["KH%d" % seq])
            else:
                for fb0 in (0, 16, 32, 48, 64):
                    G = 16 if fb0 < 64 else 1
                    R_ = G * NF
                    Re, Re_k = Ret.nxt()
                    Im, Im_k = Imt.nxt()
                    k4, k4_k = K4.nxt()
                    S.dma("sp", Re[0:R_, :], D1k[fb0:fb0 + G].rearrange("f a c -> (f a) c"), rd=["D1k"], wr=[Re_k])
                    if fb0 < 64:
                        S.dma("sp", Im[0:R_, :], D1k[64 + fb0:64 + fb0 + G].rearrange("f a c -> (f a) c"), rd=["D1k"], wr=[Im_k])
                        if fb0 == 0:
                            S.ms("pool", Im[0:NF, :], 0.0, [Im_k])
                    _k4_unit(S, Re[0:R_, :], Re_k, Im[0:R_, :], Im_k, fb0 < 64, R_, R_, A, pz, k4[0:R_, :, :], k4_k)
                    S.dma("act", KH[o, fb0:fb0 + G].rearrange("f a k c -> (f a) k c"), k4[0:R_, :, :], rd=[k4_k], wr=["KH%d" % seq])
        S.emit()


def _s1_store(S, Fb2, src_bf, src_k, px, xk, dst, a0, seq=0):
    for half in range(2):
        x_t, x_k = px.nxt()
        y_t, y_k = xk.nxt()
        hs = slice(half * 64, half * 64 + 64)
        S.mm(x_t[:], Fb2[hs, :], src_bf[hs, :], True, True, ["hFb2", src_k], [x_k])
        S.act(y_t[:], x_t[:], AF.Copy, [], [y_k], [x_k])
        S.dma("act", dst[a0 + half, :, :] if seq == 0 else dst[:, a0 + half, :], y_t[:], rd=[y_k], wr=["D1"])


def phase_hy_in(K):
    S, d = K.S, K.d
    with ExitStack() as ph:
        sb, pp, sring, pring = _mk(K, ph)
        HTs = sb("iHT", [128, 8, NTOK + 4], BF16)
        for c in (0, 257, 258, NTOK + 3):
            S.ms("pool", HTs[:, :, c:c + 1], 0.0, ["iHT"])
        for kc in range(8):
            S.dma("sp", HTs[:, kc, 1:257], d["HT"][:, kc, 0:NCTX], rd=["HT"], wr=["iHT"])
            S.dma("sp", HTs[:, kc, 259:259 + NLAT], d["HT"][:, kc, NCTX:NTOK], rd=["HT"], wr=["iHT"])
        Fb2 = sb("iFb2", [128, 128], BF16)
        S.dma("pool", Fb2[:], d["hyFb2"][:, :], wr=["hFb2"])
        cwbc = sb("icw", [128, 3, 1536], F32)
        for tap in range(3):
            S.dma("sp", cwbc[:, tap, :], d["hy_conv_w"][tap:tap + 1, :].to_broadcast([128, 1536]), wr=["icw"])
        brow = sb("ibrow", [1, 1536], BF16)
        S.dma("pool", brow[:], d["hy_conv_b"][0:1, :], wr=["ibrow"])
        ones = sb("iones", [1, 128], BF16)
        S.ms("pool", ones[:], 1.0, ["iones"])
        win = d["ab_w_in"].rearrange("(kc p) n -> p kc n", p=128)
        stg = sring("istg", 2, [128, 8, 512], F32)
        W3 = sring("iW3", 2, [128, 3, 8, 512], BF16)
        pz = pring("ipz", 3, [128, 512], F32)
        px = pring("ipx", 2, [128, 512], F32)
        xk = sring("ixk", 3, [128, 512], BF16)
        vb = sring("ivb", 2, [128, 512], BF16)
        vf = sring("ivf", 3, [128, 512], F32)
        for n in range(3):
            g_t, g_k = stg.nxt()
            w_t, w_k = W3.nxt()
            S.dma("sp", g_t[:], win[:, :, 416 + n * 512:416 + (n + 1) * 512], wr=[g_k])
            for tap in range(3):
                for kc in range(8):
                    S.tt("dve" if (kc % 2 == 0) else "pool", w_t[:, tap, kc, :], g_t[:, kc, :], cwbc[:, tap, n * 512:(n + 1) * 512],
                         ALU.mult, [g_k, "icw"], [w_k])
            for seq in (1, 0):
                q = HYSEQ[seq]
                for t in range(q["L"] // 128):
                    p_t, p_k = pz.nxt()
                    first = True
                    for tap in range(3):
                        c0 = q["base"] + t * 128 + tap - 1
                        for kc in range(8):
                            S.mm(p_t[:], HTs[:, kc, c0:c0 + 128], w_t[:, tap, kc, :], first, False, ["iHT", w_k], [p_k])
                            first = False
                    S.mm(p_t[:], ones[0:1, :], brow[0:1, n * 512:(n + 1) * 512], False, True, ["iones", "ibrow"], [p_k])
                    rows = slice(q["tok0"] + t * 128, q["tok0"] + (t + 1) * 128)
                    f_t, f_k = vf.nxt()
                    S.cp("dve", f_t[:], p_t[:], [], [f_k], [p_k])
                    S.dma("pool", d["VX"][rows, n, :], f_t[:], rd=[f_k], wr=["VX"])
                    if n == 0:
                        b_t, b_k = vb.nxt()
                        S.act(b_t[:], p_t[:], AF.Copy, [], [b_k], [p_k])
                        _s1_store(S, Fb2, b_t, b_k, px, xk, d["D1_%d" % seq], 2 * t, seq)
        S.emit()


def phase_hy_mid(K, seq, o):
    S, d = K.S, K.d
    q = HYSEQ[seq]
    NA, NF = q["NA"], q["NF"]
    with ExitStack() as ph:
        sb, pp, sring, pring = _mk(K, ph)
        Zs = sring("mZs", 3, [128, 2, 512], BF16)
        Pt = sring("mP", 3, [128, 4, 512], BF16)
        Yt = sring("mY", 3, [128, 2, 512], BF16)
        pz = pring("mpz", 2, [128, 1024], F32)
        pr = pring("mpr", 2, [128, 1024], F32)
        D1, D2, KH = d["D1_%d" % seq], d["D2_%d" % seq], d["KH%d" % seq]
        if seq == 0:
            A = _load_adft(K, sb, S, 0, "k")
            Ret = sring("mRe", 2, [NA, 8, 512], BF16)
            Imt = sring("mIm", 2, [NA, 8, 512], BF16)
            K4g = sring("mK4", 2, [NF, 8, 4, 512], BF16)
            Rg = sring("mRg", 2, [NF, 2, 8, 512], BF16)
            units = []
            for (fb0, G) in FBG:
                def load(fb0=fb0, G=G):
                    Re, Re_k = Ret.nxt()
                    Im, Im_k = Imt.nxt()
                    kg, kg_k = K4g.nxt()
                    rg, rg_k = Rg.nxt()
                    S.dma("sp", Re[:, 0:G, :], D1[0:NA, fb0:fb0 + G, :], rd=["D1"], wr=[Re_k])
                    if fb0 < 64:
                        S.dma("sp", Im[:, 0:G, :], D1[0:NA, 64 + fb0:64 + fb0 + G, :], rd=["D1"], wr=[Im_k])
                    S.dma("sp", kg[:, 0:G, :, :], KH[o, :, fb0:fb0 + G, :, :], rd=["KH%d" % seq], wr=[kg_k])
                    return Re, Re_k, Im, Im_k, kg, kg_k, rg, rg_k
                grp = {}
                for g in range(G):
                    fb = fb0 + g

                    def mk(fb=fb, g=g, fb0=fb0, G=G, grp=grp, load=load):
                        ab = {}

                        def a_():
                            if g == 0:
                                grp["t"] = load()
                            Re, Re_k, Im, Im_k, kg, kg_k, rg, rg_k = grp["t"]
                            ab["u"] = _mid_unit(S, Re[:, g, :], Re_k, Im[:, g, :], Im_k, fb not in (0, 64), NA, NF, A, A,
                                                kg[:, g, :, :], kg_k, rg[:, :, g, :], rg_k + "g%d" % g, pz, pr, Zs, Pt, Yt)
                            ab["u"][0]()

                        def b_():
                            ab["u"][1]()
                            if g == G - 1:
                                Re, Re_k, Im, Im_k, kg, kg_k, rg, rg_k = grp["t"]
                                rk = [rg_k + "g%d" % q_ for q_ in range(G)]
                                S.dma("sp", D2[0:NF, fb0:fb0 + G, :], rg[:, 0, 0:G, :], rd=rk, wr=["D2"])
                                if fb0 < 64:
                                    s0 = 1 if fb0 == 0 else 0
                                    S.dma("sp", D2[0:NF, 64 + fb0 + s0:64 + fb0 + G, :], rg[:, 1, s0:G, :], rd=rk, wr=["D2"])
                        return a_, b_
                    units.append(mk())
            _run_skewed(units, 2)
        else:
            As = _load_adft(K, sb, S, 1, "s")
            Ak = _load_adft(K, sb, S, 1, "k")
            Ret = sring("mRe", 2, [64, 512], BF16)
            Imt = sring("mIm", 2, [64, 512], BF16)
            K4g = sring("mK4", 2, [128, 4, 512], BF16)
            Rg = sring("mRg", 2, [128, 2, 512], BF16)
            for fb0 in (0, 16, 32, 48, 64):
                G = 16 if fb0 < 64 else 1
                Kr, M = G * NA, G * NF
                Re, Re_k = Ret.nxt()
                Im, Im_k = Imt.nxt()
                kg, kg_k = K4g.nxt()
                rg, rg_k = Rg.nxt()
                S.dma("sp", Re[0:Kr, :], D1[fb0:fb0 + G].rearrange("f a c -> (f a) c"), rd=["D1"], wr=[Re_k])
                if fb0 < 64:
                    S.dma("sp", Im[0:Kr, :], D1[64 + fb0:64 + fb0 + G].rearrange("f a c -> (f a) c"), rd=["D1"], wr=[Im_k])
                    if fb0 == 0:
                        S.ms("pool", Im[0:NA, :], 0.0, [Im_k])
                S.dma("sp", kg[0:M, :, :], KH[o, fb0:fb0 + G].rearrange("f a k c -> (f a) k c"), rd=["KH%d" % seq], wr=[kg_k])
                ua, ub = _mid_unit(S, Re[0:Kr, :], Re_k, Im[0:Kr, :], Im_k, fb0 < 64, Kr, M, As, Ak, kg[0:M, :, :], kg_k,
                                   rg[0:M, :, :], rg_k, pz, pr, Zs, Pt, Yt)
                ua()
                ub()
                S.dma("act", D2[fb0:fb0 + G].rearrange("f a c -> (f a) c"), rg[0:M, 0, :], rd=[rg_k], wr=["D2"])
                if fb0 < 64:
                    s0 = 1 if fb0 == 0 else 0
                    S.dma("act", D2[64 + fb0 + s0:64 + fb0 + G].rearrange("f a c -> (f a) c"), rg[s0 * NF:M, 1, :], rd=[rg_k], wr=["D2"])
        S.emit()


def phase_hy_out(K, seq, o):
    S, d = K.S, K.d
    q = HYSEQ[seq]
    NA, NF = q["NA"], q["NF"]
    with ExitStack() as ph:
        sb, pp, sring, pring = _mk(K, ph)
        Gb = sb("oGb", [128, 128], BF16)
        S.dma("pool", Gb[:], d["hyGb%d" % seq][:, :], wr=["oGb"])
        Fb2 = sb("oFb2", [128, 128], BF16)
        S.dma("pool", Fb2[:], d["hyFb2"][:, :], wr=["hFb2"])
        dbc = sb("odbc", [128, 512], F32)
        S.dma("sp", dbc[:], d["hy_bias"][o:o + 1, :].to_broadcast([128, 512]), wr=["odbc"])
        Tt = sring("oT", 2, [128, 3, 512], BF16)
        ut = sring("ou", 2, [128, 512], F32)
        xt = sring("oxg", 3, [128, 512], F32)
        tt_ = sring("ot", 3, [128, 512], F32)
        zb = sring("ozb", 2, [128, 512], BF16)
        xk = sring("oxk", 3, [128, 512], BF16)
        yo = sring("oyo", 2, [128, 4, 128], BF16)
        py = pring("opy", 2, [128, 512], F32)
        px = pring("opx", 2, [128, 512], F32)
        ptr = pring("optr", 2, [128, 1024], BF16)
        D2 = d["D2_%d" % seq]
        D2v = D2.rearrange("a f c -> f a c") if seq == 0 else D2
        units = []
        for i in range(NA // 2):
            def mk(i=i):
                st = {}

                def a_():
                    A0 = 2 * i
                    T, T_k = Tt.nxt()
                    if i == 0:
                        S.dma("sp", T[:, 0:1, :], D2v[:, NF - 1:NF, :], rd=["D2"], wr=[T_k])
                        S.dma("sp", T[:, 1:3, :], D2v[:, 0:2, :], rd=["D2"], wr=[T_k])
                    else:
                        S.dma("sp", T[:], D2v[:, A0 - 1:A0 + 2, :], rd=["D2"], wr=[T_k])
                    y_t, y_k = py.nxt()
                    for s_ in range(2):
                        rs_ = slice(s_ * 64, s_ * 64 + 64)
                        S.mm(y_t[rs_, :], Gb[:, 0:64], T[:, 1 + s_, :], True, False, ["oGb", T_k], [y_k])
                        S.mm(y_t[rs_, :], Gb[:, 64:128], T[:, s_, :], False, True, ["oGb", T_k], [y_k])
                    rows = slice(q["tok0"] + i * 128, q["tok0"] + (i + 1) * 128)
                    u_t, u_k = ut.nxt()
                    g_t, g_k = xt.nxt()
                    t_t, t_k = tt_.nxt()
                    if o == 0:
                        S.dma("sp", u_t[:], d["VX"][rows, 0, :], rd=["VX"], wr=[u_k])
                        S.dma("sp", g_t[:], d["VX"][rows, 1, :], rd=["VX"], wr=[g_k])
                    else:
                        S.dma("sp", u_t[:], d["Z1"][rows, :], rd=["Z1"], wr=[u_k])
                        S.dma("sp", g_t[:], d["VX"][rows, 2, :], rd=["VX"], wr=[g_k])
                    S.tt("pool", t_t[:], u_t[:], dbc[:], ALU.mult, [u_k, "odbc"], [t_k])
                    S.tt("dve", t_t[:], t_t[:], y_t[:], ALU.add, [t_k], [t_k], [y_k])
                    st['v'] = (t_t, t_k, g_t, g_k, rows, A0)

                def b_():
                    t_t, t_k, g_t, g_k, rows, A0 = st['v']
                    if o == 0:
                        S.tt("pool", t_t[:], t_t[:], g_t[:], ALU.mult, [t_k, g_k], [t_k])
                        S.dma("pool", d["Z1"][rows, :], t_t[:], rd=[t_k], wr=["Z1"])
                        b_t, b_k = zb.nxt()
                        S.act(b_t[:], t_t[:], AF.Copy, [t_k], [b_k])
                        _s1_store(S, Fb2, b_t, b_k, px, xk, d["D1_%d" % seq], A0, seq)
                    else:
                        b_t, b_k = zb.nxt()
                        S.tt("pool", b_t[:], t_t[:], g_t[:], ALU.mult, [t_k, g_k], [b_k])
                        p_t, p_k = ptr.nxt()
                        for cc in range(4):
                            S.tr(p_t[:, cc * 128:(cc + 1) * 128], b_t[:, cc * 128:(cc + 1) * 128], K.identb[:], [b_k], [p_k])
                        o_t, o_k = yo.nxt()
                        S.act(o_t[:, :, :], p_t[:, 0:512].rearrange("p (c t) -> p c t", t=128), AF.Copy, [], [o_k], [p_k])
                        S.dma("act", d["OT"][:, 4:8, rows], o_t[:], rd=[o_k], wr=["OT"])

                return a_, b_
            units.append(mk())
        _run_skewed(units, 1)
        S.emit()


def hyena_all(K):
    for seq in (0, 1):
        phase_hy_filters(K, seq)
    phase_hy_in(K)
    for seq in (0, 1):
        for o in range(2):
            phase_hy_mid(K, seq, o)
            phase_hy_out(K, seq, o)


def phase_hg_proj(K):
    S, d = K.S, K.d
    with ExitStack() as ph:
        sb, pp, sring, pring = _mk(K, ph)
        win = sb("gwin", [128, 8, 5120], BF16)
        wv = d["hg_w_in"].rearrange("(kc p) n -> p kc n", p=128)
        for c in range(10):
            S.dma("pool", win[:, :, c * 512:(c + 1) * 512], wv[:, :, c * 512:(c + 1) * 512], wr=["gwin"])
        lg = sb("glg", [128, 2, 2, DM], F32)
        for l in range(2):
            for dd in range(2):
                S.dma("sp", lg[:, l, dd, :], d["hg_lb_logits"][l, dd:dd + 1, :].to_broadcast([128, DM]), wr=["glg"])
        lb = sb("glb", [128, 2, DM], F32)
        oml = sb("goml", [128, 2, DM], F32)
        S.tt("dve", lg[:, 0, :, :], lg[:, 1, :, :], lg[:, 0, :, :], ALU.subtract, ["glg"], ["glg"])
        S.act(lb[:], lg[:, 0, :, :], AF.Sigmoid, ["glg"], ["glb"])
        S.act(oml[:], lg[:, 0, :, :], AF.Sigmoid, ["glg"], ["glb"], scale=-1.0)
        hT = sring("ghT", 2, [128, 8, 512], BF16)
        pz = pring("gpz", 4, [128, 512], F32)
        ob = sring("gob", 4, [128, DM], BF16)
        of = sring("gof", 3, [128, DM], F32)
        sg = sring("gsg", 2, [128, DM], F32)
        for t in range(34):
            rows = slice(t * 128, (t + 1) * 128)
            jj = t if t < 2 else (t - 2) % 4
            if jj == 0:
                h_t, h_k = hT.nxt()
                wd_ = 256 if t < 2 else 512
                S.dma("sp", h_t[:, :, 0:wd_], d["HT"][:, :, t * 128:t * 128 + wd_], rd=["HT"], wr=[h_k])
            for grp in range(5):
                if grp == 4 and t < 2:
                    continue
                pzs = []
                for nh in range(2):
                    p_t, p_k = pz.nxt()
                    c0 = grp * 1024 + nh * 512
                    for kc in range(8):
                        S.mm(p_t[:], h_t[:, kc, jj * 128:(jj + 1) * 128], win[:, kc, c0:c0 + 512], kc == 0, kc == 7, [h_k, "gwin"], [p_k])
                    pzs.append((p_t, p_k))
                if grp in (0, 4):
                    o_t, o_k = ob.nxt()
                    for nh in range(2):
                        S.act(o_t[:, nh * 512:(nh + 1) * 512], pzs[nh][0][:], AF.Silu, [], [o_k], [pzs[nh][1]])
                    if grp == 0:
                        S.dma("act", d["QS"][rows, :], o_t[:], rd=[o_k], wr=["QS"])
                    else:
                        S.dma("act", d["GG"][t * 128 - NCTX:(t + 1) * 128 - NCTX, :], o_t[:], rd=[o_k], wr=["GG"])
                elif grp == 3:
                    o_t, o_k = ob.nxt()
                    for nh in range(2):
                        S.cp("dve", o_t[:, nh * 512:(nh + 1) * 512], pzs[nh][0][:], [], [o_k], [pzs[nh][1]])
                    S.dma("pool", d["VV"][rows, :], o_t[:], rd=[o_k], wr=["VV"])
                else:
                    dd = grp - 1
                    s_t, s_k = sg.nxt()
                    f_t, f_k = of.nxt()
                    o_t, o_k = ob.nxt()
                    for nh in range(2):
                        S.act(s_t[:, nh * 512:(nh + 1) * 512], pzs[nh][0][:], AF.Sigmoid, [], [s_k], [pzs[nh][1]])
                    S.tt("dve", s_t[:], s_t[:], oml[:, dd, :], ALU.mult, [s_k, "glb"], [s_k])
                    S.tt("pool", s_t[:], s_t[:], lb[:, dd, :], ALU.add, [s_k, "glb"], [s_k])
                    S.act(f_t[:], s_t[:], AF.Ln, [s_k], [f_k])
                    S.ts("pool", o_t[:], s_t[:], -1.0, 1.0, ALU.mult, ALU.add, [s_k], [o_k])
                    S.dma("pool", d["LF"][rows, dd, :], f_t[:], rd=[f_k], wr=["LF"])
                    S.dma("pool", d["KK"][rows, dd, :], o_t[:], rd=[o_k], wr=["KK"])
        S.emit()


def phase_hg_scan(K, dd):
    S, d = K.S, K.d
    with ExitStack() as ph:
        sb, pp, sring, pring = _mk(K, ph)
        C = sb("sC", [128, 3, 128], F32)
        S.dma("sp", C[:], d["hgC"][:, dd, 0:3, :], wr=["sC"])
        TRI, TRIM, REV = (C[:, i, :] for i in range(3))
        MK = sb("sMK", [128, 2, 512], F32)
        for j in range(4):
            S.dma("sp", MK[:, :, j * 128:(j + 1) * 128], d["hgC"][:, dd, 3:5, :], wr=["sC"])
        PB, NB = MK[:, 0, :], MK[:, 1, :]
        St = sb("sS", [128, 8, 128], F32)
        Sb = sb("sSb", [128, 8, 128], BF16)
        S.ms("pool", St[:], 0.0, ["sS0", "sS1"])
        S.ms("pool", Sb[:], 0.0, ["sSb0", "sSb1"])
        lf = sring("slf", 2, [128, DM], F32)
        kk = sring("skk", 2, [128, DM], BF16)
        qs = sring("sqs", 2, [128, DM], BF16)
        vv = sring("svv", 2, [128, DM], BF16)
        E1 = sring("sE1", 4, [128, 512], F32)
        E2 = sring("sE2", 2, [128, 512], F32)
        E3 = sring("sE3", 2, [128, 512], F32)
        Er = sring("sEr", 2, [128, 512], F32)
        kb = sring("skb", 2, [128, DM], BF16)
        q1r = sring("sq1", 4, [128, 512], BF16)
        q2r = sring("sq2", 2, [128, 512], BF16)
        k2r = sring("sk2", 2, [128, 512], BF16)
        am = sring("sam", 2, [128, 512], F32)
        ab = sring("sab", 2, [128, 512], BF16)
        tmpS = sring("stS", 2, [128, 4, 128], F32)
        ot = sring("sot", 2, [128, DM], F32)
        pcum = pring("spcu", 1, [128, 512], F32)
        pcr = pring("spcr", 1, [128, 512], F32)
        prv = pring("sprv", 1, [128, 512], F32)
        ptp = pring("sptp", 1, [128, 1024], BF16)
        pat = pring("spat", 1, [128, 512], F32)
        po = pring("spo", 2, [128, 512], F32)
        pS = pring("spS", 1, [128, 512], F32)
        order = [0, 1] + list(range(2, 34)) if dd == 0 else [1, 0] + list(range(33, 1, -1))
        chunks = (0, 1) if dd == 0 else (1, 0)
        endcol = (63, 127) if dd == 0 else (0, 64)
        dst = d["OF"] if dd == 0 else d["OB"]
        for t in order:
            rows = slice(t * 128, (t + 1) * 128)
            need_o = t >= 2
            lf_t, lf_k = lf.nxt()
            kk_t, kk_k = kk.nxt()
            vv_t, vv_k = vv.nxt()
            S.dma("sp", lf_t[:], d["LF"][rows, dd, :], rd=["LF"], wr=[lf_k])
            S.dma("sp", kk_t[:], d["KK"][rows, dd, :], rd=["KK"], wr=[kk_k])
            S.dma("sp", vv_t[:], d["VV"][rows, :], rd=["VV"], wr=[vv_k])
            if need_o:
                qs_t, qs_k = qs.nxt()
                S.dma("sp", qs_t[:], d["QS"][rows, :], rd=["QS"], wr=[qs_k])
                o_sb, o_sk = ot.nxt()
            kb_t, kb_k = kb.nxt()
            G = []
            for g in range(2):
                cs = slice(g * 512, (g + 1) * 512)
                st = {}
                c_t, c_k = pcum.nxt()
                for j in range(4):
                    hs = slice((4 * g + j) * 128, (4 * g + j + 1) * 128)
                    S.mm(c_t[:, j * 128:(j + 1) * 128], lf_t[:, hs], TRI, True, True, ["sC", lf_k], [c_k])
                e1, e1_k = E1.nxt()
                S.act(e1[:], c_t[:], AF.Exp, [], [e1_k], [c_k])
                r_t, r_k = prv.nxt()
                S.mm(r_t[:], REV, lf_t[:, cs], True, True, ["sC", lf_k], [r_k])
                er, er_k = Er.nxt()
                S.act(er[:], r_t[:], AF.Exp, [], [er_k], [r_k])
                S.tt("pool", kb_t[:, cs], kk_t[:, cs], er[:], ALU.mult, [kk_k, er_k], [kb_k + "g%d" % g])
                st.update(e1=e1, e1_k=e1_k)
                if need_o:
                    cr_t, cr_k = pcr.nxt()
                    for j in range(4):
                        hs = slice((4 * g + j) * 128, (4 * g + j + 1) * 128)
                        S.mm(cr_t[:, j * 128:(j + 1) * 128], lf_t[:, hs], TRIM, True, True, ["sC", lf_k], [cr_k])
                    e2, e2_k = E2.nxt()
                    e3, e3_k = E3.nxt()
                    S.act(e2[:], cr_t[:], AF.Exp, [], [e2_k], [cr_k])
                    S.act(e3[:], cr_t[:], AF.Exp, [], [e3_k], [cr_k], scale=-1.0)
                    tp, tp_k = ptp.nxt()
                    for j in range(4):
                        hs = slice((4 * g + j) * 128, (4 * g + j + 1) * 128)
                        S.tr(tp[:, j * 128:(j + 1) * 128], qs_t[:, hs], K.identb[:], [qs_k], [tp_k])
                        S.tr(tp[:, 512 + j * 128:512 + (j + 1) * 128], kk_t[:, hs], K.identb[:], [kk_k], [tp_k])
                    q1, q1_k = q1r.nxt()
                    q2, q2_k = q2r.nxt()
                    k2, k2_k = k2r.nxt()
                    S.tt("dve", q1[:], tp[:, 0:512], e1[:], ALU.mult, [e1_k], [q1_k], [tp_k])
                    S.tt("dve", q2[:], tp[:, 0:512], e2[:], ALU.mult, [e2_k], [q2_k], [tp_k])
                    S.tt("dve", k2[:], tp[:, 512:1024], e3[:], ALU.mult, [e3_k], [k2_k], [tp_k])
                    a_t, a_k = pat.nxt()
                    for j in range(4):
                        js = slice(j * 128, (j + 1) * 128)
                        S.mm(a_t[:, js], k2[:, js], q2[:, js], True, True, [k2_k, q2_k], [a_k])
                    m_t, m_k = am.nxt()
                    b_t, b_k = ab.nxt()
                    S.tt("dve", m_t[:], a_t[:], NB, ALU.max, ["sC"], [m_k], [a_k])
                    S.tt("dve", b_t[:], m_t[:], PB, ALU.min, ["sC", m_k], [b_k])
                    o_t, o_k = po.nxt()
                    for j in range(4):
                        js = slice(j * 128, (j + 1) * 128)
                        hs = slice((4 * g + j) * 128, (4 * g + j + 1) * 128)
                        S.mm(o_t[:, js], b_t[:, js], vv_t[:, hs], j == 0, False, [b_k, vv_k], [o_k], skip_group_check=True)
                    st.update(q1=q1, q1_k=q1_k, o_t=o_t, o_k=o_k)
                G.append(st)
            for ci, c in enumerate(chunks):
                rs_ = slice(c * 64, c * 64 + 64)
                for g in range(2):
                    st = G[g]
                    e1, e1_k = st["e1"], st["e1_k"]
                    if need_o:
                        for j in range(4):
                            js = slice(j * 128, (j + 1) * 128)
                            S.mm(st["o_t"][rs_, js], st["q1"][:, j * 128 + c * 64:j * 128 + c * 64 + 64], Sb[:, 4 * g + j, :], False,
                                 (ci == 1 and j == 3), [st["q1_k"], "sSb%d" % g], [st["o_k"]], skip_group_check=True)
                    s_t, s_k = pS.nxt()
                    for j in range(4):
                        js = slice(j * 128, (j + 1) * 128)
                        hs = slice((4 * g + j) * 128, (4 * g + j + 1) * 128)
                        S.mm(s_t[:, js], kb_t[rs_, hs], vv_t[rs_, hs], j == 0, j == 3, [kb_k + "g%d" % g, vv_k], [s_k], skip_group_check=True)
                    x_t, x_k = tmpS.nxt()
                    ev = e1[:, :].rearrange("p (j t) -> p j t", t=128)[:, :, endcol[c]:endcol[c] + 1].to_broadcast([128, 4, 128])
                    S.tt("dve", x_t[:], St[:, 4 * g:4 * g + 4, :], ev, ALU.mult, [e1_k, "sS%d" % g], [x_k])
                    S.tt("dve", St[:, 4 * g:4 * g + 4, :], x_t[:], s_t[:, :].rearrange("p (j v) -> p j v", v=128), ALU.add,
                         [x_k], ["sS%d" % g], [s_k])
                    S.act(Sb[:, 4 * g:4 * g + 4, :], St[:, 4 * g:4 * g + 4, :], AF.Copy, ["sS%d" % g], ["sSb%d" % g])
            if need_o:
                for g in range(2):
                    S.act(o_sb[:, g * 512:(g + 1) * 512], G[g]["o_t"][:], AF.Copy, [], [o_sk + "g%d" % g], [G[g]["o_k"]])
                S.dma("act", dst[t * 128 - NCTX:(t + 1) * 128 - NCTX, :], o_sb[:], rd=[o_sk + "g0", o_sk + "g1"], wr=["OFB"])
        S.emit()


def phase_hg_read(K):
    S, d = K.S, K.d
    with ExitStack() as ph:
        sb, pp, sring, pring = _mk(K, ph)
        ngb = sb("rng", [128, DM], F32)
        S.dma("sp", ngb[:], d["hg_ng_row"][0:1, :].to_broadcast([128, DM]), wr=["rng"])
        oa = sring("roa", 4, [128, DM], F32)
        obt = sring("rob", 4, [128, DM], F32)
        gg = sring("rgg", 4, [128, DM], BF16)
        ssr = sring("rss", 4, [128, 16], F32)
        yb = sring("ryb", 2, [128, DM], BF16)
        yT = sring("ryT", 2, [128, 8, 512], BF16)
        junk = sb("rjunk", [128, 128], BF16)
        ptr = pring("rpt", 2, [128, DM], BF16)
        units = []
        cur = {}
        for t in range(32):
            def mk(t=t):
                st = {}

                def a_():
                    rows = slice(t * 128, (t + 1) * 128)
                    a_t, a_k = oa.nxt()
                    b_t, b_k = obt.nxt()
                    g_t, g_k = gg.nxt()
                    s_t, s_k = ssr.nxt()
                    S.dma("sp", a_t[:], d["OF"][rows, :], rd=["OFB"], wr=[a_k])
                    S.dma("sp", b_t[:], d["OB"][rows, :], rd=["OFB"], wr=[b_k])
                    S.dma("sp", g_t[:], d["GG"][rows, :], rd=["GG"], wr=[g_k])
                    S.tt("pool", a_t[:], a_t[:], b_t[:], ALU.add, [a_k, b_k], [a_k])
                    S.ms("pool", s_t[:], 0.0, [s_k])
                    for h in range(8):
                        S.act(junk[:], a_t[:, h * 128:(h + 1) * 128], AF.Square, [a_k, s_k], [s_k], accum_out=s_t[:, h:h + 1])
                    rstd_ops(S, s_t[:, 0:8], s_t[:, 8:16], s_k, s_k, 128)
                    st["v"] = (a_t, a_k, b_t, b_k, g_t, g_k, s_t, s_k)

                def b_():
                    a_t, a_k, b_t, b_k, g_t, g_k, s_t, s_k = st["v"]
                    for h in range(8):
                        hs = slice(h * 128, (h + 1) * 128)
                        S.stt("dve", b_t[:, hs], a_t[:, hs], s_t[:, 8 + h:9 + h], ngb[:, hs], ALU.mult, ALU.mult,
                              [a_k, s_k, "rng"], [b_k + "h%d" % h])
                    y_t, y_k = yb.nxt()
                    S.tt("pool", y_t[:], b_t[:], g_t[:], ALU.mult, [b_k + "h%d" % h for h in range(8)] + [g_k], [y_k])
                    p_t, p_k = ptr.nxt()
                    for kc in range(8):
                        S.tr(p_t[:, kc * 128:(kc + 1) * 128], y_t[:, kc * 128:(kc + 1) * 128], K.identb[:], [y_k], [p_k])
                    jj = t % 4
                    if jj == 0:
                        cur["o"] = yT.nxt()
                    o_t, o_k = cur["o"]
                    S.act(o_t[:, :, jj * 128:(jj + 1) * 128], p_t[:, :].rearrange("p (c t) -> p c t", t=128), AF.Copy, [], [o_k + "j%d" % jj], [p_k])
                    if jj == 3:
                        S.dma("act", d["OT"][:, :, NCTX + (t - 3) * 128:NCTX + (t + 1) * 128], o_t[:], rd=[o_k + "j%d" % q for q in range(4)], wr=["OT"])
                return a_, b_
            units.append(mk())
        _run_skewed(units, 2)
        S.emit()


SCRATCH = {
    "modrow": ([2, 2, 2, DM], F32),
    "HT": ([128, 8, NTOK], BF16),
    "OT": ([128, 8, NTOK], BF16),
    "QT": ([8, 96, NTOK], BF16),
    "KT": ([8, 96, NTOK], BF16),
    "V": ([NTOK, 512], BF16),
    "D1k": ([128, 128, 512], BF16),
    "D1_0": ([64, 128, 512], BF16), "D1_1": ([128, 4, 512], BF16), "D1k1": ([128, 8, 512], BF16),
    "D2_0": ([128, 128, 512], BF16), "D2_1": ([128, 8, 512], BF16),
    "KH0": ([2, 128, 65, 4, 512], BF16), "KH1": ([2, 65, 8, 4, 512], BF16),
    "VX": ([NTOK, 3, 512], F32), "Z1": ([NTOK, 512], F32),
    "QS": ([NTOK, DM], BF16), "LF": ([NTOK, 2, DM], F32), "KK": ([NTOK, 2, DM], BF16), "VV": ([NTOK, DM], BF16),
    "GG": ([NLAT, DM], BF16), "OF": ([NLAT, DM], F32), "OB": ([NLAT, DM], F32),
    "XA": ([NTOK, DM], F32),
    "XB": ([NTOK, DM], F32),
}

INPUTS = {
    "x": ([NLAT, DM], F32), "ctx": ([NCTX, DM], F32), "ccol": ([128, 8, 2], F32),
    "mod_w": ([2, DM, 6 * DM], F32), "mod_b": ([2, 6 * DM], F32), "mod_b_col": ([128, 2, 32, 2], F32),
    "ng_col": ([128, 2, 2, 8, 2], F32),
    "ffn_w_up": ([2, DM, 2 * FFH], F32), "ffn_w_down": ([2, FFH, DM], F32),
    "ffn_cw_col": ([2, 128, 3, 44], F32), "ffn_cb_col": ([2, 128, 44], F32),
    "final_g": ([1, DM], F32),
    "ab_w_in": ([DM, 1952], F32), "w_in_krp": ([DM, 32], F32), "mla_w_q_b": ([256, 768], F32),
    "w_q_perm": ([256, 8, 32], F32), "mla_w_kv_b": ([128, 1024], F32), "mla_g_col": ([128, 3], F32),
    "ropeC": ([32, NTOK], F32), "ropeS": ([32, NTOK], F32), "ab_w_out": ([DM, DM], F32),
    "hy_conv_w": ([3, 1536], F32), "hy_conv_b": ([1, 1536], F32), "hy_w1": ([33, 64], F32), "hy_w2": ([64, 64], F32),
    "hy_w3": ([64, 2048], F32), "hy_b_col": ([64, 2], F32), "hy_bias": ([2, 512], F32),
    "hyZ0": ([33, 2 * NLAT], F32), "hyZ1": ([33, 2 * NCTX], F32), "hyT0": ([128, 64], F32), "hyT1": ([128, 4], F32),
    "hy_delta": ([1, 512], F32), "hyFb2": ([128, 128], F32), "hyGb0": ([128, 128], F32), "hyGb1": ([128, 128], F32),
    "hyA0": ([128, 3, 128], F32), "hyA1s": ([64, 3, 128], F32), "hyA1k": ([128, 3, 128], F32),
    "hg_w_in": ([DM, 5120], F32), "hg_lb_logits": ([2, 2, DM], F32), "hg_ng_row": ([1, DM], F32),
    "hg_w_out": ([DM, DM], F32), "hgC": ([128, 2, 5, 128], F32),
}


def build(stage="all", ext=None):
    ext = ext or {}
    nc = bass.Bass("TRN2", target_bir_lowering=False)
    K = Ctx()
    K.nc = nc
    d = K.d = {}
    for name, (shape, dt) in INPUTS.items():
        d[name] = nc.dram_tensor(name, shape, dt, kind="ExternalInput").ap()
    for name, (shape, dt) in SCRATCH.items():
        d[name] = nc.dram_tensor(name, shape, dt, kind=ext.get(name, "Internal")).ap()
    d["out"] = nc.dram_tensor("out", [NLAT, DM], F32, kind="ExternalOutput").ap()
    with ExitStack() as st:
        S = K.S = Sch(nc, st)
        K.modt = st.enter_context(nc.sbuf_tensor("modt", [128, 2, 32, 2], F32))
        K.identb = st.enter_context(nc.sbuf_tensor("identb", [128, 128], BF16))
        identf = st.enter_context(nc.sbuf_tensor("identf", [128, 128], F32))
        MHALF[0] = st.enter_context(nc.sbuf_tensor("mhalf", [128, 16], F32))
        S.ms("pool", MHALF[0][:], -0.5, ["mhalf"])
        S.ms("pool", identf[:], 1.0, ["identf"])
        S.op("pool", lambda e: e.affine_select(out=identf[:], in_=identf[:], pattern=[[-1, 128]], compare_op=ALU.is_equal,
                                               fill=0.0, base=0, channel_multiplier=1), ["identf"], ["identf"])
        S.cp("pool", K.identb[:], identf[:], ["identf"], ["identb"])
        XA, XB = d["XA"], d["XB"]
        phase_mod(K)
        if stage == "all":
            x, ctx = d["x"], d["ctx"]
            phase_norm(K, 0, 0, [(ctx, 0, 1), (x, NCTX, 0)])
            phase_mla_proj(K)
            phase_attn(K)
            hyena_all(K)
            phase_outproj(K, 0, "ab_w_out", x, ctx, XA, True)
            phase_norm(K, 0, 1, [(XA[0:NCTX], 0, 1), (XA[NCTX:NTOK], NCTX, 0)])
            phase_ffn(K, 0, XA[NCTX:NTOK], XA[0:NCTX], XB[NCTX:NTOK], XB[0:NCTX], False)
            phase_norm(K, 1, 0, [(XB[0:NCTX], 0, 1), (XB[NCTX:NTOK], NCTX, 0)])
            phase_hg_proj(K)
            phase_hg_scan(K, 0)
            phase_hg_scan(K, 1)
            phase_hg_read(K)
            phase_outproj(K, 1, "hg_w_out", XB[NCTX:NTOK], XB[0:NCTX], XA, False)
            phase_norm(K, 1, 1, [(XA[NCTX:NTOK], NCTX, 0)])
            phase_ffn(K, 1, XA[NCTX:NTOK], None, d["out"], None, True, do_ctx=False)
            S.final_wait("sp", ["OUT"])
            S.final_wait("pool", ["OUT"])
        if stage == "mla0":
            phase_norm(K, 0, 0, [(d["ctx"], 0, 1), (d["x"], NCTX, 0)])
            phase_mla_proj(K)
            phase_attn(K)
            S.final_wait("sp", ["OT"])
        if stage in ("hyf0", "hyf1"):
            phase_hy_filters(K, int(stage[-1]))
            S.final_wait("sp", ["KH0", "KH1", "D1k"])
        if stage == "hyin":
            phase_norm(K, 0, 0, [(d["ctx"], 0, 1), (d["x"], NCTX, 0)])
            phase_hy_in(K)
            S.final_wait("sp", ["VX", "D1"])
            S.final_wait("pool", ["VX", "D1"])
        if stage == "hg1":
            phase_norm(K, 1, 0, [(XB[0:NCTX], 0, 1), (XB[NCTX:NTOK], NCTX, 0)])
            phase_hg_proj(K)
            phase_hg_scan(K, 0)
            phase_hg_scan(K, 1)
            phase_hg_read(K)
            phase_outproj(K, 1, "hg_w_out", XB[NCTX:NTOK], XB[0:NCTX], XA, False)
            S.final_wait("sp", ["XA"])
        if stage.startswith("hyx:"):
            phase_norm(K, 0, 0, [(d["ctx"], 0, 1), (d["x"], NCTX, 0)])
            for tok in stage[4:].split(","):
                if tok[0] == "f":
                    phase_hy_filters(K, int(tok[1]))
                elif tok == "in":
                    phase_hy_in(K)
                elif tok[0] == "m":
                    phase_hy_mid(K, int(tok[1]), int(tok[2]))
                elif tok[0] == "o":
                    phase_hy_out(K, int(tok[1]), int(tok[2]))
            S.final_wait("sp", ["D2", "D1", "Z1", "OT", "VX"])
            S.final_wait("pool", ["D2", "D1", "Z1", "OT", "VX"])
        if stage in ("hym", "hymo"):
            phase_norm(K, 0, 0, [(d["ctx"], 0, 1), (d["x"], NCTX, 0)])
            phase_hy_filters(K, 1)
            phase_hy_in(K)
            phase_hy_mid(K, 1, 0)
            if stage == "hymo":
                phase_hy_out(K, 1, 0)
            S.final_wait("sp", ["D2", "D1", "Z1"])
            S.final_wait("pool", ["D2", "D1", "Z1"])
        if stage == "hy0":
            phase_norm(K, 0, 0, [(d["ctx"], 0, 1), (d["x"], NCTX, 0)])
            hyena_all(K)
            S.final_wait("sp", ["OT", "Z1", "VX"])
        if stage == "ffn0":
            phase_norm(K, 0, 1, [(XA[0:NCTX], 0, 1), (XA[NCTX:NTOK], NCTX, 0)])
            phase_ffn(K, 0, XA[NCTX:NTOK], XA[0:NCTX], XB[NCTX:NTOK], XB[0:NCTX], False)
            S.final_wait("sp", ["XB"])
        S.emit()
    return nc


_CONST = {}


def rope_tables():
    if "rope" not in _CONST:
        pos = np.arange(NLAT)
        inv = (10000.0 ** (-np.arange(8, dtype=np.float32) / 8)).astype(np.float32)
        ang = np.stack([pos // 64, pos % 64], -1)[..., None].astype(np.float32) * inv
        C = np.ones((32, NTOK), np.float32)
        Sn = np.zeros((32, NTOK), np.float32)
        for dd in range(32):
            ax, half, fr = dd // 16, (dd % 16) // 8, dd % 8
            C[dd, NCTX:] = np.cos(ang[:, ax, fr])
            Sn[dd, NCTX:] = np.sin(ang[:, ax, fr]) * (-1.0 if half == 0 else 1.0)
        _CONST["rope"] = (C, Sn)
    return _CONST["rope"]


def hg_consts():
    if "hg" in _CONST:
        return _CONST["hg"]
    s_ = np.arange(128)[:, None]
    t_ = np.arange(128)[None, :]
    same = (s_ // 64) == (t_ // 64)
    C = np.zeros((128, 2, 5, 128), np.float32)
    for dd in range(2):
        tri = (same & ((s_ <= t_) if dd == 0 else (s_ >= t_))).astype(np.float32)
        mid = (np.arange(128) // 64) * 64 + 32
        trim = tri - tri[:, mid]
        rev = (same & ((s_ > t_) if dd == 0 else (s_ < t_))).astype(np.float32)
        C[:, dd, 0] = tri
        C[:, dd, 1] = trim
        C[:, dd, 2] = rev
        C[:, dd, 3] = tri * 3.0e38
        C[:, dd, 4] = -tri * 3.0e38
    _CONST["hg"] = C
    return C


def hy_consts():
    if "hy" in _CONST:
        return _CONST["hy"]
    c = {}
    f32 = np.float32
    deltas = np.abs(np.linspace(math.log(1e-2) / 1.5, math.log(1e-2) / 0.3, 512, dtype=f32))
    c["hy_delta"] = deltas.reshape(1, 512).astype(f32)
    bb = np.arange(64)[:, None].astype(np.float64)
    Fb = np.zeros((64, 128))
    fr = np.arange(65)[None, :]
    Fb[:, 0:65] = np.cos(2 * np.pi * fr * bb / 128)
    fr = np.arange(1, 64)[None, :]
    Fb[:, 65:128] = -np.sin(2 * np.pi * fr * bb / 128)
    c["hyFb2"] = np.concatenate([Fb, Fb], 0).astype(f32)
    Bq = np.arange(128)[None, :].astype(np.float64)
    Gb = np.zeros((128, 128))
    Gb[0] = 1.0
    fr = np.arange(1, 64)[:, None]
    Gb[1:64] = 2 * np.cos(2 * np.pi * fr * Bq / 128)
    Gb[64] = np.cos(np.pi * Bq[0])
    Gb[65:128] = -2 * np.sin(2 * np.pi * fr * Bq / 128)
    for seq, (L, NF) in enumerate(((NLAT, 128), (NCTX, 8))):
        c["hyGb%d" % seq] = (Gb / (128.0 * NF)).astype(f32)
        a = np.arange(NF)[:, None].astype(np.float64)
        th = 2 * np.pi * a * a.T / NF
        A3 = np.stack([np.cos(th), np.sin(th), -np.sin(th)], axis=1)
        if seq == 0:
            c["hyA0"] = A3.astype(f32)
        else:
            As = np.zeros((64, 3, 128)); Ak = np.zeros((128, 3, 128))
            for g in range(16):
                As[g * 4:(g + 1) * 4, :, g * 8:(g + 1) * 8] = A3[0:4]
                Ak[g * 8:(g + 1) * 8, :, g * 8:(g + 1) * 8] = A3
            c["hyA1s"] = As.astype(f32)
            c["hyA1k"] = Ak.astype(f32)
        p = np.arange(2 * L)
        lag = np.where(p <= L, p, 2 * L - p)
        lag = np.minimum(lag, L - 1)
        t = np.linspace(0.0, 1.0, L, dtype=f32)[lag]
        w = (2.0 * math.pi * lag.astype(f32) / L).astype(f32)
        fq = np.linspace(1e-4, 15, 16, dtype=f32)
        Z = np.concatenate([t[:, None], np.cos(fq[None, :] * w[:, None]), -np.sin(fq[None, :] * w[:, None])], axis=1).astype(f32)
        c["hyZ%d" % seq] = np.ascontiguousarray(Z.T)
        c["hyT%d" % seq] = np.ascontiguousarray((-t).reshape(2 * L // 128, 128).T).astype(f32)
    _CONST["hy"] = c
    return c


def host_inputs(inp, b):
    f = lambda a: np.ascontiguousarray(a, dtype=np.float32)
    m = {}
    m["x"] = f(inp["x"][b])
    m["ctx"] = f(inp["ctx"][b])
    cc = np.stack([inp["c"][b], inp["c_ctx"]], axis=-1)
    m["ccol"] = f(cc.reshape(8, 128, 2).transpose(1, 0, 2))
    m["mod_w"] = f(inp["mod_w"])
    m["mod_b"] = f(inp["mod_b"])
    mb = inp["mod_b"].reshape(2, 6, 8, 128)[:, [0, 1, 3, 4]]
    mb = mb.transpose(3, 0, 1, 2).reshape(128, 2, 32)
    m["mod_b_col"] = f(np.repeat(mb[..., None], 2, axis=-1))
    ng = np.stack([inp["norm1_g"], inp["norm2_g"]], axis=1)
    ng = ng.reshape(2, 2, 8, 128).transpose(3, 0, 1, 2)
    m["ng_col"] = f(np.repeat(ng[..., None], 2, axis=-1))
    m["ffn_w_up"] = f(inp["ffn_w_up"])
    m["ffn_w_down"] = f(inp["ffn_w_down"])
    m["ffn_cw_col"] = f(inp["ffn_conv_w"].reshape(2, 3, 44, 128).transpose(0, 3, 1, 2))
    m["ffn_cb_col"] = f(inp["ffn_conv_b"].reshape(2, 44, 128).transpose(0, 2, 1))
    m["final_g"] = f(inp["final_norm_g"].reshape(1, DM))
    m["ab_w_in"] = f(inp["ab_w_in"][0])
    perm = np.array([dd + 8 if (dd % 16) < 8 else dd - 8 for dd in range(32)])
    m["w_in_krp"] = f(inp["ab_w_in"][0][:, 384 + perm])
    wq = inp["mla_w_q_b"][0]
    m["mla_w_q_b"] = f(wq)
    m["w_q_perm"] = f(wq.reshape(256, 8, 96)[:, :, 64 + perm])
    m["mla_w_kv_b"] = f(inp["mla_w_kv_b"][0])
    m["mla_g_col"] = f(np.concatenate([inp["mla_q_norm_g"][0].reshape(2, 128).T, inp["mla_kv_norm_g"][0].reshape(1, 128).T], axis=1))
    rc, rs = rope_tables()
    m["ropeC"], m["ropeS"] = rc, rs
    m["ab_w_out"] = f(inp["ab_w_out"][0])
    m["hy_conv_w"] = f(inp["hy_conv_w"][0])
    m["hy_conv_b"] = f(inp["hy_conv_b"][0].reshape(1, 1536))
    m["hy_w1"] = f(inp["hy_w1"][0]); m["hy_w2"] = f(inp["hy_w2"][0]); m["hy_w3"] = f(inp["hy_w3"][0])
    m["hy_b_col"] = f(np.stack([inp["hy_b1"][0], inp["hy_b2"][0]], axis=1))
    m["hy_bias"] = f(inp["hy_bias"][0])
    m.update(hy_consts())
    m["hg_w_in"] = f(inp["hg_w_in"][0])
    m["hg_lb_logits"] = f(inp["hg_lb_logits"])
    m["hg_ng_row"] = f(np.tile(inp["hg_norm_g"][0], 8).reshape(1, DM))
    m["hg_w_out"] = f(inp["hg_w_out"][0])
    m["hgC"] = hg_consts()
    return m


def kernel(**inputs):
    nc = build("all")
    in_maps = [host_inputs(inputs, b) for b in range(8)]
    res = run_bass_kernel_spmd(nc, in_maps, core_ids=list(range(8)))
    return np.stack([r["out"] for r in res.results], axis=0).astype(np.float32)
```
